# Optimizing a Trainium2 kernel written in Bass

```python
import jax, jax.numpy as jnp
from jax import lax
import numpy as np

D_MODEL = 4096
BATCH = 4
SEQ = 2048
DEPTH = 1

EPS = 1e-6
MOBA_HEAD_DIM = 128
MOBA_HEADS = D_MODEL // 256
MOBA_WIDTH = MOBA_HEADS * MOBA_HEAD_DIM
MOBA_BLOCK = 256
MOBA_TOPK = 3
MOBA_QCHUNK = 16
ROPE_THETA = 10000.0
GLA_HEADS = D_MODEL // 512
GLA_K_WIDTH = D_MODEL // 2
GLA_V_WIDTH = D_MODEL
GLA_DK = GLA_K_WIDTH // GLA_HEADS
GLA_DV = GLA_V_WIDTH // GLA_HEADS
GLA_GATE_RANK = 16
GLA_GATE_TAU = 16.0
GLA_CHUNK = 64
D_FF = -(-8 * D_MODEL // (3 * 256)) * 256
IN_SPLITS = (MOBA_WIDTH, MOBA_WIDTH, MOBA_WIDTH,
             GLA_K_WIDTH, GLA_K_WIDTH, GLA_V_WIDTH,
             GLA_V_WIDTH, GLA_GATE_RANK,
             D_MODEL, D_MODEL)
IN_COLS = sum(IN_SPLITS)

kernel_name = "moba_gla_parallel_gated_hybrid"


def _rmsnorm(x, g):
    xf = x.astype(jnp.float32)
    y = xf * lax.rsqrt(jnp.mean(xf * xf, axis=-1, keepdims=True) + EPS)
    return (y * g.astype(jnp.float32)).astype(x.dtype)


def _split_cols(u, sizes):
    idx = [int(v) for v in np.cumsum(sizes)[:-1]]
    return jnp.split(u, idx, axis=-1)


def _rope(x, pos):
    d = x.shape[-1]
    half = d // 2
    inv_freq = ROPE_THETA ** (-jnp.arange(half, dtype=jnp.float32) / half)
    ang = pos.astype(jnp.float32)[:, None] * inv_freq[None, :]
    cos = jnp.cos(ang)[None, :, None, :]
    sin = jnp.sin(ang)[None, :, None, :]
    xf = x.astype(jnp.float32)
    x1, x2 = xf[..., :half], xf[..., half:]
    out = jnp.concatenate([x1 * cos - x2 * sin, x2 * cos + x1 * sin], axis=-1)
    return out.astype(x.dtype)


def _moba_attention(q, k, v):
    B, S, H, D = q.shape
    s_pad = -(-S // MOBA_BLOCK) * MOBA_BLOCK
    pad = ((0, 0), (0, s_pad - S), (0, 0), (0, 0))
    q, k, v = [jnp.pad(t, pad).transpose(0, 2, 1, 3) for t in (q, k, v)]
    nb = s_pad // MOBA_BLOCK
    scale = D ** -0.5
    kb = k.reshape(B, H, nb, MOBA_BLOCK, D)
    vb = v.reshape(B, H, nb, MOBA_BLOCK, D)

    kmean = jnp.mean(kb.astype(jnp.float32), axis=3)
    gate = jnp.einsum('bhsd,bhnd->bhsn', q.astype(jnp.float32), kmean)
    q_blk = jnp.arange(s_pad) // MOBA_BLOCK
    past = jnp.arange(nb)[None, :] < q_blk[:, None]
    gate = jnp.where(past[None, None], gate, -jnp.inf)
    k_sel = min(MOBA_TOPK, nb)
    top_val, top_idx = lax.top_k(gate, k_sel)
    top_ok = jnp.isfinite(top_val)

    n_chunks = s_pad // MOBA_QCHUNK
    qc = q.reshape(B, H, n_chunks, MOBA_QCHUNK, D).transpose(2, 0, 1, 3, 4)
    idx_c = top_idx.reshape(B, H, n_chunks, MOBA_QCHUNK, k_sel).transpose(2, 0, 1, 3, 4)
    ok_c = top_ok.reshape(B, H, n_chunks, MOBA_QCHUNK, k_sel).transpose(2, 0, 1, 3, 4)
    b_ix = jnp.arange(B)[:, None, None, None]
    h_ix = jnp.arange(H)[None, :, None, None]

    def chunk(args):
        c, qi, ii, oki = args
        q0 = c * MOBA_QCHUNK
        own = q0 // MOBA_BLOCK
        k_g = kb[b_ix, h_ix, ii]
        v_g = vb[b_ix, h_ix, ii]
        s_sel = jnp.einsum('bhqd,bhqtkd->bhqtk', qi, k_g).astype(jnp.float32) * scale
        s_sel = jnp.where(oki[..., None], s_sel, -jnp.inf)
        s_sel = s_sel.reshape(B, H, MOBA_QCHUNK, k_sel * MOBA_BLOCK)
        k_own = lax.dynamic_index_in_dim(kb, own, axis=2, keepdims=False)
        v_own = lax.dynamic_index_in_dim(vb, own, axis=2, keepdims=False)
        s_own = jnp.einsum('bhqd,bhkd->bhqk', qi, k_own).astype(jnp.float32) * scale
        qpos = q0 + jnp.arange(MOBA_QCHUNK)
        kpos = own * MOBA_BLOCK + jnp.arange(MOBA_BLOCK)
        s_own = jnp.where((kpos[None, :] <= qpos[:, None])[None, None], s_own, -jnp.inf)
        p = jax.nn.softmax(jnp.concatenate([s_sel, s_own], axis=-1), axis=-1).astype(v.dtype)
        p_sel = p[..., :k_sel * MOBA_BLOCK].reshape(B, H, MOBA_QCHUNK, k_sel, MOBA_BLOCK)
        p_own = p[..., k_sel * MOBA_BLOCK:]
        return (jnp.einsum('bhqtk,bhqtkd->bhqd', p_sel, v_g)
                + jnp.einsum('bhqk,bhkd->bhqd', p_own, v_own))

    o = lax.map(chunk, (jnp.arange(n_chunks), qc, idx_c, ok_c))
    o = o.transpose(1, 0, 3, 2, 4).reshape(B, s_pad, H * D)
    return o[:, :S]


def _gla(q, k, v, log_a):
    B, S, H, DK = q.shape
    DV = v.shape[-1]
    nc = S // GLA_CHUNK

    def chunked(t):
        return t.astype(jnp.float32).reshape(B, nc, GLA_CHUNK, H, t.shape[-1]).transpose(1, 0, 3, 2, 4)

    qc = chunked(q) * (DK ** -0.5)
    kc = chunked(k)
    vc = chunked(v)
    bcum = jnp.cumsum(chunked(log_a), axis=3)
    b_last = bcum[:, :, :, -1:, :]
    q_dec = qc * jnp.exp(bcum)
    k_inv = kc * jnp.exp(-bcum)
    k_to_end = kc * jnp.exp(b_last - bcum)
    causal = jnp.tril(jnp.ones((GLA_CHUNK, GLA_CHUNK), dtype=bool))
    attn = jnp.einsum('nbhid,nbhjd->nbhij', q_dec, k_inv)
    attn = jnp.where(causal, attn, 0.0)
    o_intra = jnp.einsum('nbhij,nbhjv->nbhiv', attn, vc)

    def step(state, inp):
        q_d, k_e, v_c, dl = inp
        o = jnp.einsum('bhid,bhdv->bhiv', q_d, state)
        state = jnp.exp(dl)[:, :, 0, :, None] * state + jnp.einsum('bhjd,bhjv->bhdv', k_e, v_c)
        return state, o

    s0 = jnp.zeros((B, H, DK, DV), jnp.float32)
    _, o_inter = lax.scan(step, s0, (q_dec, k_to_end, vc, b_last))
    o = o_intra + o_inter
    return o.transpose(1, 0, 3, 2, 4).reshape(B, S, H, DV)


def _mixer(h, w_in, gla_gate_up, gla_gate_bias, gla_out_norm_g,
           w_branch_moba, w_branch_gla, w_out):
    B, S, _ = h.shape
    u = h @ w_in
    mq, mk, mv, gq, gk, gv, gr, ga_down, g_moba, g_gla = _split_cols(u, IN_SPLITS)
    pos = jnp.arange(S)
    mq = _rope(mq.reshape(B, S, MOBA_HEADS, MOBA_HEAD_DIM), pos)
    mk = _rope(mk.reshape(B, S, MOBA_HEADS, MOBA_HEAD_DIM), pos)
    mv = mv.reshape(B, S, MOBA_HEADS, MOBA_HEAD_DIM)
    y_moba = _moba_attention(mq, mk, mv) @ w_branch_moba
    log_a = jax.nn.log_sigmoid((ga_down @ gla_gate_up + gla_gate_bias).astype(jnp.float32)) / GLA_GATE_TAU
    o = _gla(gq.reshape(B, S, GLA_HEADS, GLA_DK),
             gk.reshape(B, S, GLA_HEADS, GLA_DK),
             gv.reshape(B, S, GLA_HEADS, GLA_DV),
             log_a.reshape(B, S, GLA_HEADS, GLA_DK)).astype(h.dtype)
    o = _rmsnorm(o, gla_out_norm_g).reshape(B, S, GLA_V_WIDTH) * jax.nn.silu(gr)
    y_gla = o @ w_branch_gla
    merged = jax.nn.sigmoid(g_moba) * y_moba + jax.nn.sigmoid(g_gla) * y_gla
    return merged @ w_out


def _swiglu(h, w_gate, w_up, w_down):
    return (jax.nn.silu(h @ w_gate) * (h @ w_up)) @ w_down


def setup_inputs(seed: int = 0) -> dict:
    key = jax.random.key(seed)
    ks = jax.random.split(key, 16)
    f32 = jnp.float32

    def dense(k, fan_in, fan_out):
        return jax.random.normal(k, (DEPTH, fan_in, fan_out), f32) * fan_in ** -0.5

    def gain(k, n):
        return 1.0 + 0.02 * jax.random.normal(k, (DEPTH, n), f32)

    return {
        "x": jax.random.normal(ks[0], (BATCH, SEQ, D_MODEL), f32),
        "pre_mix_norm_g": gain(ks[1], D_MODEL),
        "w_in": dense(ks[2], D_MODEL, IN_COLS),
        "gla_gate_up": dense(ks[3], GLA_GATE_RANK, GLA_K_WIDTH),
        "gla_gate_bias": 0.1 * jax.random.normal(ks[4], (DEPTH, GLA_K_WIDTH), f32),
        "gla_out_norm_g": gain(ks[5], GLA_DV),
        "w_branch_moba": dense(ks[6], MOBA_WIDTH, D_MODEL),
        "w_branch_gla": dense(ks[7], GLA_V_WIDTH, D_MODEL),
        "w_out": dense(ks[8], D_MODEL, D_MODEL),
        "post_mix_norm_g": gain(ks[9], D_MODEL),
        "pre_ffn_norm_g": gain(ks[10], D_MODEL),
        "w_ffn_gate": dense(ks[11], D_MODEL, D_FF),
        "w_ffn_up": dense(ks[12], D_MODEL, D_FF),
        "w_ffn_down": dense(ks[13], D_FF, D_MODEL),
        "post_ffn_norm_g": gain(ks[14], D_MODEL),
    }


def reference(x, pre_mix_norm_g, w_in, gla_gate_up, gla_gate_bias, gla_out_norm_g,
              w_branch_moba, w_branch_gla, w_out, post_mix_norm_g, pre_ffn_norm_g,
              w_ffn_gate, w_ffn_up, w_ffn_down, post_ffn_norm_g):
    for layer in range(DEPTH):
        h = _rmsnorm(x, pre_mix_norm_g[layer])
        y = _mixer(h, w_in[layer], gla_gate_up[layer], gla_gate_bias[layer],
                   gla_out_norm_g[layer], w_branch_moba[layer], w_branch_gla[layer],
                   w_out[layer])
        x = x + _rmsnorm(y, post_mix_norm_g[layer])
        h = _rmsnorm(x, pre_ffn_norm_g[layer])
        y = _swiglu(h, w_ffn_gate[layer], w_ffn_up[layer], w_ffn_down[layer])
        x = x + _rmsnorm(y, post_ffn_norm_g[layer])
    return x
```

```python
from contextlib import ExitStack
import numpy as np
import concourse.bass as bass
import concourse.mybir as mybir
from concourse.bass_utils import run_bass_kernel_spmd

F32 = mybir.dt.float32
BF16 = mybir.dt.bfloat16
AF = mybir.ActivationFunctionType
ALU = mybir.AluOpType
AX = mybir.AxisListType

D = 4096
KT = 32
TOK = 1024
NT = 8
DFF = 11008
FT = 86
EPS = 1e-6
OQ, OK_, OV, OGQ, OGK, OGV, OGR, OGA, OGM, OGG = 0, 2048, 4096, 6144, 8192, 10240, 14336, 18432, 18448, 22544
INC = 26640
NEG = -3.0e38
C_ID, C_PERM, C_TRIU, C_TRISL, C_MUT, C_CM, C_NEG, C_END = 0, 128, 256, 384, 512, 640, 1152, 1154

import os
CUT = int(os.environ.get('DBG_CUT', '99'))
STOP_AFTER = None
DEBUG_OUT = ()
SHRINK = ()


class StopBuild(Exception):
    pass


class Buf:
    __slots__ = ("name", "writer", "readers", "tok")

    def __init__(self, name="", tok=None):
        self.name = name
        self.writer = None
        self.readers = {}
        self.tok = tok


def PB(name=""):
    return Buf(name, tok=Buf(name + "_tok"))


class Rec:
    __slots__ = ("eng", "fn", "deps", "signal", "val", "sem", "is_dma", "phase", "n")

    def __init__(self, eng, fn, phase):
        self.eng = eng
        self.fn = fn
        self.deps = []
        self.signal = False
        self.val = None
        self.sem = None
        self.is_dma = False
        self.phase = phase
        self.n = 1


class Slot:
    __slots__ = ("sem", "count", "last")

    def __init__(self, sem):
        self.sem = sem
        self.count = 0
        self.last = None


ENGS = ("pe", "act", "dve", "pool", "sp")


class Sched:
    def __init__(self, nc, esem, sp_sems, pool_sems):
        self.nc = nc
        self.esem = esem
        self.cnt = {e: 0 for e in ENGS}
        self.dpool = {"sp": [Slot(s) for s in sp_sems], "pool": [Slot(s) for s in pool_sems]}
        self.dnext = {"sp": 0, "pool": 0}
        self.waited = {e: {} for e in ENGS}
        self.phase = 0
        self.q = {e: [] for e in ENGS}
        self.ninst = 0

    def _deps(self, r, reads, writes):
        deps = {}

        def add(d, kind):
            if d is None or d is r or d.phase != self.phase:
                return
            if (not d.is_dma) and (not r.is_dma) and d.eng == r.eng:
                if r.eng == "pe":
                    return
            deps[id(d)] = d

        for b in reads:
            add(b.writer, "w")
        for b in writes:
            add(b.writer, "w")
            for rd in b.readers.values():
                add(rd, "r")
        for d in deps.values():
            r.deps.append(d)
            if not d.is_dma:
                d.signal = True
        for b in reads:
            b.readers[("dma", id(r)) if r.is_dma else r.eng] = r
        for b in writes:
            b.writer = r
            b.readers = {}

    def op(self, eng, fn, reads=(), writes=()):
        toks = []
        for b in list(reads) + list(writes):
            if b.tok is not None and b.tok not in toks:
                toks.append(b.tok)
        if toks:
            writes = list(writes) + toks
        r = Rec(eng, fn, self.phase)
        self._deps(r, reads, writes)
        self.q[eng].append(r)
        return r

    def dma(self, queue, fn, reads=(), writes=(), n=1):
        pool = self.dpool[queue]
        slot = pool[self.dnext[queue] % len(pool)]
        self.dnext[queue] += 1
        r = Rec(queue, fn, self.phase)
        r.is_dma = True
        r.sem = slot.sem
        r.n = n
        if slot.last is not None and slot.last.phase == self.phase:
            r.deps.append(slot.last)
        slot.count += 16 * n
        r.val = slot.count
        slot.last = r
        self._deps(r, reads, writes)
        self.q[queue].append(r)
        return r

    def end_phase(self):
        deps = []
        for e in ("pe", "act", "dve", "pool"):
            for r in reversed(self.q[e]):
                if r.fn is not None:
                    deps.append(r)
                    break
        for queue in self.dpool:
            for slot in self.dpool[queue]:
                if slot.last is not None and slot.last.phase == self.phase:
                    deps.append(slot.last)
        b = Rec("sp", lambda e: e.nop(), self.phase)
        seen = set()
        for d in deps:
            if id(d) in seen:
                continue
            seen.add(id(d))
            b.deps.append(d)
            if not d.is_dma:
                d.signal = True
        b.signal = True
        self.q["sp"].append(b)
        for e in ("pe", "act", "dve", "pool"):
            rr = Rec(e, None, self.phase)
            rr.deps = [b]
            self.q[e].append(rr)
        for e in ENGS:
            for r in self.q[e]:
                if (not r.is_dma) and r.signal:
                    self.cnt[e] += 1
                    r.val = self.cnt[e]
        nc = self.nc
        with nc.Block() as blk:
            for e, reg in (("pe", blk.tensor), ("act", blk.scalar), ("dve", blk.vector),
                           ("pool", blk.gpsimd), ("sp", blk.sync)):
                lst = self.q[e]
                waited = self.waited[e]
                esem = self.esem

                def body(eng, e=e, lst=lst, waited=waited):
                    for r in lst:
                        for d in r.deps:
                            sem = d.sem if d.is_dma else esem[d.eng]
                            val = d.val
                            assert val is not None
                            key = id(sem)
                            if waited.get(key, 0) >= val:
                                continue
                            eng.wait_ge(sem, val)
                            waited[key] = val
                        if r.fn is None:
                            continue
                        ins = r.fn(eng)
                        self.ninst += 1
                        if r.is_dma:
                            if isinstance(ins, (list, tuple)):
                                assert len(ins) == r.n
                                for i_ in ins:
                                    i_.then_inc(r.sem, 16)
                            else:
                                assert r.n == 1
                                ins.then_inc(r.sem, 16)
                        elif r.signal:
                            ins.then_inc(esem[e], 1)

                reg(body)
        self.q = {e: [] for e in ENGS}
        self.phase += 1
        build_program.ninst = self.ninst
        if STOP_AFTER is not None and self.phase > STOP_AFTER:
            raise StopBuild()


def MM(out, lhsT, rhs, start=True, stop=True):
    return lambda e: e.matmul(out, lhsT, rhs, start=start, stop=stop)


def TR(out, in_, ident):
    return lambda e: e.transpose(out, in_, ident)


def ACT(out, in_, func, bias=0.0, scale=1.0, accum_out=None):
    if accum_out is None:
        return lambda e: e.activation(out=out, in_=in_, func=func, bias=bias, scale=scale)
    return lambda e: e.activation(out=out, in_=in_, func=func, bias=bias, scale=scale, accum_out=accum_out)


def TT(out, a, b, op):
    return lambda e: e.tensor_tensor(out=out, in0=a, in1=b, op=op)


def TS(out, a, s1, s2, op0, op1=None):
    if op1 is None:
        return lambda e: e.tensor_scalar(out=out, in0=a, scalar1=s1, scalar2=None, op0=op0)
    return lambda e: e.tensor_scalar(out=out, in0=a, scalar1=s1, scalar2=s2, op0=op0, op1=op1)


def STT(out, a, s, b, op0, op1):
    return lambda e: e.scalar_tensor_tensor(out=out, in0=a, scalar=s, in1=b, op0=op0, op1=op1)


def CP(out, in_):
    return lambda e: e.tensor_copy(out=out, in_=in_)


def RCP(out, in_):
    return lambda e: e.reciprocal(out=out, in_=in_)


def DMA(out, in_):
    return lambda e: e.dma_start(out=out, in_=in_)


def MSET(ap, v):
    return lambda e: e.memset(ap, v)


def build_program():
    nc = bass.Bass("TRN2", target_bir_lowering=False)
    try:
        _build(nc)
    except StopBuild:
        pass
    return nc


def _build(nc):

    def din(name, shape, dt=F32):
        if name in SHRINK:
            t = nc.dram_tensor(name + "_tiny", [128, 128], dt, kind="ExternalInput")
            return nc.dram_tensor(name + "_fake", list(shape), dt, kind="Internal").ap()
        return nc.dram_tensor(name, list(shape), dt, kind="ExternalInput").ap()

    def dscr(name, shape, dt):
        kind = "ExternalOutput" if name in DEBUG_OUT else "Internal"
        return nc.dram_tensor(name, list(shape), dt, kind=kind).ap()

    x_own = din("x_own", [TOK, D])
    x_pre = din("x_pre", [TOK, D])
    w_in = din("w_in", [D, INC])
    w_bm = din("w_bm", [2048, D])
    w_bg = din("w_bg", [D, D])
    w_out = din("w_out", [D, D])
    w_fg = din("w_fg", [D, DFF])
    w_fu = din("w_fu", [D, DFF])
    w_fd = din("w_fd", [DFF, D])
    gup_d = din("gup", [16, 2048])
    gvec = din("gvec", [128, 64])
    gbias_d = din("gbias", [128, 64])
    glab_d = din("glab", [128, 2048])
    g512_d = din("g512", [128, 512])
    gpm_d = din("gpm", [128, D])
    gpf_d = din("gpf", [128, D])
    cosT_d = din("cosT", [128, 2048])
    sinT_d = din("sinT", [128, 2048])
    cb_d = din("cblob", [128, C_END])
    out_d = nc.dram_tensor("out", [TOK, D], F32, kind="ExternalOutput").ap()

    kt_s = dscr("kt_s", [16, 128, 1024], BF16)
    v_s = dscr("v_s", [8, 128, 8 * 2 * 129], BF16)
    moT_s = dscr("moT_s", [16, 128, 1024], BF16)
    goT_s = dscr("goT_s", [32, 128, 1024], BF16)
    sgm_s = dscr("sgm_s", [32, 128, 1024], BF16)
    sgg_s = dscr("sgg_s", [32, 128, 1024], BF16)
    y_s = dscr("y_s", [TOK, D], F32)
    x1_s = dscr("x1_s", [TOK, D], F32)
    y2_s = dscr("y2_s", [TOK, D], F32)
    st_s = dscr("st_s", [8, 128, 1024], F32)
    h2T_s = dscr("h2T_s", [2, 128, KT, 512], BF16)

    def wview(W):
        return W.rearrange("(kt p) n -> p kt n", p=128)

    w_in_v, w_bm_v, w_bg_v, w_out_v = wview(w_in), wview(w_bm), wview(w_bg), wview(w_out)
    w_fg_v, w_fu_v, w_fd_v = wview(w_fg), wview(w_fu), wview(w_fd)

    with ExitStack() as G:
        uid = [0]

        def sb(name, shape, dt, es=G):
            uid[0] += 1
            return es.enter_context(nc.sbuf_tensor("%s_s%d" % (name, uid[0]), list(shape), dt))

        def psum(name, shape, dt, es):
            uid[0] += 1
            return es.enter_context(nc.psum_tensor("%s_p%d" % (name, uid[0]), list(shape), dt))

        nsp, npool = 40, 8
        esem = {e: G.enter_context(nc.semaphore("e_" + e)) for e in ENGS}
        sp_sems = [G.enter_context(nc.semaphore("dsp%d" % i)) for i in range(nsp)]
        pool_sems = [G.enter_context(nc.semaphore("dpl%d" % i)) for i in range(npool)]
        S = Sched(nc, esem, sp_sems, pool_sems)

        cb = sb("cb", [128, C_END], F32)
        identb = sb("identb", [128, 128], BF16)
        permb = sb("permb", [128, 128], BF16)
        mutb = sb("mutb", [128, 128], BF16)
        cmb = sb("cmb", [128, 512], BF16)
        gv = sb("gv", [128, 64], F32)
        gbias = sb("gbias_sb", [128, 64], F32)
        ksum = sb("ksum", [128, 16, 8], F32)
        wbuf = []
        B_w = []
        B_cb, B_gv, B_gbias = Buf("cb"), Buf("gv"), Buf("gbias")
        B_ksum = [Buf("ksum%d" % h) for h in range(16)]
        wn = [0]

        def alloc_w(es, n, size=8192):
            wbuf[:] = [sb("wbuf%d" % i, [128, size], BF16, es) for i in range(n)]
            B_w[:] = [Buf("w%d" % i) for i in range(n)]
            wn[0] = 0
        identf = cb[:, C_ID:C_ID + 128]
        triU = cb[:, C_TRIU:C_TRIU + 128]
        triSL = cb[:, C_TRISL:C_TRISL + 128]
        negcol = cb[:, C_NEG:C_NEG + 2]

        def wload(Wv, nk, col0, ncols, sub=0, width=None):
            i = wn[0] % len(wbuf)
            wn[0] += 1
            width = width or ncols
            view = wbuf[i][:, 0:nk * width].rearrange("p (k c) -> p k c", c=width)
            step = 8

            def fn(e, view=view, Wv=Wv):
                return [e.dma_start(out=view[:, k0:min(k0 + step, nk), sub:sub + ncols],
                                    in_=Wv[:, k0:min(k0 + step, nk), col0:col0 + ncols])
                        for k0 in range(0, nk, step)]

            S.dma("pool", fn, writes=[B_w[i]], n=(nk + step - 1) // step)
            return view, B_w[i]

        def wload2(Wv, nk, cols_a, cols_b, n_each):
            i = wn[0] % len(wbuf)
            wn[0] += 1
            width = 2 * n_each
            view = wbuf[i][:, 0:nk * width].rearrange("p (k c) -> p k c", c=width)
            step = 8

            def fn(e, view=view, Wv=Wv):
                ins = []
                for k0 in range(0, nk, step):
                    ins.append(e.dma_start(out=view[:, k0:k0 + step, 0:n_each], in_=Wv[:, k0:k0 + step, cols_a:cols_a + n_each]))
                    ins.append(e.dma_start(out=view[:, k0:k0 + step, n_each:width], in_=Wv[:, k0:k0 + step, cols_b:cols_b + n_each]))
                return ins

            S.dma("pool", fn, writes=[B_w[i]], n=2 * (nk // step))
            return view, B_w[i]

        S.dma("sp", DMA(cb[:], cb_d[:, :]), writes=[B_cb])
        S.dma("sp", DMA(gv[:], gvec[:, :]), writes=[B_gv])
        S.dma("sp", DMA(gbias[:], gbias_d[:, :]), writes=[B_gbias])
        S.op("dve", CP(identb[:], cb[:, C_ID:C_ID + 128]), reads=[B_cb], writes=[B_cb])
        S.op("dve", CP(permb[:], cb[:, C_PERM:C_PERM + 128]), reads=[B_cb], writes=[B_cb])
        S.op("dve", CP(mutb[:], cb[:, C_MUT:C_MUT + 128]), reads=[B_cb], writes=[B_cb])
        S.op("dve", CP(cmb[:], cb[:, C_CM:C_CM + 512]), reads=[B_cb], writes=[B_cb])
        S.end_phase()

        def norm_transpose(es, src, hT, B_hT, goff, ps_t, B_pst, x1_dst=None):
            xt = [sb("xt%d" % i, [128, D], F32, es) for i in range(2)]
            xn = [sb("xn%d" % i, [128, D], BF16, es) for i in range(2)]
            st = sb("nt_st", [128, 4, NT], F32, es)
            B_xt = [Buf(), Buf()]
            B_xn = [Buf(), Buf()]
            B_st = [Buf() for _ in range(NT)]
            for i in range(NT):
                j = i % 2
                S.dma("sp", DMA(xt[j][:], src[i * 128:(i + 1) * 128, :]), writes=[B_xt[j]])
                S.op("act", ACT(xn[j][:], xt[j][:], AF.Square, accum_out=st[:, 0, i:i + 1]),
                     reads=[B_xt[j]], writes=[B_xn[j], B_st[i]])
                S.op("dve", TS(st[:, 1, i:i + 1], st[:, 0, i:i + 1], 1.0 / D, EPS, ALU.mult, ALU.add),
                     reads=[B_st[i]], writes=[B_st[i]])
                S.op("act", ACT(st[:, 2, i:i + 1], st[:, 1, i:i + 1], AF.Sqrt), reads=[B_st[i]], writes=[B_st[i]])
                S.op("dve", RCP(st[:, 3, i:i + 1], st[:, 2, i:i + 1]), reads=[B_st[i]], writes=[B_st[i]])
                S.op("act", ACT(xn[j][:], xt[j][:], AF.Copy, scale=st[:, 3, i:i + 1]),
                     reads=[B_xt[j], B_st[i]], writes=[B_xn[j]])
                for g in range(4):
                    pj = g % 2
                    for c in range(8):
                        k = g * 8 + c
                        S.op("pe", TR(ps_t[pj][:, c * 128:(c + 1) * 128], xn[j][:, k * 128:(k + 1) * 128], identb[:]),
                             reads=[B_xn[j], B_cb], writes=[B_pst[pj]])
                    S.op("dve", TT(hT[:, g * 8:(g + 1) * 8, i * 128:(i + 1) * 128],
                                   ps_t[pj][:].rearrange("p (a b) -> p a b", b=128),
                                   gv[:, goff + g * 8:goff + (g + 1) * 8].unsqueeze(2).to_broadcast([128, 8, 128]),
                                   ALU.mult),
                         reads=[B_pst[pj], B_gv], writes=[B_hT[i]])

        def rope_evict(ps, B_ps, dst, B_dst, pos0, ps_r, B_psr, tmp, B_tmp, cosT, sinT, B_rope):
            xb, ta, tb = tmp
            RC = int(os.environ.get('DBG_RCUT', '9'))
            S.op("act", ACT(xb[:], ps[:], AF.Copy), reads=[B_ps], writes=[B_tmp[0]])
            if RC == 1:
                return
            S.op("pe", MM(ps_r[:], permb[:], xb[:]), reads=[B_tmp[0], B_cb], writes=[B_psr])
            if RC == 2:
                return
            S.op("dve", TT(ta[:], ps[:], cosT[:, pos0:pos0 + 512], ALU.mult), reads=[B_ps, B_rope], writes=[B_tmp[1]])
            if RC == 3:
                return
            S.op("dve", TT(tb[:], ps_r[:], sinT[:, pos0:pos0 + 512], ALU.mult), reads=[B_psr, B_rope], writes=[B_tmp[2]])
            if RC == 4:
                return
            S.op("dve", TT(dst, ta[:], tb[:], ALU.add), reads=[B_tmp[1], B_tmp[2]], writes=[B_dst])

        def run_pass(own):
            with ExitStack() as es:
                src = x_own if own else x_pre
                alloc_w(es, 3)
                hT = sb("hT", [128, KT, TOK], BF16, es)
                B_hT = [Buf("hT%d" % i) for i in range(NT)]
                pss = [psum("ps%d" % i, [128, 512], F32, es) for i in range(7)]
                B_ps = [PB("ps%d" % i) for i in range(7)]
                ps_t = psum("pst", [128, 1024], BF16, es)
                B_pst1 = PB("pst")
                with ExitStack() as es2:
                    norm_transpose(es2, src, hT, B_hT, 0, [ps_t, ps_t], [B_pst1, B_pst1])
                if "hT_dbg" in DEBUG_OUT and own:
                    hT_dbg = nc.dram_tensor("hT_dbg", [128, KT, TOK], BF16, kind="ExternalOutput").ap()
                    S.dma("sp", DMA(hT_dbg[:, :, :], hT[:]), reads=B_hT)
                S.end_phase()
                pos_base = 1024 if own else 0
                ntg = 2

                with ExitStack() as es2:
                    cosT = sb("cosT", [128, 2048], F32, es2)
                    sinT = sb("sinT", [128, 2048], F32, es2)
                    B_rope = Buf("rope")
                    S.dma("sp", DMA(cosT[:], cosT_d[:, :]), writes=[B_rope])
                    S.dma("sp", DMA(sinT[:], sinT_d[:, :]), reads=[B_rope], writes=[B_rope])
                    KTb = sb("KTb", [128, 2, 2048], BF16, es2)
                    Vb = sb("Vb", [128, 16, 2, 129], BF16, es2)
                    QTb = sb("QTb", [128, 2, 1024], BF16, es2)
                    xb = sb("r_xb", [128, 512], BF16, es2)
                    ta = sb("r_ta", [128, 512], F32, es2)
                    tb = sb("r_tb", [128, 512], F32, es2)
                    B_tmp = [Buf(), Buf(), Buf()]
                    B_KT = [[Buf() for _ in range(4)] for _ in range(2)]
                    B_V = [Buf() for _ in range(16)]
                    B_QT = [[Buf() for _ in range(2)] for _ in range(2)]
                    B_Vones = Buf()
                    S.op("dve", MSET(Vb[:, :, :, 128:129], 1.0), writes=B_V)
                    if own:
                        ksb = sb("ksb", [128, 8], BF16, es2)
                        Gb = sb("Gb", [128, 8, 8], F32, es2)
                        top8 = sb("top8", [128, 8, 8], F32, es2)
                        thr = sb("thr", [128, 8], F32, es2)
                        sel = sb("sel", [128, 8, 8], F32, es2)
                        PT = [sb("PT%d" % i, [128, 2, 256], BF16, es2) for i in range(2)]
                        Oacc = sb("Oacc", [128, 8, 129], F32, es2)
                        rec = sb("rec", [128, 8], F32, es2)
                        mo_tok = sb("mo_tok", [128, 8, 256], BF16, es2)
                        moT_g = sb("moT_g", [128, 2, 1024], BF16, es2)
                        B_ksb, B_Gb, B_top8, B_thr, B_sel = Buf(), Buf(), Buf(), Buf(), Buf()
                        B_PT = [Buf(), Buf()]
                        B_Oacc = [Buf() for _ in range(8)]
                        B_rec = Buf()
                        B_mo = [Buf() for _ in range(8)]
                        B_moTg = Buf()
                    for g in range(8):
                        if own:
                            for hl in range(2):
                                S.dma("sp", DMA(KTb[:, hl, 0:1024], kt_s[2 * g + hl, :, :]), writes=[B_KT[hl][0], B_KT[hl][1]])
                            S.dma("sp", DMA(Vb[:, 0:8, :, :].rearrange("p a b c -> p (a b c)"), v_s[g, :, :]), writes=B_V[0:8])
                        wk, Bwk = wload(w_in_v, KT, OK_ + g * 256, 256)
                        if CUT == 1:
                            break
                        for hl in range(2):
                            for tg in range(ntg):
                                ps = pss[tg % 2]
                                Bp = B_ps[tg % 2]
                                for k in range(KT):
                                    S.op("pe", MM(ps[:], wk[:, k, hl * 128:(hl + 1) * 128], hT[:, k, tg * 512:(tg + 1) * 512],
                                                  start=(k == 0), stop=(k == KT - 1)),
                                         reads=[Bwk] + B_hT[tg * 4:(tg + 1) * 4], writes=[Bp])
                                if CUT == 2:
                                    continue
                                gi = (2 if own else 0) + tg
                                rope_evict(ps, Bp, KTb[:, hl, gi * 512:(gi + 1) * 512], B_KT[hl][gi], pos_base + tg * 512,
                                           pss[2], B_ps[2], (xb, ta, tb), B_tmp, cosT, sinT, B_rope)
                            if CUT in (2, 3):
                                continue
                            b0 = 4 if own else 0
                            S.op("dve", lambda e, hl=hl, b0=b0, g=g: e.tensor_reduce(
                                out=ksum[:, 2 * g + hl, b0:b0 + 4],
                                in_=KTb[:, hl, b0 * 256:(b0 + 4) * 256].rearrange("p (b t) -> p b t", t=256),
                                axis=AX.X, op=ALU.add),
                                 reads=B_KT[hl][(2 if own else 0):(4 if own else 2)], writes=[B_ksum[2 * g + hl]])
                        if CUT in (2, 3, 4):
                            break
                        wv, Bwv = wload(w_in_v, KT, OV + g * 256, 256)
                        for i in range(NT):
                            ps = pss[3 + i % 2]
                            Bp = B_ps[3 + i % 2]
                            for k in range(KT):
                                S.op("pe", MM(ps[:, 0:256], hT[:, k, i * 128:(i + 1) * 128], wv[:, k, :],
                                              start=(k == 0), stop=(k == KT - 1)),
                                     reads=[Bwv, B_hT[i]], writes=[Bp])
                            ti = (8 if own else 0) + i
                            S.op("act", ACT(Vb[:, ti, :, 0:128], ps[:, 0:256].rearrange("p (a b) -> p a b", b=128), AF.Copy),
                                 reads=[Bp], writes=[B_V[ti]])
                        if CUT == 5:
                            break
                        if not own:
                            for hl in range(2):
                                S.dma("sp", DMA(kt_s[2 * g + hl, :, :], KTb[:, hl, 0:1024]), reads=[B_KT[hl][0], B_KT[hl][1]])
                            S.dma("sp", DMA(v_s[g, :, :], Vb[:, 0:8, :, :].rearrange("p a b c -> p (a b c)")), reads=B_V[0:8])
                            continue
                        wq, Bwq = wload(w_in_v, KT, OQ + g * 256, 256)
                        for hl in range(2):
                            for tg in range(ntg):
                                ps = pss[tg % 2]
                                Bp = B_ps[tg % 2]
                                for k in range(KT):
                                    S.op("pe", MM(ps[:], wq[:, k, hl * 128:(hl + 1) * 128], hT[:, k, tg * 512:(tg + 1) * 512],
                                                  start=(k == 0), stop=(k == KT - 1)),
                                         reads=[Bwq] + B_hT[tg * 4:(tg + 1) * 4], writes=[Bp])
                                rope_evict(ps, Bp, QTb[:, hl, tg * 512:(tg + 1) * 512], B_QT[hl][tg], pos_base + tg * 512,
                                           pss[2], B_ps[2], (xb, ta, tb), B_tmp, cosT, sinT, B_rope)
                        for hl in range(2):
                            h = 2 * g + hl
                            S.op("act", ACT(ksb[:], ksum[:, h, :], AF.Copy), reads=[B_ksum[h]], writes=[B_ksb])
                            gps = pss[2]
                            for qt in range(8):
                                S.op("pe", MM(gps[:, qt * 8:(qt + 1) * 8], QTb[:, hl, qt * 128:(qt + 1) * 128], ksb[:]),
                                     reads=[B_QT[hl][qt // 4], B_ksb], writes=[B_ps[2]])
                            S.op("dve", TT(Gb[:].rearrange("p a b -> p (a b)"), gps[:, 0:64], gbias[:], ALU.add),
                                 reads=[B_ps[2], B_gbias], writes=[B_Gb])
                            for qt in range(8):
                                S.op("dve", lambda e, qt=qt: e.max(out=top8[:, qt, :], in_=Gb[:, qt, :]),
                                     reads=[B_Gb], writes=[B_top8])
                            S.op("dve", TS(thr[:], top8[:, :, 2], -1.0e30, None, ALU.max), reads=[B_top8], writes=[B_thr])
                            S.op("dve", TT(sel[:], Gb[:], thr[:].unsqueeze(2).to_broadcast([128, 8, 8]), ALU.is_ge),
                                 reads=[B_Gb, B_thr], writes=[B_sel])
                            step = 0
                            for jb in range(4):
                                for n in range(5 + jb):
                                    ownb = (n == 4 + jb)
                                    sps = pss[step % 2]
                                    Bsp = B_ps[step % 2]
                                    pt = PT[step % 2]
                                    Bpt = B_PT[step % 2]
                                    ops = pss[5 + step % 2]
                                    Bop = B_ps[5 + step % 2]
                                    step += 1
                                    for a in range(2):
                                        kti = 2 * n + a
                                        S.op("pe", MM(sps[:, a * 256:(a + 1) * 256], KTb[:, hl, kti * 128:(kti + 1) * 128],
                                                      QTb[:, hl, jb * 256:(jb + 1) * 256]),
                                             reads=[B_KT[hl][kti // 4], B_QT[hl][jb // 2]], writes=[Bsp])
                                    S.op("act", ACT(pt[:].rearrange("p a b -> p (a b)"), sps[:], AF.Exp, scale=128.0 ** -0.5),
                                         reads=[Bsp], writes=[Bpt])
                                    if ownb:
                                        S.op("dve", TT(pt[:].rearrange("p a b -> p (a b)"), pt[:].rearrange("p a b -> p (a b)"),
                                                       cmb[:], ALU.mult), reads=[Bpt, B_cb], writes=[Bpt])
                                    for b in range(2):
                                        qt = jb * 2 + b
                                        alist = [0] if (ownb and b == 0) else [0, 1]
                                        for ai, a in enumerate(alist):
                                            S.op("pe", MM(ops[:, b * 160:b * 160 + 129], pt[:, a, b * 128:(b + 1) * 128],
                                                          Vb[:, 2 * n + a, hl, :], start=(ai == 0), stop=(ai == len(alist) - 1)),
                                                 reads=[Bpt, B_V[2 * n + a]], writes=[Bop])
                                        o_ps = ops[:, b * 160:b * 160 + 129]
                                        if n == 0:
                                            S.op("dve", TS(Oacc[:, qt, :], o_ps, sel[:, qt, 0:1], None, ALU.mult),
                                                 reads=[Bop, B_sel], writes=[B_Oacc[qt]])
                                        elif ownb:
                                            S.op("dve", TT(Oacc[:, qt, :], o_ps, Oacc[:, qt, :], ALU.add),
                                                 reads=[Bop, B_Oacc[qt]], writes=[B_Oacc[qt]])
                                        else:
                                            S.op("dve", STT(Oacc[:, qt, :], o_ps, sel[:, qt, n:n + 1], Oacc[:, qt, :], ALU.mult, ALU.add),
                                                 reads=[Bop, B_sel, B_Oacc[qt]], writes=[B_Oacc[qt]])
                            S.op("dve", RCP(rec[:], Oacc[:, :, 128]), reads=B_Oacc, writes=[B_rec])
                            for qt in range(8):
                                S.op("dve", TS(mo_tok[:, qt, hl * 128:(hl + 1) * 128], Oacc[:, qt, 0:128], rec[:, qt:qt + 1], None, ALU.mult),
                                     reads=[B_Oacc[qt], B_rec], writes=[B_mo[qt]])
                        for qt in range(8):
                            for hl in range(2):
                                S.op("pe", TR(ps_t[:, hl * 128:(hl + 1) * 128], mo_tok[:, qt, hl * 128:(hl + 1) * 128], identb[:]),
                                     reads=[B_mo[qt], B_cb], writes=[B_pst1])
                            S.op("act", ACT(moT_g[:, :, qt * 128:(qt + 1) * 128], ps_t[:, 0:256].rearrange("p (a b) -> p a b", b=128), AF.Copy),
                                 reads=[B_pst1], writes=[B_moTg])
                        for hl in range(2):
                            S.dma("sp", DMA(moT_s[2 * g + hl, :, :], moT_g[:, hl, :]), reads=[B_moTg])
                S.end_phase()

                with ExitStack() as es2:
                    gadT = sb("gadT", [16, TOK], F32, es2)
                    gup = sb("gup_sb", [16, 2048], F32, es2)
                    glab = sb("glab_sb", [128, 2048], F32, es2)
                    g512 = sb("g512_sb", [128, 512], F32, es2)
                    B_gad, B_gup, B_glab, B_g512 = Buf(), Buf(), Buf(), Buf()
                    S.dma("sp", DMA(gup[:], gup_d[:, :]), writes=[B_gup])
                    S.dma("sp", DMA(glab[:], glab_d[:, :]), writes=[B_glab])
                    S.dma("sp", DMA(g512[:], g512_d[:, :]), writes=[B_g512])
                    wga, Bwga = wload(w_in_v, KT, OGA, 16)
                    for tg in range(2):
                        for k in range(KT):
                            S.op("pe", MM(pss[0][0:16, :], wga[:, k, 0:16], hT[:, k, tg * 512:(tg + 1) * 512],
                                          start=(k == 0), stop=(k == KT - 1)),
                                 reads=[Bwga] + B_hT[tg * 4:(tg + 1) * 4], writes=[B_ps[0]])
                        S.op("act", ACT(gadT[:, tg * 512:(tg + 1) * 512], pss[0][0:16, :], AF.Copy), reads=[B_ps[0]], writes=[B_gad])
                    zb = sb("zb", [128, 256], F32, es2)
                    e1 = sb("e1", [128, 256], F32, es2)
                    lg_all = sb("lg_all", [128, NT, 256], F32, es2)
                    dec = [sb("dec%d" % i, [128, 256], F32, es2) for i in range(2)]
                    ebl = sb("ebl", [128, 4], F32, es2)
                    qd_all = sb("qd_all", [128, NT, 256], BF16, es2)
                    ki_all = sb("ki_all", [128, NT, 256], BF16, es2)
                    ke_all = sb("ke_all", [128, NT, 256], BF16, es2)
                    vb_all = sb("vb_all", [128, NT, 512], BF16, es2)
                    sg_all = sb("sg_all", [128, NT, 512], BF16, es2)
                    qdT = sb("qdT", [128, 2, 128], BF16, es2)
                    kiT = sb("kiT", [128, 2, 128], BF16, es2)
                    At = sb("At", [128, 128], BF16, es2)
                    Sbf = sb("Sbf", [128, 2, 512], BF16, es2)
                    st_h = [sb("st_h%d" % i, [128, 2, 512], F32, es2) for i in range(2)]
                    gst = sb("gst", [128, 4], F32, es2)
                    on = sb("on", [128, 512], F32, es2)
                    ot = sb("ot", [128, 512], BF16, es2)
                    junk2 = sb("junk2", [128, 512], BF16, es2)
                    goT_h = sb("goT_h", [128, 4, TOK], BF16, es2)
                    (B_zb, B_e1, B_ebl, B_qdT, B_kiT, B_At, B_Sbf, B_gst, B_on, B_ot, B_junk2, B_goT) = [Buf() for _ in range(12)]
                    B_lg = [Buf() for _ in range(NT)]
                    B_qd = [Buf() for _ in range(NT)]
                    B_ki = [Buf() for _ in range(NT)]
                    B_ke = [Buf() for _ in range(NT)]
                    B_vb = [Buf() for _ in range(NT)]
                    B_sg = [Buf() for _ in range(NT)]
                    B_dec = [Buf(), Buf()]
                    B_sth = [Buf(), Buf()]
                    z_ps, bc_ps = pss[3][:, 0:256], pss[3][:, 256:512]
                    rv_ps, bl_ps, A_ps = pss[4][:, 0:256], pss[4][:, 256:260], pss[4][:, 264:392]
                    B_z, B_bc = Buf("z", B_ps[3].tok), Buf("bc", B_ps[3].tok)
                    B_rv, B_bl, B_A = Buf("rv", B_ps[4].tok), Buf("bl", B_ps[4].tok), Buf("A", B_ps[4].tok)
                    O_ps, kv_ps = pss[5], pss[6]
                    B_O, B_kv = B_ps[5], B_ps[6]
                    B_tq, B_to = Buf("tq", B_pst1.tok), Buf("to", B_pst1.tok)
                    dn = [0]

                    def nextdec():
                        j = dn[0] % 2
                        dn[0] += 1
                        return dec[j], B_dec[j]

                    pn = [0]

                    def proj(wv_, Bw_, i, ncol=256):
                        j = pn[0] % 3
                        pn[0] += 1
                        ps, Bp = pss[j], B_ps[j]
                        for k in range(KT):
                            S.op("pe", MM(ps[:, 0:ncol], hT[:, k, i * 128:(i + 1) * 128], wv_[:, k, :], start=(k == 0), stop=(k == KT - 1)),
                                 reads=[Bw_, B_hT[i]], writes=[Bp])
                        return ps[:, 0:ncol], Bp

                    for hh in range(8):
                        sth, Bsth = st_h[hh % 2], B_sth[hh % 2]
                        if own:
                            S.dma("sp", DMA(sth[:].rearrange("p a b -> p (a b)"), st_s[hh, :, :]), writes=[Bsth])
                        else:
                            S.op("dve", MSET(sth[:], 0.0), writes=[Bsth])
                        for i in range(NT):
                            tok = slice(i * 128, (i + 1) * 128)
                            S.op("pe", MM(z_ps, gadT[0:16, tok], gup[0:16, hh * 256:(hh + 1) * 256]),
                                 reads=[B_gad, B_gup], writes=[B_z])
                            S.op("dve", TT(zb[:], z_ps, glab[:, hh * 256:(hh + 1) * 256], ALU.add), reads=[B_z, B_glab], writes=[B_zb])
                            S.op("act", ACT(e1[:], zb[:], AF.Exp, scale=-1.0), reads=[B_zb], writes=[B_e1])
                            S.op("act", ACT(lg_all[:, i, :], e1[:], AF.Ln, bias=1.0), reads=[B_e1], writes=[B_lg[i]])
                        if own:
                            wq_, Bwq_ = wload(w_in_v, KT, OGQ + hh * 256, 256)
                            for i in range(NT):
                                S.op("pe", MM(bc_ps, triU, lg_all[:, i, :]), reads=[B_lg[i], B_cb], writes=[B_bc])
                                d_, Bd_ = nextdec()
                                S.op("act", ACT(d_[:], bc_ps, AF.Exp), reads=[B_bc], writes=[Bd_])
                                ps, Bp = proj(wq_, Bwq_, i)
                                S.op("dve", STT(qd_all[:, i, :], ps, 1.0 / 16.0, d_[:], ALU.mult, ALU.mult), reads=[Bp, Bd_], writes=[B_qd[i]])
                        wk_, Bwk_ = wload(w_in_v, KT, OGK + hh * 256, 256)
                        for i in range(NT):
                            S.op("pe", MM(rv_ps, triSL, lg_all[:, i, :]), reads=[B_lg[i], B_cb], writes=[B_rv])
                            d_, Bd_ = nextdec()
                            S.op("act", ACT(d_[:], rv_ps, AF.Exp), reads=[B_rv], writes=[Bd_])
                            if own:
                                S.op("pe", MM(bc_ps, triU, lg_all[:, i, :]), reads=[B_lg[i], B_cb], writes=[B_bc])
                                d2_, Bd2_ = nextdec()
                                S.op("act", ACT(d2_[:], bc_ps, AF.Exp, scale=-1.0), reads=[B_bc], writes=[Bd2_])
                            ps, Bp = proj(wk_, Bwk_, i)
                            S.op("dve", TT(ke_all[:, i, :], ps, d_[:], ALU.mult), reads=[Bp, Bd_], writes=[B_ke[i]])
                            if own:
                                S.op("dve", TT(ki_all[:, i, :], ps, d2_[:], ALU.mult), reads=[Bp, Bd2_], writes=[B_ki[i]])
                        for vh in range(2):
                            wv_, Bwv_ = wload(w_in_v, KT, OGV + hh * 512 + vh * 256, 256)
                            for i in range(NT):
                                ps, Bp = proj(wv_, Bwv_, i)
                                S.op("act", ACT(vb_all[:, i, vh * 256:(vh + 1) * 256], ps, AF.Copy), reads=[Bp], writes=[B_vb[i]])
                        if own:
                            for vh in range(2):
                                wg_, Bwg_ = wload(w_in_v, KT, OGR + hh * 512 + vh * 256, 256)
                                for i in range(NT):
                                    ps, Bp = proj(wg_, Bwg_, i)
                                    S.op("act", ACT(sg_all[:, i, vh * 256:(vh + 1) * 256], ps, AF.Silu), reads=[Bp], writes=[B_sg[i]])
                            S.op("act", ACT(Sbf[:], sth[:], AF.Copy), reads=[Bsth], writes=[B_Sbf])
                        for i in range(NT):
                            tok = slice(i * 128, (i + 1) * 128)
                            for dt in range(2):
                                S.op("pe", MM(bl_ps[:, 2 * dt:2 * dt + 2], lg_all[:, i, dt * 128:(dt + 1) * 128], negcol),
                                     reads=[B_lg[i], B_cb], writes=[B_bl])
                            S.op("act", ACT(ebl[:], bl_ps, AF.Exp), reads=[B_bl], writes=[B_ebl])
                            if own:
                                for dt in range(2):
                                    S.op("pe", TR(ps_t[:, dt * 128:(dt + 1) * 128], qd_all[:, i, dt * 128:(dt + 1) * 128], identb[:]),
                                         reads=[B_qd[i], B_cb], writes=[B_tq])
                                    S.op("pe", TR(ps_t[:, 256 + dt * 128:256 + (dt + 1) * 128], ki_all[:, i, dt * 128:(dt + 1) * 128], identb[:]),
                                         reads=[B_ki[i], B_cb], writes=[B_tq])
                                S.op("act", ACT(qdT[:], ps_t[:, 0:256].rearrange("p (a b) -> p a b", b=128), AF.Copy), reads=[B_tq], writes=[B_qdT])
                                S.op("act", ACT(kiT[:], ps_t[:, 256:512].rearrange("p (a b) -> p a b", b=128), AF.Copy), reads=[B_tq], writes=[B_kiT])
                                for dt in range(2):
                                    S.op("pe", MM(A_ps, kiT[:, dt, :], qdT[:, dt, :], start=(dt == 0), stop=(dt == 1)),
                                         reads=[B_kiT, B_qdT], writes=[B_A])
                                S.op("dve", TT(At[:], A_ps, mutb[:], ALU.mult), reads=[B_A, B_cb], writes=[B_At])
                                S.op("pe", MM(O_ps[:], At[:], vb_all[:, i, :], start=True, stop=False), reads=[B_At, B_vb[i]], writes=[B_O])
                                for dt in range(2):
                                    S.op("pe", MM(O_ps[:], qdT[:, dt, :], Sbf[:, dt, :], start=False, stop=(dt == 1)),
                                         reads=[B_qdT, B_Sbf], writes=[B_O])
                            for dt in range(2):
                                S.op("pe", MM(kv_ps[:], ke_all[:, i, dt * 128:(dt + 1) * 128], vb_all[:, i, :]), reads=[B_ke[i], B_vb[i]], writes=[B_kv])
                                S.op("dve", STT(sth[:, dt, :], sth[:, dt, :], ebl[:, 2 * dt:2 * dt + 1], kv_ps[:], ALU.mult, ALU.add),
                                     reads=[Bsth, B_ebl, B_kv], writes=[Bsth])
                            if own:
                                if i < NT - 1:
                                    S.op("act", ACT(Sbf[:], sth[:], AF.Copy), reads=[Bsth], writes=[B_Sbf])
                                S.op("act", ACT(junk2[:], O_ps[:], AF.Square, accum_out=gst[:, 0:1]), reads=[B_O], writes=[B_junk2, B_gst])
                                S.op("dve", TS(gst[:, 1:2], gst[:, 0:1], 1.0 / 512.0, EPS, ALU.mult, ALU.add), reads=[B_gst], writes=[B_gst])
                                S.op("act", ACT(gst[:, 2:3], gst[:, 1:2], AF.Sqrt), reads=[B_gst], writes=[B_gst])
                                S.op("dve", RCP(gst[:, 3:4], gst[:, 2:3]), reads=[B_gst], writes=[B_gst])
                                S.op("dve", STT(on[:], O_ps[:], gst[:, 3:4], g512[:], ALU.mult, ALU.mult), reads=[B_O, B_gst, B_g512], writes=[B_on])
                                S.op("dve", TT(ot[:], on[:], sg_all[:, i, :], ALU.mult), reads=[B_on, B_sg[i]], writes=[B_ot])
                                for c in range(4):
                                    S.op("pe", TR(ps_t[:, 512 + c * 128:512 + (c + 1) * 128], ot[:, c * 128:(c + 1) * 128], identb[:]),
                                         reads=[B_ot, B_cb], writes=[B_to])
                                S.op("act", ACT(goT_h[:, :, tok], ps_t[:, 512:1024].rearrange("p (a b) -> p a b", b=128), AF.Copy),
                                     reads=[B_to], writes=[B_goT])
                        if own:
                            for c in range(4):
                                S.dma("sp", DMA(goT_s[hh * 4 + c, :, :], goT_h[:, c, :]), reads=[B_goT])
                        else:
                            S.dma("sp", DMA(st_s[hh, :, :], sth[:].rearrange("p a b -> p (a b)")), reads=[Bsth])
                S.end_phase()
                if not own:
                    return

                with ExitStack() as es2:
                    sgt = [sb("sgt%d" % i, [128, 4, TOK], BF16, es2) for i in range(2)]
                    B_sgt = [Buf(), Buf()]
                    ci = 0
                    for (off, dst) in ((OGM, sgm_s), (OGG, sgg_s)):
                        for cc in range(16):
                            wg_, Bwg_ = wload(w_in_v, KT, off + cc * 256, 256)
                            st_ = sgt[ci % 2]
                            Bst_ = B_sgt[ci % 2]
                            ci += 1
                            for sub in range(2):
                                for tg in range(2):
                                    ps = pss[(sub * 2 + tg) % 4]
                                    Bp = B_ps[(sub * 2 + tg) % 4]
                                    for k in range(KT):
                                        S.op("pe", MM(ps[:], wg_[:, k, sub * 128:(sub + 1) * 128], hT[:, k, tg * 512:(tg + 1) * 512],
                                                      start=(k == 0), stop=(k == KT - 1)),
                                             reads=[Bwg_] + B_hT[tg * 4:(tg + 1) * 4], writes=[Bp])
                                    S.op("act", ACT(st_[:, sub, tg * 512:(tg + 1) * 512], ps[:], AF.Sigmoid), reads=[Bp], writes=[Bst_])
                            for sub in range(2):
                                S.dma("sp", DMA(dst[cc * 2 + sub, :, :], st_[:, sub, :]), reads=[Bst_])
                S.end_phase()
            return None

        run_pass(False)
        run_pass(True)

        with ExitStack() as es:
            pss = [psum("ps%d" % i, [128, 512], F32, es) for i in range(8)]
            B_ps = [PB("ps%d" % i) for i in range(8)]
            mT = sb("mT", [128, KT, TOK], BF16, es)
            B_mT = [Buf() for _ in range(NT)]
            with ExitStack() as es2:
                alloc_w(es2, 2, 4096)
                moT = sb("moT", [128, 16, TOK], BF16, es2)
                goT = sb("goT", [128, 32, TOK], BF16, es2)
                B_moT, B_goT2 = Buf(), Buf()
                B_moTp = [Buf() for _ in range(2)]
                B_goTp = [Buf() for _ in range(4)]
                for q_ in range(2):
                    S.dma("sp", DMA(moT[:, q_ * 8:(q_ + 1) * 8, :], moT_s[q_ * 8:(q_ + 1) * 8, :, :].rearrange("h p t -> p h t")), writes=[B_moTp[q_]])
                for q_ in range(4):
                    S.dma("sp", DMA(goT[:, q_ * 8:(q_ + 1) * 8, :], goT_s[q_ * 8:(q_ + 1) * 8, :, :].rearrange("h p t -> p h t")), writes=[B_goTp[q_]])
                sgm = [sb("sgm%d" % i, [128, TOK], BF16, es2) for i in range(2)]
                sgg = [sb("sgg%d" % i, [128, TOK], BF16, es2) for i in range(2)]
                B_sgm, B_sgg = [Buf(), Buf()], [Buf(), Buf()]
                t1 = sb("t1", [128, 512], F32, es2)
                t2 = sb("t2", [128, 512], F32, es2)
                B_t1, B_t2 = Buf(), Buf()
                for c in range(32):
                    j = c % 2
                    wm, Bwm = wload(w_bm_v, 16, c * 128, 128)
                    wg_, Bwg_ = wload(w_bg_v, KT, c * 128, 128)
                    S.dma("sp", DMA(sgm[j][:], sgm_s[c, :, :]), writes=[B_sgm[j]])
                    S.dma("sp", DMA(sgg[j][:], sgg_s[c, :, :]), writes=[B_sgg[j]])
                    for tg in range(2):
                        pm, Bpm = pss[tg * 2], B_ps[tg * 2]
                        pg, Bpg = pss[tg * 2 + 1], B_ps[tg * 2 + 1]
                        for k in range(16):
                            S.op("pe", MM(pm[:], wm[:, k, :], moT[:, k, tg * 512:(tg + 1) * 512],
                                          start=(k == 0), stop=(k == 15)), reads=[Bwm] + B_moTp, writes=[Bpm])
                        for k in range(KT):
                            S.op("pe", MM(pg[:], wg_[:, k, :], goT[:, k, tg * 512:(tg + 1) * 512],
                                          start=(k == 0), stop=(k == KT - 1)), reads=[Bwg_] + B_goTp, writes=[Bpg])
                        S.op("dve", TT(t1[:], pm[:], sgm[j][:, tg * 512:(tg + 1) * 512], ALU.mult), reads=[Bpm, B_sgm[j]], writes=[B_t1])
                        S.op("dve", TT(t2[:], pg[:], sgg[j][:, tg * 512:(tg + 1) * 512], ALU.mult), reads=[Bpg, B_sgg[j]], writes=[B_t2])
                        S.op("dve", TT(mT[:, c, tg * 512:(tg + 1) * 512], t1[:], t2[:], ALU.add), reads=[B_t1, B_t2],
                             writes=B_mT[tg * 4:(tg + 1) * 4])
                S.end_phase()
            alloc_w(es, 3)
            ybuf = [sb("ybuf%d" % i, [128, 256], F32, es) for i in range(2)]
            B_yb = [Buf(), Buf()]
            ci = 0
            for cc in range(16):
                wo, Bwo = wload(w_out_v, KT, cc * 256, 256)
                for i in range(NT):
                    ps, Bp = pss[4 + ci % 4], B_ps[4 + ci % 4]
                    yb, Byb = ybuf[ci % 2], B_yb[ci % 2]
                    ci += 1
                    for k in range(KT):
                        S.op("pe", MM(ps[:, 0:256], mT[:, k, i * 128:(i + 1) * 128], wo[:, k, :], start=(k == 0), stop=(k == KT - 1)),
                             reads=[Bwo, B_mT[i]], writes=[Bp])
                    S.op("act", ACT(yb[:], ps[:, 0:256], AF.Copy), reads=[Bp], writes=[Byb])
                    S.dma("sp", DMA(y_s[i * 128:(i + 1) * 128, cc * 256:(cc + 1) * 256], yb[:]), reads=[Byb])
            S.end_phase()

        with ExitStack() as es:
            h2T = sb("h2T", [128, KT, TOK], BF16, es)
            B_h2T = [Buf() for _ in range(NT)]
            ps_t = psum("pst", [128, 1024], BF16, es)
            B_pst1 = PB("pst")
            gpm = sb("gpm_sb", [128, D], F32, es)
            B_gpm = Buf()
            S.dma("sp", DMA(gpm[:], gpm_d[:, :]), writes=[B_gpm])
            yt = [sb("yt%d" % i, [128, D], F32, es) for i in range(2)]
            xt = [sb("xt%d" % i, [128, D], F32, es) for i in range(2)]
            x1t = [sb("x1t%d" % i, [128, D], F32, es) for i in range(2)]
            xn = [sb("xn%d" % i, [128, D], BF16, es) for i in range(2)]
            st = sb("st4", [128, 8, NT], F32, es)
            B_yt, B_xt, B_x1t, B_xn = [Buf(), Buf()], [Buf(), Buf()], [Buf(), Buf()], [Buf(), Buf()]
            B_junk = Buf()
            B_st = [Buf() for _ in range(NT)]
            for i in range(NT):
                j = i % 2
                rows = slice(i * 128, (i + 1) * 128)
                S.dma("sp", DMA(yt[j][:], y_s[rows, :]), writes=[B_yt[j]])
                S.dma("sp", DMA(xt[j][:], x_own[rows, :]), writes=[B_xt[j]])
                S.op("act", ACT(xn[j][:], yt[j][:], AF.Square, accum_out=st[:, 0, i:i + 1]), reads=[B_yt[j]], writes=[B_xn[j], B_st[i]])
                S.op("dve", TS(st[:, 1, i:i + 1], st[:, 0, i:i + 1], 1.0 / D, EPS, ALU.mult, ALU.add), reads=[B_st[i]], writes=[B_st[i]])
                S.op("act", ACT(st[:, 2, i:i + 1], st[:, 1, i:i + 1], AF.Sqrt), reads=[B_st[i]], writes=[B_st[i]])
                S.op("dve", RCP(st[:, 3, i:i + 1], st[:, 2, i:i + 1]), reads=[B_st[i]], writes=[B_st[i]])
                S.op("dve", STT(x1t[j][:], yt[j][:], st[:, 3, i:i + 1], gpm[:], ALU.mult, ALU.mult),
                     reads=[B_yt[j], B_st[i], B_gpm], writes=[B_x1t[j]])
                S.op("dve", TT(x1t[j][:], x1t[j][:], xt[j][:], ALU.add), reads=[B_x1t[j], B_xt[j]], writes=[B_x1t[j]])
                S.dma("sp", DMA(x1_s[rows, :], x1t[j][:]), reads=[B_x1t[j]])
                S.op("act", ACT(xn[j][:], x1t[j][:], AF.Square, accum_out=st[:, 4, i:i + 1]), reads=[B_x1t[j]], writes=[B_xn[j], B_st[i]])
                S.op("dve", TS(st[:, 5, i:i + 1], st[:, 4, i:i + 1], 1.0 / D, EPS, ALU.mult, ALU.add), reads=[B_st[i]], writes=[B_st[i]])
                S.op("act", ACT(st[:, 6, i:i + 1], st[:, 5, i:i + 1], AF.Sqrt), reads=[B_st[i]], writes=[B_st[i]])
                S.op("dve", RCP(st[:, 7, i:i + 1], st[:, 6, i:i + 1]), reads=[B_st[i]], writes=[B_st[i]])
                S.op("act", ACT(xn[j][:], x1t[j][:], AF.Copy, scale=st[:, 7, i:i + 1]), reads=[B_x1t[j], B_st[i]], writes=[B_xn[j]])
                for g in range(4):
                    for c in range(8):
                        k = g * 8 + c
                        S.op("pe", TR(ps_t[:, c * 128:(c + 1) * 128], xn[j][:, k * 128:(k + 1) * 128], identb[:]),
                             reads=[B_xn[j], B_cb], writes=[B_pst1])
                    S.op("dve", TT(h2T[:, g * 8:(g + 1) * 8, i * 128:(i + 1) * 128],
                                   ps_t[:].rearrange("p (a b) -> p a b", b=128),
                                   gv[:, 32 + g * 8:32 + (g + 1) * 8].unsqueeze(2).to_broadcast([128, 8, 128]), ALU.mult),
                         reads=[B_pst1, B_gv], writes=[B_h2T[i]])
            for half in range(2):
                S.dma("sp", DMA(h2T_s[half, :, :, :], h2T[:, :, half * 512:(half + 1) * 512]), reads=B_h2T[half * 4:(half + 1) * 4])
            S.end_phase()

        with ExitStack() as es:
            pss = [psum("ps%d" % i, [128, 512], F32, es) for i in range(8)]
            B_ps = [PB("ps%d" % i) for i in range(8)]
            alloc_w(es, 3)
            hid = sb("hid", [128, FT, 512], BF16, es)
            B_hid = [Buf() for _ in range(FT)]
            sgf = [sb("sgf%d" % i, [128, 512], F32, es) for i in range(2)]
            B_sgf = [Buf(), Buf()]
            y2b = [sb("y2b%d" % i, [128, 512], F32, es) for i in range(2)]
            B_y2b = [Buf(), Buf()]
            h2Th = sb("h2Th", [128, KT, 512], BF16, es)
            B_h2Th = Buf()
            for half in range(2):
                S.dma("sp", DMA(h2Th[:], h2T_s[half, :, :, :]), writes=[B_h2Th])
                ci = 0
                for cc in range(43):
                    wg_, Bwg_ = wload(w_fg_v, KT, cc * 256, 256)
                    wu_, Bwu_ = wload(w_fu_v, KT, cc * 256, 256)
                    for sub in range(2):
                        f = cc * 2 + sub
                        pg, Bpg = pss[(ci % 2) * 2], B_ps[(ci % 2) * 2]
                        pu, Bpu = pss[(ci % 2) * 2 + 1], B_ps[(ci % 2) * 2 + 1]
                        sg_, Bsg_ = sgf[ci % 2], B_sgf[ci % 2]
                        ci += 1
                        for k in range(KT):
                            S.op("pe", MM(pg[:], wg_[:, k, sub * 128:(sub + 1) * 128], h2Th[:, k, :], start=(k == 0), stop=(k == KT - 1)),
                                 reads=[Bwg_, B_h2Th], writes=[Bpg])
                        for k in range(KT):
                            S.op("pe", MM(pu[:], wu_[:, k, sub * 128:(sub + 1) * 128], h2Th[:, k, :], start=(k == 0), stop=(k == KT - 1)),
                                 reads=[Bwu_, B_h2Th], writes=[Bpu])
                        S.op("act", ACT(sg_[:], pg[:], AF.Silu), reads=[Bpg], writes=[Bsg_])
                        S.op("dve", TT(hid[:, f, :], sg_[:], pu[:], ALU.mult), reads=[Bsg_, Bpu], writes=[B_hid[f]])
                S.end_phase()
                ci = 0
                for qd_ in range(4):
                    c0 = qd_ * 1024
                    for f0 in range(0, FT, 8):
                        nf = min(8, FT - f0)
                        i3 = wn[0] % len(wbuf)
                        wn[0] += 1
                        wd = wbuf[i3][:, 0:8 * 1024].rearrange("p (k c) -> p k c", c=1024)
                        Bwd = B_w[i3]

                        def fn(e, wd=wd, f0=f0, nf=nf, c0=c0):
                            return [e.dma_start(out=wd[:, 0:nf, :], in_=w_fd_v[:, f0:f0 + nf, c0:c0 + 1024])]

                        S.dma("pool", fn, writes=[Bwd], n=1)
                        for fi in range(nf):
                            f = f0 + fi
                            for tt in range(4):
                                for cg in range(2):
                                    S.op("pe", MM(pss[tt * 2 + cg][:], hid[:, f, tt * 128:(tt + 1) * 128], wd[:, fi, cg * 512:(cg + 1) * 512],
                                                  start=(f == 0), stop=(f == FT - 1)),
                                         reads=[Bwd, B_hid[f]], writes=[B_ps[tt * 2 + cg]])
                    for tt in range(4):
                        for cg in range(2):
                            yb, Byb = y2b[ci % 2], B_y2b[ci % 2]
                            ci += 1
                            S.op("act", ACT(yb[:], pss[tt * 2 + cg][:], AF.Copy), reads=[B_ps[tt * 2 + cg]], writes=[Byb])
                            r0 = half * 512 + tt * 128
                            S.dma("sp", DMA(y2_s[r0:r0 + 128, c0 + cg * 512:c0 + (cg + 1) * 512], yb[:]), reads=[Byb])
                S.end_phase()

        with ExitStack() as es:
            gpf = sb("gpf_sb", [128, D], F32, es)
            B_gpf = Buf()
            S.dma("sp", DMA(gpf[:], gpf_d[:, :]), writes=[B_gpf])
            yt = [sb("yt%d" % i, [128, D], F32, es) for i in range(2)]
            xt = [sb("xt%d" % i, [128, D], F32, es) for i in range(2)]
            ot_ = [sb("ot%d" % i, [128, D], F32, es) for i in range(2)]
            junk = sb("junk", [128, D], BF16, es)
            st = sb("st5", [128, 4, NT], F32, es)
            B_yt, B_xt, B_ot = [Buf(), Buf()], [Buf(), Buf()], [Buf(), Buf()]
            B_junk = Buf()
            B_st = [Buf() for _ in range(NT)]
            for i in range(NT):
                j = i % 2
                rows = slice(i * 128, (i + 1) * 128)
                S.dma("sp", DMA(yt[j][:], y2_s[rows, :]), writes=[B_yt[j]])
                S.dma("sp", DMA(xt[j][:], x1_s[rows, :]), writes=[B_xt[j]])
                S.op("act", ACT(junk[:], yt[j][:], AF.Square, accum_out=st[:, 0, i:i + 1]), reads=[B_yt[j]], writes=[B_junk, B_st[i]])
                S.op("dve", TS(st[:, 1, i:i + 1], st[:, 0, i:i + 1], 1.0 / D, EPS, ALU.mult, ALU.add), reads=[B_st[i]], writes=[B_st[i]])
                S.op("act", ACT(st[:, 2, i:i + 1], st[:, 1, i:i + 1], AF.Sqrt), reads=[B_st[i]], writes=[B_st[i]])
                S.op("dve", RCP(st[:, 3, i:i + 1], st[:, 2, i:i + 1]), reads=[B_st[i]], writes=[B_st[i]])
                S.op("dve", STT(ot_[j][:], yt[j][:], st[:, 3, i:i + 1], gpf[:], ALU.mult, ALU.mult),
                     reads=[B_yt[j], B_st[i], B_gpf], writes=[B_ot[j]])
                S.op("dve", TT(ot_[j][:], ot_[j][:], xt[j][:], ALU.add), reads=[B_ot[j], B_xt[j]], writes=[B_ot[j]])
                S.dma("sp", DMA(out_d[rows, :], ot_[j][:]), reads=[B_ot[j]])
            S.end_phase()


def _consts():
    cb = np.zeros((128, C_END), np.float32)
    idx = np.arange(128)
    cb[:, C_ID:C_ID + 128] = np.eye(128, dtype=np.float32)
    perm = np.zeros((128, 128), np.float32)
    perm[(idx + 64) % 128, idx] = 1.0
    cb[:, C_PERM:C_PERM + 128] = perm
    le = (idx[:, None] <= idx[None, :]).astype(np.float32)
    gt = (idx[:, None] > idx[None, :]).astype(np.float32)
    cb[:, C_TRIU:C_TRIU + 128] = le * (-1.0 / 16.0)
    cb[:, C_TRISL:C_TRISL + 128] = gt * (-1.0 / 16.0)
    cb[:, C_MUT:C_MUT + 128] = le
    cm = np.zeros((128, 2, 256), np.float32)
    q = np.arange(256)
    for a in range(2):
        cm[:, a, :] = ((a * 128 + idx)[:, None] <= q[None, :]).astype(np.float32)
    cb[:, C_CM:C_CM + 512] = cm.reshape(128, 512)
    cb[:, C_NEG:C_NEG + 2] = -1.0 / 16.0
    half = 64
    inv_freq = (10000.0 ** (-np.arange(half, dtype=np.float32) / half)).astype(np.float32)
    ang = np.arange(2048, dtype=np.float32)[None, :] * inv_freq[:, None]
    cos = np.cos(ang).astype(np.float32)
    sin = np.sin(ang).astype(np.float32)
    cosT = np.concatenate([cos, cos], axis=0)
    sinT = np.concatenate([-sin, sin], axis=0)
    return cb, np.ascontiguousarray(cosT), np.ascontiguousarray(sinT)


def _gbias(odd):
    gb = np.full((128, 8, 8), NEG, np.float32)
    for qt in range(8):
        jb = qt // 2
        if odd:
            gb[:, qt, 0:4] = 0.0
        gb[:, qt, 4:4 + jb] = 0.0
    return gb.reshape(128, 64)


_NC_CACHE = {}


def _in_maps(x, pre_mix_norm_g, w_in, gla_gate_up, gla_gate_bias, gla_out_norm_g,
             w_branch_moba, w_branch_gla, w_out, post_mix_norm_g, pre_ffn_norm_g,
             w_ffn_gate, w_ffn_up, w_ffn_down, post_ffn_norm_g):
    f = lambda a: np.ascontiguousarray(np.asarray(a, dtype=np.float32))
    x = f(x)
    cb, cosT, sinT = _consts()
    gvec = np.concatenate([f(pre_mix_norm_g)[0].reshape(32, 128).T, f(pre_ffn_norm_g)[0].reshape(32, 128).T], axis=1)
    common = {
        "w_in": f(w_in)[0], "w_bm": f(w_branch_moba)[0], "w_bg": f(w_branch_gla)[0], "w_out": f(w_out)[0],
        "w_fg": f(w_ffn_gate)[0], "w_fu": f(w_ffn_up)[0], "w_fd": f(w_ffn_down)[0],
        "gup": f(gla_gate_up)[0], "gvec": np.ascontiguousarray(gvec),
        "glab": np.ascontiguousarray(np.broadcast_to(f(gla_gate_bias)[0][None, :], (128, 2048))),
        "g512": np.ascontiguousarray(np.broadcast_to(f(gla_out_norm_g)[0][None, :], (128, 512))),
        "gpm": np.ascontiguousarray(np.broadcast_to(f(post_mix_norm_g)[0][None, :], (128, D))),
        "gpf": np.ascontiguousarray(np.broadcast_to(f(post_ffn_norm_g)[0][None, :], (128, D))),
        "cosT": cosT, "sinT": sinT, "cblob": cb,
    }
    zeros = np.zeros((TOK, D), np.float32)
    in_maps = []
    for c in range(8):
        b, hf = c // 2, c % 2
        m = dict(common)
        m["x_own"] = np.ascontiguousarray(x[b, hf * TOK:(hf + 1) * TOK])
        m["x_pre"] = np.ascontiguousarray(x[b, 0:TOK]) if hf == 1 else zeros
        m["gbias"] = _gbias(hf == 1)
        in_maps.append(m)
    return in_maps


def kernel(**inputs):
    if "nc" not in _NC_CACHE:
        _NC_CACHE["nc"] = build_program()
    nc = _NC_CACHE["nc"]
    in_maps = _in_maps(**inputs)
    res = run_bass_kernel_spmd(nc, in_maps, core_ids=list(range(8)))
    out = np.empty((4, 2048, D), np.float32)
    for c in range(8):
        b, hf = c // 2, c % 2
        out[b, hf * TOK:(hf + 1) * TOK] = res.results[c]["out"]
    return out
```

```python
from contextlib import ExitStack
import numpy as np
import concourse.bass as bass
import concourse.mybir as mybir
from concourse.bass_utils import run_bass_kernel_spmd

F32 = mybir.dt.float32
BF16 = mybir.dt.bfloat16
AF = mybir.ActivationFunctionType
ALU = mybir.AluOpType
AX = mybir.AxisListType

D = 4096
KT = 32
TOK = 1024
NT = 8
DFF = 11008
FT = 86
EPS = 1e-6
OQ, OK_, OV, OGQ, OGK, OGV, OGR, OGA, OGM, OGG = 0, 2048, 4096, 6144, 8192, 10240, 14336, 18432, 18448, 22544
INC = 26640
NEG = -3.0e38
C_ID, C_PERM, C_TRIU, C_TRISL, C_MUT, C_CM, C_NEG, C_END = 0, 128, 256, 384, 512, 640, 1152, 1154

STOP_AFTER = None
DEBUG_OUT = ()
SHRINK = ()


class StopBuild(Exception):
    pass


class Buf:
    __slots__ = ("name", "writer", "readers", "tok")

    def __init__(self, name="", tok=None):
        self.name = name
        self.writer = None
        self.readers = {}
        self.tok = tok


def PB(name=""):
    return Buf(name, tok=Buf(name + "_tok"))


class Rec:
    __slots__ = ("eng", "fn", "deps", "signal", "val", "sem", "is_dma", "phase", "n")

    def __init__(self, eng, fn, phase):
        self.eng = eng
        self.fn = fn
        self.deps = []
        self.signal = False
        self.val = None
        self.sem = None
        self.is_dma = False
        self.phase = phase
        self.n = 1


class Slot:
    __slots__ = ("sem", "count", "last")

    def __init__(self, sem):
        self.sem = sem
        self.count = 0
        self.last = None


ENGS = ("pe", "act", "dve", "pool", "sp")


class Sched:
    def __init__(self, nc, esem, sp_sems, pool_sems):
        self.nc = nc
        self.esem = esem
        self.cnt = {e: 0 for e in ENGS}
        self.dpool = {"sp": [Slot(s) for s in sp_sems], "pool": [Slot(s) for s in pool_sems]}
        self.dnext = {"sp": 0, "pool": 0}
        self.waited = {e: {} for e in ENGS}
        self.phase = 0
        self.q = {e: [] for e in ENGS}
        self.ninst = 0

    def _deps(self, r, reads, writes):
        deps = {}

        def add(d, kind):
            if d is None or d is r or d.phase != self.phase:
                return
            if (not d.is_dma) and (not r.is_dma) and d.eng == r.eng:
                if r.eng == "pe":
                    return
            deps[id(d)] = d

        for b in reads:
            add(b.writer, "w")
        for b in writes:
            add(b.writer, "w")
            for rd in b.readers.values():
                add(rd, "r")
        for d in deps.values():
            r.deps.append(d)
            if not d.is_dma:
                d.signal = True
        for b in reads:
            b.readers[("dma", id(r)) if r.is_dma else r.eng] = r
        for b in writes:
            b.writer = r
            b.readers = {}

    def op(self, eng, fn, reads=(), writes=()):
        toks = []
        for b in list(reads) + list(writes):
            if b.tok is not None and b.tok not in toks:
                toks.append(b.tok)
        if toks:
            writes = list(writes) + toks
        r = Rec(eng, fn, self.phase)
        self._deps(r, reads, writes)
        self.q[eng].append(r)
        return r

    def dma(self, queue, fn, reads=(), writes=(), n=1):
        pool = self.dpool[queue]
        slot = pool[self.dnext[queue] % len(pool)]
        self.dnext[queue] += 1
        r = Rec(queue, fn, self.phase)
        r.is_dma = True
        r.sem = slot.sem
        r.n = n
        if slot.last is not None and slot.last.phase == self.phase:
            r.deps.append(slot.last)
        slot.count += 16 * n
        r.val = slot.count
        slot.last = r
        self._deps(r, reads, writes)
        self.q[queue].append(r)
        return r

    def end_phase(self):
        deps = []
        for e in ("pe", "act", "dve", "pool"):
            for r in reversed(self.q[e]):
                if r.fn is not None:
                    deps.append(r)
                    break
        for queue in self.dpool:
            for slot in self.dpool[queue]:
                if slot.last is not None and slot.last.phase == self.phase:
                    deps.append(slot.last)
        b = Rec("sp", lambda e: e.nop(), self.phase)
        seen = set()
        for d in deps:
            if id(d) in seen:
                continue
            seen.add(id(d))
            b.deps.append(d)
            if not d.is_dma:
                d.signal = True
        b.signal = True
        self.q["sp"].append(b)
        for e in ("pe", "act", "dve", "pool"):
            rr = Rec(e, None, self.phase)
            rr.deps = [b]
            self.q[e].append(rr)
        for e in ENGS:
            for r in self.q[e]:
                if (not r.is_dma) and r.signal:
                    self.cnt[e] += 1
                    r.val = self.cnt[e]
        nc = self.nc
        with nc.Block() as blk:
            for e, reg in (("pe", blk.tensor), ("act", blk.scalar), ("dve", blk.vector),
                           ("pool", blk.gpsimd), ("sp", blk.sync)):
                lst = self.q[e]
                waited = self.waited[e]
                esem = self.esem

                def body(eng, e=e, lst=lst, waited=waited):
                    for r in lst:
                        for d in r.deps:
                            sem = d.sem if d.is_dma else esem[d.eng]
                            val = d.val
                            assert val is not None
                            key = id(sem)
                            if waited.get(key, 0) >= val:
                                continue
                            eng.wait_ge(sem, val)
                            waited[key] = val
                        if r.fn is None:
                            continue
                        ins = r.fn(eng)
                        self.ninst += 1
                        if r.is_dma:
                            if isinstance(ins, (list, tuple)):
                                assert len(ins) == r.n
                                for i_ in ins:
                                    i_.then_inc(r.sem, 16)
                            else:
                                assert r.n == 1
                                ins.then_inc(r.sem, 16)
                        elif r.signal:
                            ins.then_inc(esem[e], 1)

                reg(body)
        self.q = {e: [] for e in ENGS}
        self.phase += 1
        build_program.ninst = self.ninst
        if STOP_AFTER is not None and self.phase > STOP_AFTER:
            raise StopBuild()


def MM(out, lhsT, rhs, start=True, stop=True):
    return lambda e: e.matmul(out, lhsT, rhs, start=start, stop=stop)


def TR(out, in_, ident):
    return lambda e: e.transpose(out, in_, ident)


def ACT(out, in_, func, bias=0.0, scale=1.0, accum_out=None):
    if accum_out is None:
        return lambda e: e.activation(out=out, in_=in_, func=func, bias=bias, scale=scale)
    return lambda e: e.activation(out=out, in_=in_, func=func, bias=bias, scale=scale, accum_out=accum_out)


def TT(out, a, b, op):
    return lambda e: e.tensor_tensor(out=out, in0=a, in1=b, op=op)


def TS(out, a, s1, s2, op0, op1=None):
    if op1 is None:
        return lambda e: e.tensor_scalar(out=out, in0=a, scalar1=s1, scalar2=None, op0=op0)
    return lambda e: e.tensor_scalar(out=out, in0=a, scalar1=s1, scalar2=s2, op0=op0, op1=op1)


def STT(out, a, s, b, op0, op1):
    return lambda e: e.scalar_tensor_tensor(out=out, in0=a, scalar=s, in1=b, op0=op0, op1=op1)


def CP(out, in_):
    return lambda e: e.tensor_copy(out=out, in_=in_)


def RCP(out, in_):
    return lambda e: e.reciprocal(out=out, in_=in_)


def DMA(out, in_):
    return lambda e: e.dma_start(out=out, in_=in_)


def MSET(ap, v):
    return lambda e: e.memset(ap, v)


def build_program():
    nc = bass.Bass("TRN2", target_bir_lowering=False)
    try:
        _build(nc)
    except StopBuild:
        pass
    return nc


def _build(nc):

    def din(name, shape, dt=F32):
        if name in SHRINK:
            t = nc.dram_tensor(name + "_tiny", [128, 128], dt, kind="ExternalInput")
            return nc.dram_tensor(name + "_fake", list(shape), dt, kind="Internal").ap()
        return nc.dram_tensor(name, list(shape), dt, kind="ExternalInput").ap()

    def dscr(name, shape, dt):
        kind = "ExternalOutput" if name in DEBUG_OUT else "Internal"
        return nc.dram_tensor(name, list(shape), dt, kind=kind).ap()

    x_own = din("x_own", [TOK, D])
    x_pre = din("x_pre", [TOK, D])
    w_in = din("w_in", [D, INC])
    w_bm = din("w_bm", [2048, D])
    w_bg = din("w_bg", [D, D])
    w_out = din("w_out", [D, D])
    w_fg = din("w_fg", [D, DFF])
    w_fu = din("w_fu", [D, DFF])
    w_fd = din("w_fd", [DFF, D])
    gup_d = din("gup", [16, 2048])
    gvec = din("gvec", [128, 64])
    gbias_d = din("gbias", [128, 64])
    glab_d = din("glab", [128, 2048])
    g512_d = din("g512", [128, 512])
    gpm_d = din("gpm", [128, D])
    gpf_d = din("gpf", [128, D])
    cosT_d = din("cosT", [128, 2048])
    sinT_d = din("sinT", [128, 2048])
    cb_d = din("cblob", [128, C_END])
    out_d = nc.dram_tensor("out", [TOK, D], F32, kind="ExternalOutput").ap()

    kt_s = dscr("kt_s", [16, 128, 1024], BF16)
    v_s = dscr("v_s", [8, 128, 8 * 2 * 129], BF16)
    moT_s = dscr("moT_s", [16, 128, 1024], BF16)
    goT_s = dscr("goT_s", [32, 128, 1024], BF16)
    sgm_s = dscr("sgm_s", [32, 128, 1024], BF16)
    sgg_s = dscr("sgg_s", [32, 128, 1024], BF16)
    y_s = dscr("y_s", [TOK, D], F32)
    x1_s = dscr("x1_s", [TOK, D], F32)
    y2_s = dscr("y2_s", [TOK, D], F32)
    st_s = dscr("st_s", [8, 128, 1024], F32)
    h2T_s = dscr("h2T_s", [2, 128, KT, 512], BF16)

    def wview(W):
        return W.rearrange("(kt p) n -> p kt n", p=128)

    w_in_v, w_bm_v, w_bg_v, w_out_v = wview(w_in), wview(w_bm), wview(w_bg), wview(w_out)
    w_fg_v, w_fu_v, w_fd_v = wview(w_fg), wview(w_fu), wview(w_fd)

    with ExitStack() as G:
        uid = [0]

        def sb(name, shape, dt, es=G):
            uid[0] += 1
            return es.enter_context(nc.sbuf_tensor("%s_s%d" % (name, uid[0]), list(shape), dt))

        def psum(name, shape, dt, es):
            uid[0] += 1
            return es.enter_context(nc.psum_tensor("%s_p%d" % (name, uid[0]), list(shape), dt))

        nsp, npool = 40, 8
        esem = {e: G.enter_context(nc.semaphore("e_" + e)) for e in ENGS}
        sp_sems = [G.enter_context(nc.semaphore("dsp%d" % i)) for i in range(nsp)]
        pool_sems = [G.enter_context(nc.semaphore("dpl%d" % i)) for i in range(npool)]
        S = Sched(nc, esem, sp_sems, pool_sems)

        cb = sb("cb", [128, C_END], F32)
        identb = sb("identb", [128, 128], BF16)
        permb = sb("permb", [128, 128], BF16)
        mutb = sb("mutb", [128, 128], BF16)
        cmb = sb("cmb", [128, 512], BF16)
        gv = sb("gv", [128, 64], F32)
        gbias = sb("gbias_sb", [128, 64], F32)
        ksum = sb("ksum", [128, 16, 8], F32)
        wbuf = []
        B_w = []
        B_cb, B_gv, B_gbias = Buf("cb"), Buf("gv"), Buf("gbias")
        B_ksum = [Buf("ksum%d" % h) for h in range(16)]
        wn = [0]

        def alloc_w(es, n, size=8192):
            wbuf[:] = [sb("wbuf%d" % i, [128, size], BF16, es) for i in range(n)]
            B_w[:] = [Buf("w%d" % i) for i in range(n)]
            wn[0] = 0
        identf = cb[:, C_ID:C_ID + 128]
        triU = cb[:, C_TRIU:C_TRIU + 128]
        triSL = cb[:, C_TRISL:C_TRISL + 128]
        negcol = cb[:, C_NEG:C_NEG + 2]

        def wload(Wv, nk, col0, ncols, sub=0, width=None):
            i = wn[0] % len(wbuf)
            wn[0] += 1
            width = width or ncols
            view = wbuf[i][:, 0:nk * width].rearrange("p (k c) -> p k c", c=width)
            step = 8

            def fn(e, view=view, Wv=Wv):
                return [e.dma_start(out=view[:, k0:min(k0 + step, nk), sub:sub + ncols],
                                    in_=Wv[:, k0:min(k0 + step, nk), col0:col0 + ncols])
                        for k0 in range(0, nk, step)]

            S.dma("pool", fn, writes=[B_w[i]], n=(nk + step - 1) // step)
            return view, B_w[i]

        def wload2(Wv, nk, cols_a, cols_b, n_each):
            i = wn[0] % len(wbuf)
            wn[0] += 1
            width = 2 * n_each
            view = wbuf[i][:, 0:nk * width].rearrange("p (k c) -> p k c", c=width)
            step = 8

            def fn(e, view=view, Wv=Wv):
                ins = []
                for k0 in range(0, nk, step):
                    ins.append(e.dma_start(out=view[:, k0:k0 + step, 0:n_each], in_=Wv[:, k0:k0 + step, cols_a:cols_a + n_each]))
                    ins.append(e.dma_start(out=view[:, k0:k0 + step, n_each:width], in_=Wv[:, k0:k0 + step, cols_b:cols_b + n_each]))
                return ins

            S.dma("pool", fn, writes=[B_w[i]], n=2 * (nk // step))
            return view, B_w[i]

        S.dma("sp", DMA(cb[:], cb_d[:, :]), writes=[B_cb])
        S.dma("sp", DMA(gv[:], gvec[:, :]), writes=[B_gv])
        S.dma("sp", DMA(gbias[:], gbias_d[:, :]), writes=[B_gbias])
        S.op("dve", CP(identb[:], cb[:, C_ID:C_ID + 128]), reads=[B_cb], writes=[B_cb])
        S.op("dve", CP(permb[:], cb[:, C_PERM:C_PERM + 128]), reads=[B_cb], writes=[B_cb])
        S.op("dve", CP(mutb[:], cb[:, C_MUT:C_MUT + 128]), reads=[B_cb], writes=[B_cb])
        S.op("dve", CP(cmb[:], cb[:, C_CM:C_CM + 512]), reads=[B_cb], writes=[B_cb])
        S.end_phase()

        def norm_transpose(es, src, hT, B_hT, goff, ps_t, B_pst, x1_dst=None):
            xt = [sb("xt%d" % i, [128, D], F32, es) for i in range(2)]
            xn = [sb("xn%d" % i, [128, D], BF16, es) for i in range(2)]
            st = sb("nt_st", [128, 4, NT], F32, es)
            B_xt = [Buf(), Buf()]
            B_xn = [Buf(), Buf()]
            B_st = [Buf() for _ in range(NT)]
            for i in range(NT):
                j = i % 2
                S.dma("sp", DMA(xt[j][:], src[i * 128:(i + 1) * 128, :]), writes=[B_xt[j]])
                S.op("act", ACT(xn[j][:], xt[j][:], AF.Square, accum_out=st[:, 0, i:i + 1]),
                     reads=[B_xt[j]], writes=[B_xn[j], B_st[i]])
                S.op("dve", TS(st[:, 1, i:i + 1], st[:, 0, i:i + 1], 1.0 / D, EPS, ALU.mult, ALU.add),
                     reads=[B_st[i]], writes=[B_st[i]])
                S.op("act", ACT(st[:, 2, i:i + 1], st[:, 1, i:i + 1], AF.Sqrt), reads=[B_st[i]], writes=[B_st[i]])
                S.op("dve", RCP(st[:, 3, i:i + 1], st[:, 2, i:i + 1]), reads=[B_st[i]], writes=[B_st[i]])
                S.op("act", ACT(xn[j][:], xt[j][:], AF.Copy, scale=st[:, 3, i:i + 1]),
                     reads=[B_xt[j], B_st[i]], writes=[B_xn[j]])
                for g in range(4):
                    pj = g % 2
                    for c in range(8):
                        k = g * 8 + c
                        S.op("pe", TR(ps_t[pj][:, c * 128:(c + 1) * 128], xn[j][:, k * 128:(k + 1) * 128], identb[:]),
                             reads=[B_xn[j], B_cb], writes=[B_pst[pj]])
                    S.op("dve", TT(hT[:, g * 8:(g + 1) * 8, i * 128:(i + 1) * 128],
                                   ps_t[pj][:].rearrange("p (a b) -> p a b", b=128),
                                   gv[:, goff + g * 8:goff + (g + 1) * 8].unsqueeze(2).to_broadcast([128, 8, 128]),
                                   ALU.mult),
                         reads=[B_pst[pj], B_gv], writes=[B_hT[i]])

        def rope_evict(ps, B_ps, dst, B_dst, pos0, ps_r, B_psr, tmp, B_tmp, cosT, sinT, B_rope):
            xb, ta, tb = tmp
            S.op("act", ACT(xb[:], ps[:], AF.Copy), reads=[B_ps], writes=[B_tmp[0]])
            S.op("pe", MM(ps_r[:], permb[:], xb[:]), reads=[B_tmp[0], B_cb], writes=[B_psr])
            S.op("dve", TT(ta[:], ps[:], cosT[:, pos0:pos0 + 512], ALU.mult), reads=[B_ps, B_rope], writes=[B_tmp[1]])
            S.op("dve", TT(tb[:], ps_r[:], sinT[:, pos0:pos0 + 512], ALU.mult), reads=[B_psr, B_rope], writes=[B_tmp[2]])
            S.op("dve", TT(dst, ta[:], tb[:], ALU.add), reads=[B_tmp[1], B_tmp[2]], writes=[B_dst])

        def run_pass(own):
            with ExitStack() as es:
                src = x_own if own else x_pre
                alloc_w(es, 3)
                hT = sb("hT", [128, KT, TOK], BF16, es)
                B_hT = [Buf("hT%d" % i) for i in range(NT)]
                pss = [psum("ps%d" % i, [128, 512], F32, es) for i in range(7)]
                B_ps = [PB("ps%d" % i) for i in range(7)]
                ps_t = psum("pst", [128, 1024], BF16, es)
                B_pst1 = PB("pst")
                with ExitStack() as es2:
                    norm_transpose(es2, src, hT, B_hT, 0, [ps_t, ps_t], [B_pst1, B_pst1])
                if "hT_dbg" in DEBUG_OUT and own:
                    hT_dbg = nc.dram_tensor("hT_dbg", [128, KT, TOK], BF16, kind="ExternalOutput").ap()
                    S.dma("sp", DMA(hT_dbg[:, :, :], hT[:]), reads=B_hT)
                S.end_phase()
                pos_base = 1024 if own else 0
                ntg = 2

                with ExitStack() as es2:
                    cosT = sb("cosT", [128, 2048], F32, es2)
                    sinT = sb("sinT", [128, 2048], F32, es2)
                    B_rope = Buf("rope")
                    S.dma("sp", DMA(cosT[:], cosT_d[:, :]), writes=[B_rope])
                    S.dma("sp", DMA(sinT[:], sinT_d[:, :]), reads=[B_rope], writes=[B_rope])
                    nset = 2 if own else 1
                    KTs = [sb("KTb", [128, 2, 2048], BF16, es2) for _ in range(nset)]
                    Vs = [sb("Vb", [128, 16, 2, 129], BF16, es2) for _ in range(nset)]
                    QTs = [sb("QTb", [128, 2, 1024], BF16, es2) for _ in range(nset)]
                    xb = sb("r_xb", [128, 512], BF16, es2)
                    ta = sb("r_ta", [128, 512], F32, es2)
                    tb = sb("r_tb", [128, 512], F32, es2)
                    B_tmp = [Buf(), Buf(), Buf()]
                    B_KTs = [[[Buf() for _ in range(4)] for _ in range(2)] for _ in range(nset)]
                    B_Vs = [[Buf() for _ in range(16)] for _ in range(nset)]
                    B_QTs = [[[Buf() for _ in range(2)] for _ in range(2)] for _ in range(nset)]
                    for s_ in range(nset):
                        S.op("dve", MSET(Vs[s_][:, :, :, 128:129], 1.0), writes=B_Vs[s_])
                    if own:
                        ksb = sb("ksb", [128, 8], BF16, es2)
                        Gb = sb("Gb", [128, 8, 8], F32, es2)
                        top8 = sb("top8", [128, 8, 8], F32, es2)
                        thr = sb("thr", [128, 8], F32, es2)
                        sel = sb("sel", [128, 8, 8], F32, es2)
                        PT = [sb("PT%d" % i, [128, 2, 256], BF16, es2) for i in range(2)]
                        Oacc = sb("Oacc", [128, 8, 129], F32, es2)
                        rec = sb("rec", [128, 8], F32, es2)
                        mo_tok = sb("mo_tok", [128, 8, 256], BF16, es2)
                        moT_g = sb("moT_g", [128, 2, 1024], BF16, es2)
                        B_ksb, B_Gb, B_top8, B_thr, B_sel = Buf(), Buf(), Buf(), Buf(), Buf()
                        B_PT = [Buf(), Buf()]
                        B_Oacc = [Buf() for _ in range(8)]
                        B_rec = Buf()
                        B_mo = [Buf() for _ in range(8)]
                        B_moTg = Buf()

                    def proj_gen(g, s_):
                        KTb, Vb, QTb = KTs[s_], Vs[s_], QTs[s_]
                        B_KT, B_V, B_QT = B_KTs[s_], B_Vs[s_], B_QTs[s_]
                        if own:
                            for hl in range(2):
                                S.dma("sp", DMA(KTb[:, hl, 0:1024], kt_s[2 * g + hl, :, :]), writes=[B_KT[hl][0], B_KT[hl][1]])
                            S.dma("sp", DMA(Vb[:, 0:8, :, :].rearrange("p a b c -> p (a b c)"), v_s[g, :, :]), writes=B_V[0:8])
                        wk, Bwk = wload(w_in_v, KT, OK_ + g * 256, 256)
                        for hl in range(2):
                            for tg in range(ntg):
                                ps = pss[tg % 2]
                                Bp = B_ps[tg % 2]
                                for k in range(KT):
                                    S.op("pe", MM(ps[:], wk[:, k, hl * 128:(hl + 1) * 128], hT[:, k, tg * 512:(tg + 1) * 512],
                                                  start=(k == 0), stop=(k == KT - 1)),
                                         reads=[Bwk] + B_hT[tg * 4:(tg + 1) * 4], writes=[Bp])
                                gi = (2 if own else 0) + tg
                                rope_evict(ps, Bp, KTb[:, hl, gi * 512:(gi + 1) * 512], B_KT[hl][gi], pos_base + tg * 512,
                                           pss[2], B_ps[2], (xb, ta, tb), B_tmp, cosT, sinT, B_rope)
                                yield
                            b0 = 4 if own else 0
                            S.op("dve", lambda e, hl=hl, b0=b0, g=g, KTb=KTb: e.tensor_reduce(
                                out=ksum[:, 2 * g + hl, b0:b0 + 4],
                                in_=KTb[:, hl, b0 * 256:(b0 + 4) * 256].rearrange("p (b t) -> p b t", t=256),
                                axis=AX.X, op=ALU.add),
                                 reads=B_KT[hl][(2 if own else 0):(4 if own else 2)], writes=[B_ksum[2 * g + hl]])
                        wv, Bwv = wload(w_in_v, KT, OV + g * 256, 256)
                        for i in range(NT):
                            ps = pss[i % 2]
                            Bp = B_ps[i % 2]
                            for k in range(KT):
                                S.op("pe", MM(ps[:, 0:256], hT[:, k, i * 128:(i + 1) * 128], wv[:, k, :],
                                              start=(k == 0), stop=(k == KT - 1)),
                                     reads=[Bwv, B_hT[i]], writes=[Bp])
                            ti = (8 if own else 0) + i
                            S.op("act", ACT(Vb[:, ti, :, 0:128], ps[:, 0:256].rearrange("p (a b) -> p a b", b=128), AF.Copy),
                                 reads=[Bp], writes=[B_V[ti]])
                            if i % 2 == 1:
                                yield
                        if not own:
                            for hl in range(2):
                                S.dma("sp", DMA(kt_s[2 * g + hl, :, :], KTb[:, hl, 0:1024]), reads=[B_KT[hl][0], B_KT[hl][1]])
                            S.dma("sp", DMA(v_s[g, :, :], Vb[:, 0:8, :, :].rearrange("p a b c -> p (a b c)")), reads=B_V[0:8])
                            return
                        wq, Bwq = wload(w_in_v, KT, OQ + g * 256, 256)
                        for hl in range(2):
                            for tg in range(ntg):
                                ps = pss[tg % 2]
                                Bp = B_ps[tg % 2]
                                for k in range(KT):
                                    S.op("pe", MM(ps[:], wq[:, k, hl * 128:(hl + 1) * 128], hT[:, k, tg * 512:(tg + 1) * 512],
                                                  start=(k == 0), stop=(k == KT - 1)),
                                         reads=[Bwq] + B_hT[tg * 4:(tg + 1) * 4], writes=[Bp])
                                rope_evict(ps, Bp, QTb[:, hl, tg * 512:(tg + 1) * 512], B_QT[hl][tg], pos_base + tg * 512,
                                           pss[2], B_ps[2], (xb, ta, tb), B_tmp, cosT, sinT, B_rope)
                                yield

                    def attn_gen(g, s_):
                        KTb, Vb, QTb = KTs[s_], Vs[s_], QTs[s_]
                        B_KT, B_V, B_QT = B_KTs[s_], B_Vs[s_], B_QTs[s_]
                        for hl in range(2):
                            h = 2 * g + hl
                            S.op("act", ACT(ksb[:], ksum[:, h, :], AF.Copy), reads=[B_ksum[h]], writes=[B_ksb])
                            gps = pss[2]
                            for qt in range(8):
                                S.op("pe", MM(gps[:, qt * 8:(qt + 1) * 8], QTb[:, hl, qt * 128:(qt + 1) * 128], ksb[:]),
                                     reads=[B_QT[hl][qt // 4], B_ksb], writes=[B_ps[2]])
                            S.op("dve", TT(Gb[:].rearrange("p a b -> p (a b)"), gps[:, 0:64], gbias[:], ALU.add),
                                 reads=[B_ps[2], B_gbias], writes=[B_Gb])
                            for qt in range(8):
                                S.op("dve", lambda e, qt=qt: e.max(out=top8[:, qt, :], in_=Gb[:, qt, :]),
                                     reads=[B_Gb], writes=[B_top8])
                            S.op("dve", TS(thr[:], top8[:, :, 2], -1.0e30, None, ALU.max), reads=[B_top8], writes=[B_thr])
                            S.op("dve", TT(sel[:], Gb[:], thr[:].unsqueeze(2).to_broadcast([128, 8, 8]), ALU.is_ge),
                                 reads=[B_Gb, B_thr], writes=[B_sel])
                            yield
                            step = 0
                            for jb in range(4):
                                for n in range(5 + jb):
                                    ownb = (n == 4 + jb)
                                    sps = pss[3 + step % 2]
                                    Bsp = B_ps[3 + step % 2]
                                    pt = PT[step % 2]
                                    Bpt = B_PT[step % 2]
                                    ops = pss[5 + step % 2]
                                    Bop = B_ps[5 + step % 2]
                                    step += 1
                                    for a in range(2):
                                        kti = 2 * n + a
                                        S.op("pe", MM(sps[:, a * 256:(a + 1) * 256], KTb[:, hl, kti * 128:(kti + 1) * 128],
                                                      QTb[:, hl, jb * 256:(jb + 1) * 256]),
                                             reads=[B_KT[hl][kti // 4], B_QT[hl][jb // 2]], writes=[Bsp])
                                    S.op("act", ACT(pt[:].rearrange("p a b -> p (a b)"), sps[:], AF.Exp, scale=128.0 ** -0.5),
                                         reads=[Bsp], writes=[Bpt])
                                    if ownb:
                                        S.op("dve", TT(pt[:].rearrange("p a b -> p (a b)"), pt[:].rearrange("p a b -> p (a b)"),
                                                       cmb[:], ALU.mult), reads=[Bpt, B_cb], writes=[Bpt])
                                    for b in range(2):
                                        qt = jb * 2 + b
                                        alist = [0] if (ownb and b == 0) else [0, 1]
                                        for ai, a in enumerate(alist):
                                            S.op("pe", MM(ops[:, b * 160:b * 160 + 129], pt[:, a, b * 128:(b + 1) * 128],
                                                          Vb[:, 2 * n + a, hl, :], start=(ai == 0), stop=(ai == len(alist) - 1)),
                                                 reads=[Bpt, B_V[2 * n + a]], writes=[Bop])
                                        o_ps = ops[:, b * 160:b * 160 + 129]
                                        if n == 0:
                                            S.op("dve", TS(Oacc[:, qt, :], o_ps, sel[:, qt, 0:1], None, ALU.mult),
                                                 reads=[Bop, B_sel], writes=[B_Oacc[qt]])
                                        elif ownb:
                                            S.op("dve", TT(Oacc[:, qt, :], o_ps, Oacc[:, qt, :], ALU.add),
                                                 reads=[Bop, B_Oacc[qt]], writes=[B_Oacc[qt]])
                                        else:
                                            S.op("dve", STT(Oacc[:, qt, :], o_ps, sel[:, qt, n:n + 1], Oacc[:, qt, :], ALU.mult, ALU.add),
                                                 reads=[Bop, B_sel, B_Oacc[qt]], writes=[B_Oacc[qt]])
                                    yield
                            S.op("dve", RCP(rec[:], Oacc[:, :, 128]), reads=B_Oacc, writes=[B_rec])
                            for qt in range(8):
                                S.op("dve", TS(mo_tok[:, qt, hl * 128:(hl + 1) * 128], Oacc[:, qt, 0:128], rec[:, qt:qt + 1], None, ALU.mult),
                                     reads=[B_Oacc[qt], B_rec], writes=[B_mo[qt]])
                        for qt in range(8):
                            for hl in range(2):
                                S.op("pe", TR(ps_t[:, hl * 128:(hl + 1) * 128], mo_tok[:, qt, hl * 128:(hl + 1) * 128], identb[:]),
                                     reads=[B_mo[qt], B_cb], writes=[B_pst1])
                            S.op("act", ACT(moT_g[:, :, qt * 128:(qt + 1) * 128], ps_t[:, 0:256].rearrange("p (a b) -> p a b", b=128), AF.Copy),
                                 reads=[B_pst1], writes=[B_moTg])
                        for hl in range(2):
                            S.dma("sp", DMA(moT_s[2 * g + hl, :, :], moT_g[:, hl, :]), reads=[B_moTg])
                        yield

                    def drain(gen):
                        for _ in gen:
                            pass

                    if not own:
                        for g in range(8):
                            drain(proj_gen(g, 0))
                    else:
                        drain(proj_gen(0, 0))
                        for g in range(8):
                            A_ = attn_gen(g, g % 2)
                            P_ = proj_gen(g + 1, (g + 1) % 2) if g < 7 else iter(())
                            a_alive, p_alive = True, True
                            while a_alive or p_alive:
                                for _ in range(4):
                                    if a_alive:
                                        try:
                                            next(A_)
                                        except StopIteration:
                                            a_alive = False
                                if p_alive:
                                    try:
                                        next(P_)
                                    except StopIteration:
                                        p_alive = False
                S.end_phase()

                with ExitStack() as es2:
                    gadT = sb("gadT", [16, TOK], F32, es2)
                    gup = sb("gup_sb", [16, 2048], F32, es2)
                    glabs = [sb("glab_sb", [128, 256], F32, es2) for _ in range(2)]
                    B_glabs = [Buf(), Buf()]
                    g512 = sb("g512_sb", [128, 512], F32, es2)
                    B_gad, B_gup, B_glab, B_g512 = Buf(), Buf(), Buf(), Buf()
                    S.dma("sp", DMA(gup[:], gup_d[:, :]), writes=[B_gup])
                    S.dma("sp", DMA(g512[:], g512_d[:, :]), writes=[B_g512])
                    wga, Bwga = wload(w_in_v, KT, OGA, 16)
                    for tg in range(2):
                        for k in range(KT):
                            S.op("pe", MM(pss[0][0:16, :], wga[:, k, 0:16], hT[:, k, tg * 512:(tg + 1) * 512],
                                          start=(k == 0), stop=(k == KT - 1)),
                                 reads=[Bwga] + B_hT[tg * 4:(tg + 1) * 4], writes=[B_ps[0]])
                        S.op("act", ACT(gadT[:, tg * 512:(tg + 1) * 512], pss[0][0:16, :], AF.Copy), reads=[B_ps[0]], writes=[B_gad])
                    zb = sb("zb", [128, 256], F32, es2)
                    e1 = sb("e1", [128, 256], F32, es2)
                    lg_all = sb("lg_all", [128, NT, 256], F32, es2)
                    dec = [sb("dec%d" % i, [128, 256], F32, es2) for i in range(2)]
                    ebl = sb("ebl", [128, 4], F32, es2)
                    qd_all = sb("qd_all", [128, NT, 256], BF16, es2)
                    ki_all = sb("ki_all", [128, NT, 256], BF16, es2)
                    ke_all = sb("ke_all", [128, NT, 256], BF16, es2)
                    vb_all = sb("vb_all", [128, NT, 512], BF16, es2)
                    sg_all = sb("sg_all", [128, NT, 512], BF16, es2)
                    qdT = sb("qdT", [128, 2, 128], BF16, es2)
                    kiT = sb("kiT", [128, 2, 128], BF16, es2)
                    At = sb("At", [128, 128], BF16, es2)
                    Sbf = sb("Sbf", [128, 2, 512], BF16, es2)
                    st_h = [sb("st_h%d" % i, [128, 2, 512], F32, es2) for i in range(2)]
                    gst = sb("gst", [128, 4], F32, es2)
                    on_all = sb("on_all", [128, NT, 512], BF16, es2)
                    ot = sb("ot", [128, 512], BF16, es2)
                    goT_h = sb("goT_h", [128, 4, TOK], BF16, es2)
                    (B_zb, B_e1, B_ebl, B_qdT, B_kiT, B_At, B_Sbf, B_gst, B_ot, B_junk2, B_goT) = [Buf() for _ in range(11)]
                    B_on = [Buf() for _ in range(NT)]
                    B_lg = [Buf() for _ in range(NT)]
                    B_qd = [Buf() for _ in range(NT)]
                    B_ki = [Buf() for _ in range(NT)]
                    B_ke = [Buf() for _ in range(NT)]
                    B_vb = [Buf() for _ in range(NT)]
                    B_sg = [Buf() for _ in range(NT)]
                    B_dec = [Buf(), Buf()]
                    B_sth = [Buf(), Buf()]
                    z_ps, bc_ps = pss[3][:, 0:256], pss[3][:, 256:512]
                    rv_ps, bl_ps, A_ps = pss[4][:, 0:256], pss[4][:, 256:260], pss[4][:, 264:392]
                    B_z, B_bc = Buf("z", B_ps[3].tok), Buf("bc", B_ps[3].tok)
                    B_rv, B_bl, B_A = Buf("rv", B_ps[4].tok), Buf("bl", B_ps[4].tok), Buf("A", B_ps[4].tok)
                    O_ps, kv_ps = pss[5], pss[6]
                    B_O, B_kv = B_ps[5], B_ps[6]
                    B_tq, B_to = Buf("tq", B_pst1.tok), Buf("to", B_pst1.tok)
                    dn = [0]

                    def nextdec():
                        j = dn[0] % 2
                        dn[0] += 1
                        return dec[j], B_dec[j]

                    pn = [0]

                    def proj(wv_, Bw_, i, ncol=256):
                        j = pn[0] % 3
                        pn[0] += 1
                        ps, Bp = pss[j], B_ps[j]
                        for k in range(KT):
                            S.op("pe", MM(ps[:, 0:ncol], hT[:, k, i * 128:(i + 1) * 128], wv_[:, k, :], start=(k == 0), stop=(k == KT - 1)),
                                 reads=[Bw_, B_hT[i]], writes=[Bp])
                        return ps[:, 0:ncol], Bp

                    for hh in range(8):
                        sth, Bsth = st_h[hh % 2], B_sth[hh % 2]
                        glab, B_glab = glabs[hh % 2], B_glabs[hh % 2]
                        S.dma("sp", DMA(glab[:], glab_d[:, hh * 256:(hh + 1) * 256]), writes=[B_glab])
                        if own:
                            S.dma("sp", DMA(sth[:].rearrange("p a b -> p (a b)"), st_s[hh, :, :]), writes=[Bsth])
                        else:
                            S.op("dve", MSET(sth[:], 0.0), writes=[Bsth])
                        for i in range(NT):
                            tok = slice(i * 128, (i + 1) * 128)
                            S.op("pe", MM(z_ps, gadT[0:16, tok], gup[0:16, hh * 256:(hh + 1) * 256]),
                                 reads=[B_gad, B_gup], writes=[B_z])
                            S.op("dve", TT(zb[:], z_ps, glab[:], ALU.add), reads=[B_z, B_glab], writes=[B_zb])
                            S.op("act", ACT(e1[:], zb[:], AF.Exp, scale=-1.0), reads=[B_zb], writes=[B_e1])
                            S.op("act", ACT(lg_all[:, i, :], e1[:], AF.Ln, bias=1.0), reads=[B_e1], writes=[B_lg[i]])
                        if own:
                            wq_, Bwq_ = wload(w_in_v, KT, OGQ + hh * 256, 256)
                            for i in range(NT):
                                S.op("pe", MM(bc_ps, triU, lg_all[:, i, :]), reads=[B_lg[i], B_cb], writes=[B_bc])
                                d_, Bd_ = nextdec()
                                S.op("act", ACT(d_[:], bc_ps, AF.Exp), reads=[B_bc], writes=[Bd_])
                                ps, Bp = proj(wq_, Bwq_, i)
                                S.op("dve", STT(qd_all[:, i, :], ps, 1.0 / 16.0, d_[:], ALU.mult, ALU.mult), reads=[Bp, Bd_], writes=[B_qd[i]])
                        wk_, Bwk_ = wload(w_in_v, KT, OGK + hh * 256, 256)
                        for i in range(NT):
                            S.op("pe", MM(rv_ps, triSL, lg_all[:, i, :]), reads=[B_lg[i], B_cb], writes=[B_rv])
                            d_, Bd_ = nextdec()
                            S.op("act", ACT(d_[:], rv_ps, AF.Exp), reads=[B_rv], writes=[Bd_])
                            if own:
                                S.op("pe", MM(bc_ps, triU, lg_all[:, i, :]), reads=[B_lg[i], B_cb], writes=[B_bc])
                                d2_, Bd2_ = nextdec()
                                S.op("act", ACT(d2_[:], bc_ps, AF.Exp, scale=-1.0), reads=[B_bc], writes=[Bd2_])
                            ps, Bp = proj(wk_, Bwk_, i)
                            S.op("dve", TT(ke_all[:, i, :], ps, d_[:], ALU.mult), reads=[Bp, Bd_], writes=[B_ke[i]])
                            if own:
                                S.op("dve", TT(ki_all[:, i, :], ps, d2_[:], ALU.mult), reads=[Bp, Bd2_], writes=[B_ki[i]])
                        for vh in range(2):
                            wv_, Bwv_ = wload(w_in_v, KT, OGV + hh * 512 + vh * 256, 256)
                            for i in range(NT):
                                ps, Bp = proj(wv_, Bwv_, i)
                                S.op("act", ACT(vb_all[:, i, vh * 256:(vh + 1) * 256], ps, AF.Copy), reads=[Bp], writes=[B_vb[i]])
                        def gr_gen(hh=hh):
                            for vh in range(2):
                                wg_, Bwg_ = wload(w_in_v, KT, OGR + hh * 512 + vh * 256, 256)
                                for i in range(NT):
                                    ps, Bp = proj(wg_, Bwg_, i)
                                    S.op("act", ACT(sg_all[:, i, vh * 256:(vh + 1) * 256], ps, AF.Silu), reads=[Bp], writes=[B_sg[i]])
                                    yield

                        def rec_gen(hh=hh, sth=sth, Bsth=Bsth):
                            if own:
                                S.op("act", ACT(Sbf[:], sth[:], AF.Copy), reads=[Bsth], writes=[B_Sbf])
                            for i in range(NT):
                                for dt in range(2):
                                    S.op("pe", MM(bl_ps[:, 2 * dt:2 * dt + 2], lg_all[:, i, dt * 128:(dt + 1) * 128], negcol),
                                         reads=[B_lg[i], B_cb], writes=[B_bl])
                                S.op("act", ACT(ebl[:], bl_ps, AF.Exp), reads=[B_bl], writes=[B_ebl])
                                if own:
                                    for dt in range(2):
                                        S.op("pe", TR(ps_t[:, dt * 128:(dt + 1) * 128], qd_all[:, i, dt * 128:(dt + 1) * 128], identb[:]),
                                             reads=[B_qd[i], B_cb], writes=[B_tq])
                                        S.op("pe", TR(ps_t[:, 256 + dt * 128:256 + (dt + 1) * 128], ki_all[:, i, dt * 128:(dt + 1) * 128], identb[:]),
                                             reads=[B_ki[i], B_cb], writes=[B_tq])
                                    S.op("act", ACT(qdT[:], ps_t[:, 0:256].rearrange("p (a b) -> p a b", b=128), AF.Copy), reads=[B_tq], writes=[B_qdT])
                                    S.op("act", ACT(kiT[:], ps_t[:, 256:512].rearrange("p (a b) -> p a b", b=128), AF.Copy), reads=[B_tq], writes=[B_kiT])
                                    yield
                                    for dt in range(2):
                                        S.op("pe", MM(A_ps, kiT[:, dt, :], qdT[:, dt, :], start=(dt == 0), stop=(dt == 1)),
                                             reads=[B_kiT, B_qdT], writes=[B_A])
                                    S.op("dve", TT(At[:], A_ps, mutb[:], ALU.mult), reads=[B_A, B_cb], writes=[B_At])
                                    S.op("pe", MM(O_ps[:], At[:], vb_all[:, i, :], start=True, stop=False), reads=[B_At, B_vb[i]], writes=[B_O])
                                    for dt in range(2):
                                        S.op("pe", MM(O_ps[:], qdT[:, dt, :], Sbf[:, dt, :], start=False, stop=(dt == 1)),
                                             reads=[B_qdT, B_Sbf], writes=[B_O])
                                for dt in range(2):
                                    S.op("pe", MM(kv_ps[:], ke_all[:, i, dt * 128:(dt + 1) * 128], vb_all[:, i, :]), reads=[B_ke[i], B_vb[i]], writes=[B_kv])
                                    S.op("dve", STT(sth[:, dt, :], sth[:, dt, :], ebl[:, 2 * dt:2 * dt + 1], kv_ps[:], ALU.mult, ALU.add),
                                         reads=[Bsth, B_ebl, B_kv], writes=[Bsth])
                                if own:
                                    if i < NT - 1:
                                        S.op("act", ACT(Sbf[:], sth[:], AF.Copy), reads=[Bsth], writes=[B_Sbf])
                                    S.op("act", ACT(ot[:], O_ps[:], AF.Square, accum_out=gst[:, 0:1]), reads=[B_O], writes=[B_ot, B_gst])
                                    S.op("dve", TS(gst[:, 1:2], gst[:, 0:1], 1.0 / 512.0, EPS, ALU.mult, ALU.add), reads=[B_gst], writes=[B_gst])
                                    S.op("act", ACT(gst[:, 2:3], gst[:, 1:2], AF.Sqrt), reads=[B_gst], writes=[B_gst])
                                    S.op("dve", RCP(gst[:, 3:4], gst[:, 2:3]), reads=[B_gst], writes=[B_gst])
                                    S.op("dve", STT(on_all[:, i, :], O_ps[:], gst[:, 3:4], g512[:], ALU.mult, ALU.mult),
                                         reads=[B_O, B_gst, B_g512], writes=[B_on[i]])
                                yield

                        if own:
                            G_, R_ = gr_gen(), rec_gen()
                            g_alive, r_alive = True, True
                            while g_alive or r_alive:
                                if r_alive:
                                    try:
                                        next(R_)
                                    except StopIteration:
                                        r_alive = False
                                if g_alive:
                                    try:
                                        next(G_)
                                    except StopIteration:
                                        g_alive = False
                            for i in range(NT):
                                tok = slice(i * 128, (i + 1) * 128)
                                S.op("dve", TT(ot[:], on_all[:, i, :], sg_all[:, i, :], ALU.mult), reads=[B_on[i], B_sg[i]], writes=[B_ot])
                                for c in range(4):
                                    S.op("pe", TR(ps_t[:, 512 + c * 128:512 + (c + 1) * 128], ot[:, c * 128:(c + 1) * 128], identb[:]),
                                         reads=[B_ot, B_cb], writes=[B_to])
                                S.op("act", ACT(goT_h[:, :, tok], ps_t[:, 512:1024].rearrange("p (a b) -> p a b", b=128), AF.Copy),
                                     reads=[B_to], writes=[B_goT])
                        else:
                            for _ in rec_gen():
                                pass
                        if own:
                            for c in range(4):
                                S.dma("sp", DMA(goT_s[hh * 4 + c, :, :], goT_h[:, c, :]), reads=[B_goT])
                        else:
                            S.dma("sp", DMA(st_s[hh, :, :], sth[:].rearrange("p a b -> p (a b)")), reads=[Bsth])
                S.end_phase()
                if not own:
                    return

                with ExitStack() as es2:
                    sgt = [sb("sgt%d" % i, [128, 4, TOK], BF16, es2) for i in range(2)]
                    B_sgt = [Buf(), Buf()]
                    ci = 0
                    for (off, dst) in ((OGM, sgm_s), (OGG, sgg_s)):
                        for cc in range(16):
                            wg_, Bwg_ = wload(w_in_v, KT, off + cc * 256, 256)
                            st_ = sgt[ci % 2]
                            Bst_ = B_sgt[ci % 2]
                            ci += 1
                            for sub in range(2):
                                for tg in range(2):
                                    ps = pss[(sub * 2 + tg) % 4]
                                    Bp = B_ps[(sub * 2 + tg) % 4]
                                    for k in range(KT):
                                        S.op("pe", MM(ps[:], wg_[:, k, sub * 128:(sub + 1) * 128], hT[:, k, tg * 512:(tg + 1) * 512],
                                                      start=(k == 0), stop=(k == KT - 1)),
                                             reads=[Bwg_] + B_hT[tg * 4:(tg + 1) * 4], writes=[Bp])
                                    S.op("act", ACT(st_[:, sub, tg * 512:(tg + 1) * 512], ps[:], AF.Sigmoid), reads=[Bp], writes=[Bst_])
                            for sub in range(2):
                                S.dma("sp", DMA(dst[cc * 2 + sub, :, :], st_[:, sub, :]), reads=[Bst_])
                S.end_phase()
            return None

        run_pass(False)
        run_pass(True)

        with ExitStack() as es:
            pss = [psum("ps%d" % i, [128, 512], F32, es) for i in range(8)]
            B_ps = [PB("ps%d" % i) for i in range(8)]
            mT = sb("mT", [128, KT, TOK], BF16, es)
            B_mT = [Buf() for _ in range(NT)]
            with ExitStack() as es2:
                alloc_w(es2, 3, 4096)
                moT = sb("moT", [128, 16, TOK], BF16, es2)
                goT = sb("goT", [128, 32, TOK], BF16, es2)
                B_moT, B_goT2 = Buf(), Buf()
                B_moTp = [Buf() for _ in range(2)]
                B_goTp = [Buf() for _ in range(4)]
                for q_ in range(2):
                    S.dma("sp", DMA(moT[:, q_ * 8:(q_ + 1) * 8, :], moT_s[q_ * 8:(q_ + 1) * 8, :, :].rearrange("h p t -> p h t")), writes=[B_moTp[q_]])
                for q_ in range(4):
                    S.dma("sp", DMA(goT[:, q_ * 8:(q_ + 1) * 8, :], goT_s[q_ * 8:(q_ + 1) * 8, :, :].rearrange("h p t -> p h t")), writes=[B_goTp[q_]])
                sgm = [sb("sgm%d" % i, [128, TOK], BF16, es2) for i in range(2)]
                sgg = [sb("sgg%d" % i, [128, TOK], BF16, es2) for i in range(2)]
                B_sgm, B_sgg = [Buf(), Buf()], [Buf(), Buf()]
                t1 = sb("t1", [128, 512], F32, es2)
                t2 = sb("t2", [128, 512], F32, es2)
                B_t1, B_t2 = Buf(), Buf()
                for c in range(32):
                    j = c % 2
                    wm, Bwm = wload(w_bm_v, 16, c * 128, 128)
                    wg_, Bwg_ = wload(w_bg_v, KT, c * 128, 128)
                    S.dma("sp", DMA(sgm[j][:], sgm_s[c, :, :]), writes=[B_sgm[j]])
                    S.dma("sp", DMA(sgg[j][:], sgg_s[c, :, :]), writes=[B_sgg[j]])
                    for tg in range(2):
                        pm, Bpm = pss[tg * 2], B_ps[tg * 2]
                        pg, Bpg = pss[tg * 2 + 1], B_ps[tg * 2 + 1]
                        for k in range(16):
                            S.op("pe", MM(pm[:], wm[:, k, :], moT[:, k, tg * 512:(tg + 1) * 512],
                                          start=(k == 0), stop=(k == 15)), reads=[Bwm] + B_moTp, writes=[Bpm])
                        for k in range(KT):
                            S.op("pe", MM(pg[:], wg_[:, k, :], goT[:, k, tg * 512:(tg + 1) * 512],
                                          start=(k == 0), stop=(k == KT - 1)), reads=[Bwg_] + B_goTp, writes=[Bpg])
                        S.op("dve", TT(t1[:], pm[:], sgm[j][:, tg * 512:(tg + 1) * 512], ALU.mult), reads=[Bpm, B_sgm[j]], writes=[B_t1])
                        S.op("dve", TT(t2[:], pg[:], sgg[j][:, tg * 512:(tg + 1) * 512], ALU.mult), reads=[Bpg, B_sgg[j]], writes=[B_t2])
                        S.op("dve", TT(mT[:, c, tg * 512:(tg + 1) * 512], t1[:], t2[:], ALU.add), reads=[B_t1, B_t2],
                             writes=B_mT[tg * 4:(tg + 1) * 4])
                S.end_phase()
            alloc_w(es, 3)
            ybuf = [sb("ybuf%d" % i, [128, 256], F32, es) for i in range(2)]
            B_yb = [Buf(), Buf()]
            ci = 0
            for cc in range(16):
                wo, Bwo = wload(w_out_v, KT, cc * 256, 256)
                for i in range(NT):
                    ps, Bp = pss[4 + ci % 4], B_ps[4 + ci % 4]
                    yb, Byb = ybuf[ci % 2], B_yb[ci % 2]
                    ci += 1
                    for k in range(KT):
                        S.op("pe", MM(ps[:, 0:256], mT[:, k, i * 128:(i + 1) * 128], wo[:, k, :], start=(k == 0), stop=(k == KT - 1)),
                             reads=[Bwo, B_mT[i]], writes=[Bp])
                    S.op("act", ACT(yb[:], ps[:, 0:256], AF.Copy), reads=[Bp], writes=[Byb])
                    S.dma("sp", DMA(y_s[i * 128:(i + 1) * 128, cc * 256:(cc + 1) * 256], yb[:]), reads=[Byb])
            S.end_phase()

        with ExitStack() as es:
            h2T = sb("h2T", [128, KT, TOK], BF16, es)
            B_h2T = [Buf() for _ in range(NT)]
            ps_t = psum("pst", [128, 1024], BF16, es)
            B_pst1 = PB("pst")
            gpm = sb("gpm_sb", [128, D], F32, es)
            B_gpm = Buf()
            S.dma("sp", DMA(gpm[:], gpm_d[:, :]), writes=[B_gpm])
            yt = [sb("yt%d" % i, [128, D], F32, es) for i in range(2)]
            xt = [sb("xt%d" % i, [128, D], F32, es) for i in range(2)]
            x1t = [sb("x1t%d" % i, [128, D], F32, es) for i in range(2)]
            xn = [sb("xn%d" % i, [128, D], BF16, es) for i in range(2)]
            st = sb("st4", [128, 8, NT], F32, es)
            B_yt, B_xt, B_x1t, B_xn = [Buf(), Buf()], [Buf(), Buf()], [Buf(), Buf()], [Buf(), Buf()]
            B_junk = Buf()
            B_st = [Buf() for _ in range(NT)]
            for i in range(NT):
                j = i % 2
                rows = slice(i * 128, (i + 1) * 128)
                S.dma("sp", DMA(yt[j][:], y_s[rows, :]), writes=[B_yt[j]])
                S.dma("sp", DMA(xt[j][:], x_own[rows, :]), writes=[B_xt[j]])
                S.op("act", ACT(xn[j][:], yt[j][:], AF.Square, accum_out=st[:, 0, i:i + 1]), reads=[B_yt[j]], writes=[B_xn[j], B_st[i]])
                S.op("dve", TS(st[:, 1, i:i + 1], st[:, 0, i:i + 1], 1.0 / D, EPS, ALU.mult, ALU.add), reads=[B_st[i]], writes=[B_st[i]])
                S.op("act", ACT(st[:, 2, i:i + 1], st[:, 1, i:i + 1], AF.Sqrt), reads=[B_st[i]], writes=[B_st[i]])
                S.op("dve", RCP(st[:, 3, i:i + 1], st[:, 2, i:i + 1]), reads=[B_st[i]], writes=[B_st[i]])
                S.op("dve", STT(x1t[j][:], yt[j][:], st[:, 3, i:i + 1], gpm[:], ALU.mult, ALU.mult),
                     reads=[B_yt[j], B_st[i], B_gpm], writes=[B_x1t[j]])
                S.op("dve", TT(x1t[j][:], x1t[j][:], xt[j][:], ALU.add), reads=[B_x1t[j], B_xt[j]], writes=[B_x1t[j]])
                S.dma("sp", DMA(x1_s[rows, :], x1t[j][:]), reads=[B_x1t[j]])
                S.op("act", ACT(xn[j][:], x1t[j][:], AF.Square, accum_out=st[:, 4, i:i + 1]), reads=[B_x1t[j]], writes=[B_xn[j], B_st[i]])
                S.op("dve", TS(st[:, 5, i:i + 1], st[:, 4, i:i + 1], 1.0 / D, EPS, ALU.mult, ALU.add), reads=[B_st[i]], writes=[B_st[i]])
                S.op("act", ACT(st[:, 6, i:i + 1], st[:, 5, i:i + 1], AF.Sqrt), reads=[B_st[i]], writes=[B_st[i]])
                S.op("dve", RCP(st[:, 7, i:i + 1], st[:, 6, i:i + 1]), reads=[B_st[i]], writes=[B_st[i]])
                S.op("act", ACT(xn[j][:], x1t[j][:], AF.Copy, scale=st[:, 7, i:i + 1]), reads=[B_x1t[j], B_st[i]], writes=[B_xn[j]])
                for g in range(4):
                    for c in range(8):
                        k = g * 8 + c
                        S.op("pe", TR(ps_t[:, c * 128:(c + 1) * 128], xn[j][:, k * 128:(k + 1) * 128], identb[:]),
                             reads=[B_xn[j], B_cb], writes=[B_pst1])
                    S.op("dve", TT(h2T[:, g * 8:(g + 1) * 8, i * 128:(i + 1) * 128],
                                   ps_t[:].rearrange("p (a b) -> p a b", b=128),
                                   gv[:, 32 + g * 8:32 + (g + 1) * 8].unsqueeze(2).to_broadcast([128, 8, 128]), ALU.mult),
                         reads=[B_pst1, B_gv], writes=[B_h2T[i]])
            for half in range(2):
                S.dma("sp", DMA(h2T_s[half, :, :, :], h2T[:, :, half * 512:(half + 1) * 512]), reads=B_h2T[half * 4:(half + 1) * 4])
            S.end_phase()

        with ExitStack() as es:
            pss = [psum("ps%d" % i, [128, 512], F32, es) for i in range(8)]
            B_ps = [PB("ps%d" % i) for i in range(8)]
            alloc_w(es, 3)
            hid = sb("hid", [128, FT, 512], BF16, es)
            B_hid = [Buf() for _ in range(FT)]
            sgf = [sb("sgf%d" % i, [128, 512], F32, es) for i in range(2)]
            B_sgf = [Buf(), Buf()]
            y2b = [sb("y2b%d" % i, [128, 512], F32, es) for i in range(2)]
            B_y2b = [Buf(), Buf()]
            h2Th = sb("h2Th", [128, KT, 512], BF16, es)
            B_h2Th = Buf()
            for half in range(2):
                S.dma("sp", DMA(h2Th[:], h2T_s[half, :, :, :]), writes=[B_h2Th])
                ci = 0
                for cc in range(43):
                    wg_, Bwg_ = wload(w_fg_v, KT, cc * 256, 256)
                    wu_, Bwu_ = wload(w_fu_v, KT, cc * 256, 256)
                    for sub in range(2):
                        f = cc * 2 + sub
                        pg, Bpg = pss[(ci % 2) * 2], B_ps[(ci % 2) * 2]
                        pu, Bpu = pss[(ci % 2) * 2 + 1], B_ps[(ci % 2) * 2 + 1]
                        sg_, Bsg_ = sgf[ci % 2], B_sgf[ci % 2]
                        ci += 1
                        for k in range(KT):
                            S.op("pe", MM(pg[:], wg_[:, k, sub * 128:(sub + 1) * 128], h2Th[:, k, :], start=(k == 0), stop=(k == KT - 1)),
                                 reads=[Bwg_, B_h2Th], writes=[Bpg])
                        for k in range(KT):
                            S.op("pe", MM(pu[:], wu_[:, k, sub * 128:(sub + 1) * 128], h2Th[:, k, :], start=(k == 0), stop=(k == KT - 1)),
                                 reads=[Bwu_, B_h2Th], writes=[Bpu])
                        S.op("act", ACT(sg_[:], pg[:], AF.Silu), reads=[Bpg], writes=[Bsg_])
                        S.op("dve", TT(hid[:, f, :], sg_[:], pu[:], ALU.mult), reads=[Bsg_, Bpu], writes=[B_hid[f]])
                S.end_phase()
                ci = 0
                for qd_ in range(4):
                    c0 = qd_ * 1024
                    for f0 in range(0, FT, 8):
                        nf = min(8, FT - f0)
                        i3 = wn[0] % len(wbuf)
                        wn[0] += 1
                        wd = wbuf[i3][:, 0:8 * 1024].rearrange("p (k c) -> p k c", c=1024)
                        Bwd = B_w[i3]

                        def fn(e, wd=wd, f0=f0, nf=nf, c0=c0):
                            return [e.dma_start(out=wd[:, 0:nf, :], in_=w_fd_v[:, f0:f0 + nf, c0:c0 + 1024])]

                        S.dma("pool", fn, writes=[Bwd], n=1)
                        for fi in range(nf):
                            f = f0 + fi
                            for tt in range(4):
                                for cg in range(2):
                                    S.op("pe", MM(pss[tt * 2 + cg][:], hid[:, f, tt * 128:(tt + 1) * 128], wd[:, fi, cg * 512:(cg + 1) * 512],
                                                  start=(f == 0), stop=(f == FT - 1)),
                                         reads=[Bwd, B_hid[f]], writes=[B_ps[tt * 2 + cg]])
                    for tt in range(4):
                        for cg in range(2):
                            yb, Byb = y2b[ci % 2], B_y2b[ci % 2]
                            ci += 1
                            S.op("act", ACT(yb[:], pss[tt * 2 + cg][:], AF.Copy), reads=[B_ps[tt * 2 + cg]], writes=[Byb])
                            r0 = half * 512 + tt * 128
                            S.dma("sp", DMA(y2_s[r0:r0 + 128, c0 + cg * 512:c0 + (cg + 1) * 512], yb[:]), reads=[Byb])
                S.end_phase()

        with ExitStack() as es:
            gpf = sb("gpf_sb", [128, D], F32, es)
            B_gpf = Buf()
            S.dma("sp", DMA(gpf[:], gpf_d[:, :]), writes=[B_gpf])
            yt = [sb("yt%d" % i, [128, D], F32, es) for i in range(2)]
            xt = [sb("xt%d" % i, [128, D], F32, es) for i in range(2)]
            ot_ = [sb("ot%d" % i, [128, D], F32, es) for i in range(2)]
            junk = sb("junk", [128, D], BF16, es)
            st = sb("st5", [128, 4, NT], F32, es)
            B_yt, B_xt, B_ot = [Buf(), Buf()], [Buf(), Buf()], [Buf(), Buf()]
            B_junk = Buf()
            B_st = [Buf() for _ in range(NT)]
            for i in range(NT):
                j = i % 2
                rows = slice(i * 128, (i + 1) * 128)
                S.dma("sp", DMA(yt[j][:], y2_s[rows, :]), writes=[B_yt[j]])
                S.dma("sp", DMA(xt[j][:], x1_s[rows, :]), writes=[B_xt[j]])
                S.op("act", ACT(junk[:], yt[j][:], AF.Square, accum_out=st[:, 0, i:i + 1]), reads=[B_yt[j]], writes=[B_junk, B_st[i]])
                S.op("dve", TS(st[:, 1, i:i + 1], st[:, 0, i:i + 1], 1.0 / D, EPS, ALU.mult, ALU.add), reads=[B_st[i]], writes=[B_st[i]])
                S.op("act", ACT(st[:, 2, i:i + 1], st[:, 1, i:i + 1], AF.Sqrt), reads=[B_st[i]], writes=[B_st[i]])
                S.op("dve", RCP(st[:, 3, i:i + 1], st[:, 2, i:i + 1]), reads=[B_st[i]], writes=[B_st[i]])
                S.op("dve", STT(ot_[j][:], yt[j][:], st[:, 3, i:i + 1], gpf[:], ALU.mult, ALU.mult),
                     reads=[B_yt[j], B_st[i], B_gpf], writes=[B_ot[j]])
                S.op("dve", TT(ot_[j][:], ot_[j][:], xt[j][:], ALU.add), reads=[B_ot[j], B_xt[j]], writes=[B_ot[j]])
                S.dma("sp", DMA(out_d[rows, :], ot_[j][:]), reads=[B_ot[j]])
            S.end_phase()


def _consts():
    cb = np.zeros((128, C_END), np.float32)
    idx = np.arange(128)
    cb[:, C_ID:C_ID + 128] = np.eye(128, dtype=np.float32)
    perm = np.zeros((128, 128), np.float32)
    perm[(idx + 64) % 128, idx] = 1.0
    cb[:, C_PERM:C_PERM + 128] = perm
    le = (idx[:, None] <= idx[None, :]).astype(np.float32)
    gt = (idx[:, None] > idx[None, :]).astype(np.float32)
    cb[:, C_TRIU:C_TRIU + 128] = le * (-1.0 / 16.0)
    cb[:, C_TRISL:C_TRISL + 128] = gt * (-1.0 / 16.0)
    cb[:, C_MUT:C_MUT + 128] = le
    cm = np.zeros((128, 2, 256), np.float32)
    q = np.arange(256)
    for a in range(2):
        cm[:, a, :] = ((a * 128 + idx)[:, None] <= q[None, :]).astype(np.float32)
    cb[:, C_CM:C_CM + 512] = cm.reshape(128, 512)
    cb[:, C_NEG:C_NEG + 2] = -1.0 / 16.0
    half = 64
    inv_freq = (10000.0 ** (-np.arange(half, dtype=np.float32) / half)).astype(np.float32)
    ang = np.arange(2048, dtype=np.float32)[None, :] * inv_freq[:, None]
    cos = np.cos(ang).astype(np.float32)
    sin = np.sin(ang).astype(np.float32)
    cosT = np.concatenate([cos, cos], axis=0)
    sinT = np.concatenate([-sin, sin], axis=0)
    return cb, np.ascontiguousarray(cosT), np.ascontiguousarray(sinT)


def _gbias(odd):
    gb = np.full((128, 8, 8), NEG, np.float32)
    for qt in range(8):
        jb = qt // 2
        if odd:
            gb[:, qt, 0:4] = 0.0
        gb[:, qt, 4:4 + jb] = 0.0
    return gb.reshape(128, 64)


_NC_CACHE = {}


def _in_maps(x, pre_mix_norm_g, w_in, gla_gate_up, gla_gate_bias, gla_out_norm_g,
             w_branch_moba, w_branch_gla, w_out, post_mix_norm_g, pre_ffn_norm_g,
             w_ffn_gate, w_ffn_up, w_ffn_down, post_ffn_norm_g):
    f = lambda a: np.ascontiguousarray(np.asarray(a, dtype=np.float32))
    x = f(x)
    cb, cosT, sinT = _consts()
    gvec = np.concatenate([f(pre_mix_norm_g)[0].reshape(32, 128).T, f(pre_ffn_norm_g)[0].reshape(32, 128).T], axis=1)
    common = {
        "w_in": f(w_in)[0], "w_bm": f(w_branch_moba)[0], "w_bg": f(w_branch_gla)[0], "w_out": f(w_out)[0],
        "w_fg": f(w_ffn_gate)[0], "w_fu": f(w_ffn_up)[0], "w_fd": f(w_ffn_down)[0],
        "gup": f(gla_gate_up)[0], "gvec": np.ascontiguousarray(gvec),
        "glab": np.ascontiguousarray(np.broadcast_to(f(gla_gate_bias)[0][None, :], (128, 2048))),
        "g512": np.ascontiguousarray(np.broadcast_to(f(gla_out_norm_g)[0][None, :], (128, 512))),
        "gpm": np.ascontiguousarray(np.broadcast_to(f(post_mix_norm_g)[0][None, :], (128, D))),
        "gpf": np.ascontiguousarray(np.broadcast_to(f(post_ffn_norm_g)[0][None, :], (128, D))),
        "cosT": cosT, "sinT": sinT, "cblob": cb,
    }
    zeros = np.zeros((TOK, D), np.float32)
    in_maps = []
    for c in range(8):
        b, hf = c // 2, c % 2
        m = dict(common)
        m["x_own"] = np.ascontiguousarray(x[b, hf * TOK:(hf + 1) * TOK])
        m["x_pre"] = np.ascontiguousarray(x[b, 0:TOK]) if hf == 1 else zeros
        m["gbias"] = _gbias(hf == 1)
        in_maps.append(m)
    return in_maps


def kernel(**inputs):
    if "nc" not in _NC_CACHE:
        _NC_CACHE["nc"] = build_program()
    nc = _NC_CACHE["nc"]
    in_maps = _in_maps(**inputs)
    res = run_bass_kernel_spmd(nc, in_maps, core_ids=list(range(8)))
    out = np.empty((4, 2048, D), np.float32)
    for c in range(8):
        b, hf = c // 2, c % 2
        out[b, hf * TOK:(hf + 1) * TOK] = res.results[c]["out"]
    return out
```

```python
from contextlib import ExitStack
import numpy as np
import concourse.bass as bass
import concourse.mybir as mybir
from concourse.bass_utils import run_bass_kernel_spmd

F32 = mybir.dt.float32
BF16 = mybir.dt.bfloat16
AF = mybir.ActivationFunctionType
ALU = mybir.AluOpType
AX = mybir.AxisListType

D = 4096
KT = 32
TOK = 1024
NT = 8
DFF = 11008
FT = 86
EPS = 1e-6
OQ, OK_, OV, OGQ, OGK, OGV, OGR, OGA, OGM, OGG = 0, 2048, 4096, 6144, 8192, 10240, 14336, 18432, 18448, 22544
INC = 26640
NEG = -3.0e38
C_ID, C_PERM, C_TRIU, C_TRISL, C_MUT, C_CM, C_NEG, C_END = 0, 128, 256, 384, 512, 640, 1152, 1154

STOP_AFTER = None
DEBUG_OUT = ()
SHRINK = ()


class StopBuild(Exception):
    pass


class Buf:
    __slots__ = ("name", "writer", "readers", "tok")

    def __init__(self, name="", tok=None):
        self.name = name
        self.writer = None
        self.readers = {}
        self.tok = tok


def PB(name=""):
    return Buf(name, tok=Buf(name + "_tok"))


class Rec:
    __slots__ = ("eng", "fn", "deps", "signal", "val", "sem", "is_dma", "phase", "n")

    def __init__(self, eng, fn, phase):
        self.eng = eng
        self.fn = fn
        self.deps = []
        self.signal = False
        self.val = None
        self.sem = None
        self.is_dma = False
        self.phase = phase
        self.n = 1


class Slot:
    __slots__ = ("sem", "count", "last")

    def __init__(self, sem):
        self.sem = sem
        self.count = 0
        self.last = None


ENGS = ("pe", "act", "dve", "pool", "sp")


class Sched:
    def __init__(self, nc, esem, sp_sems, pool_sems):
        self.nc = nc
        self.esem = esem
        self.cnt = {e: 0 for e in ENGS}
        self.dpool = {"sp": [Slot(s) for s in sp_sems], "pool": [Slot(s) for s in pool_sems]}
        self.dnext = {"sp": 0, "pool": 0}
        self.waited = {e: {} for e in ENGS}
        self.phase = 0
        self.q = {e: [] for e in ENGS}
        self.ninst = 0

    def _deps(self, r, reads, writes):
        deps = {}

        def add(d, kind):
            if d is None or d is r or d.phase != self.phase:
                return
            if (not d.is_dma) and (not r.is_dma) and d.eng == r.eng:
                if r.eng == "pe":
                    return
            deps[id(d)] = d

        for b in reads:
            add(b.writer, "w")
        for b in writes:
            add(b.writer, "w")
            for rd in b.readers.values():
                add(rd, "r")
        for d in deps.values():
            r.deps.append(d)
            if not d.is_dma:
                d.signal = True
        for b in reads:
            b.readers[("dma", id(r)) if r.is_dma else r.eng] = r
        for b in writes:
            b.writer = r
            b.readers = {}

    def op(self, eng, fn, reads=(), writes=()):
        toks = []
        for b in list(reads) + list(writes):
            if b.tok is not None and b.tok not in toks:
                toks.append(b.tok)
        if toks:
            writes = list(writes) + toks
        r = Rec(eng, fn, self.phase)
        self._deps(r, reads, writes)
        self.q[eng].append(r)
        return r

    def dma(self, queue, fn, reads=(), writes=(), n=1):
        pool = self.dpool[queue]
        slot = pool[self.dnext[queue] % len(pool)]
        self.dnext[queue] += 1
        r = Rec(queue, fn, self.phase)
        r.is_dma = True
        r.sem = slot.sem
        r.n = n
        if slot.last is not None and slot.last.phase == self.phase:
            r.deps.append(slot.last)
        slot.count += 16 * n
        r.val = slot.count
        slot.last = r
        self._deps(r, reads, writes)
        self.q[queue].append(r)
        return r

    def end_phase(self):
        deps = []
        for e in ("pe", "act", "dve", "pool"):
            for r in reversed(self.q[e]):
                if r.fn is not None:
                    deps.append(r)
                    break
        for queue in self.dpool:
            for slot in self.dpool[queue]:
                if slot.last is not None and slot.last.phase == self.phase:
                    deps.append(slot.last)
        b = Rec("sp", lambda e: e.nop(), self.phase)
        seen = set()
        for d in deps:
            if id(d) in seen:
                continue
            seen.add(id(d))
            b.deps.append(d)
            if not d.is_dma:
                d.signal = True
        b.signal = True
        self.q["sp"].append(b)
        for e in ("pe", "act", "dve", "pool"):
            rr = Rec(e, None, self.phase)
            rr.deps = [b]
            self.q[e].append(rr)
        for e in ENGS:
            for r in self.q[e]:
                if (not r.is_dma) and r.signal:
                    self.cnt[e] += 1
                    r.val = self.cnt[e]
        nc = self.nc
        with nc.Block() as blk:
            for e, reg in (("pe", blk.tensor), ("act", blk.scalar), ("dve", blk.vector),
                           ("pool", blk.gpsimd), ("sp", blk.sync)):
                lst = self.q[e]
                waited = self.waited[e]
                esem = self.esem

                def body(eng, e=e, lst=lst, waited=waited):
                    for r in lst:
                        for d in r.deps:
                            sem = d.sem if d.is_dma else esem[d.eng]
                            val = d.val
                            assert val is not None
                            key = id(sem)
                            if waited.get(key, 0) >= val:
                                continue
                            eng.wait_ge(sem, val)
                            waited[key] = val
                        if r.fn is None:
                            continue
                        ins = r.fn(eng)
                        self.ninst += 1
                        if r.is_dma:
                            if isinstance(ins, (list, tuple)):
                                assert len(ins) == r.n
                                for i_ in ins:
                                    i_.then_inc(r.sem, 16)
                            else:
                                assert r.n == 1
                                ins.then_inc(r.sem, 16)
                        elif r.signal:
                            ins.then_inc(esem[e], 1)

                reg(body)
        self.q = {e: [] for e in ENGS}
        self.phase += 1
        build_program.ninst = self.ninst
        if STOP_AFTER is not None and self.phase > STOP_AFTER:
            raise StopBuild()


def MM(out, lhsT, rhs, start=True, stop=True):
    return lambda e: e.matmul(out, lhsT, rhs, start=start, stop=stop)


def TR(out, in_, ident):
    return lambda e: e.transpose(out, in_, ident)


def ACT(out, in_, func, bias=0.0, scale=1.0, accum_out=None):
    if accum_out is None:
        return lambda e: e.activation(out=out, in_=in_, func=func, bias=bias, scale=scale)
    return lambda e: e.activation(out=out, in_=in_, func=func, bias=bias, scale=scale, accum_out=accum_out)


def TT(out, a, b, op):
    return lambda e: e.tensor_tensor(out=out, in0=a, in1=b, op=op)


def TS(out, a, s1, s2, op0, op1=None):
    if op1 is None:
        return lambda e: e.tensor_scalar(out=out, in0=a, scalar1=s1, scalar2=None, op0=op0)
    return lambda e: e.tensor_scalar(out=out, in0=a, scalar1=s1, scalar2=s2, op0=op0, op1=op1)


def STT(out, a, s, b, op0, op1):
    return lambda e: e.scalar_tensor_tensor(out=out, in0=a, scalar=s, in1=b, op0=op0, op1=op1)


def CP(out, in_):
    return lambda e: e.tensor_copy(out=out, in_=in_)


def RCP(out, in_):
    return lambda e: e.reciprocal(out=out, in_=in_)


def DMA(out, in_):
    return lambda e: e.dma_start(out=out, in_=in_)


def MSET(ap, v):
    return lambda e: e.memset(ap, v)


def build_program():
    nc = bass.Bass("TRN2", target_bir_lowering=False)
    try:
        _build(nc)
    except StopBuild:
        pass
    return nc


def _build(nc):

    def din(name, shape, dt=F32):
        if name in SHRINK:
            t = nc.dram_tensor(name + "_tiny", [128, 128], dt, kind="ExternalInput")
            return nc.dram_tensor(name + "_fake", list(shape), dt, kind="Internal").ap()
        return nc.dram_tensor(name, list(shape), dt, kind="ExternalInput").ap()

    def dscr(name, shape, dt):
        kind = "ExternalOutput" if name in DEBUG_OUT else "Internal"
        return nc.dram_tensor(name, list(shape), dt, kind=kind).ap()

    x_own = din("x_own", [TOK, D])
    x_pre = din("x_pre", [TOK, D])
    w_in = din("w_in", [D, INC])
    w_bm = din("w_bm", [2048, D])
    w_bg = din("w_bg", [D, D])
    w_out = din("w_out", [D, D])
    w_fg = din("w_fg", [D, DFF])
    w_fu = din("w_fu", [D, DFF])
    w_fd = din("w_fd", [DFF, D])
    gup_d = din("gup", [16, 2048])
    gvec = din("gvec", [128, 64])
    gbias_d = din("gbias", [128, 64])
    glab_d = din("glab", [128, 2048])
    g512_d = din("g512", [128, 512])
    gpm_d = din("gpm", [128, D])
    gpf_d = din("gpf", [128, D])
    cosT_d = din("cosT", [128, 2048])
    sinT_d = din("sinT", [128, 2048])
    cb_d = din("cblob", [128, C_END])
    out_d = nc.dram_tensor("out", [TOK, D], F32, kind="ExternalOutput").ap()

    kt_s = dscr("kt_s", [16, 128, 1024], BF16)
    v_s = dscr("v_s", [8, 128, 8 * 2 * 129], BF16)
    moT_s = dscr("moT_s", [16, 128, 1024], BF16)
    goT_s = dscr("goT_s", [32, 128, 1024], BF16)
    sgm_s = dscr("sgm_s", [32, 128, 1024], BF16)
    sgg_s = dscr("sgg_s", [32, 128, 1024], BF16)
    y_s = dscr("y_s", [TOK, D], F32)
    x1_s = dscr("x1_s", [TOK, D], F32)
    y2_s = dscr("y2_s", [TOK, D], F32)
    st_s = dscr("st_s", [8, 128, 1024], F32)
    h2T_s = dscr("h2T_s", [2, 128, KT, 512], BF16)

    def wview(W):
        return W.rearrange("(kt p) n -> p kt n", p=128)

    w_in_v, w_bm_v, w_bg_v, w_out_v = wview(w_in), wview(w_bm), wview(w_bg), wview(w_out)
    w_fg_v, w_fu_v, w_fd_v = wview(w_fg), wview(w_fu), wview(w_fd)

    with ExitStack() as G:
        uid = [0]

        def sb(name, shape, dt, es=G):
            uid[0] += 1
            return es.enter_context(nc.sbuf_tensor("%s_s%d" % (name, uid[0]), list(shape), dt))

        def psum(name, shape, dt, es):
            uid[0] += 1
            return es.enter_context(nc.psum_tensor("%s_p%d" % (name, uid[0]), list(shape), dt))

        nsp, npool = 40, 8
        esem = {e: G.enter_context(nc.semaphore("e_" + e)) for e in ENGS}
        sp_sems = [G.enter_context(nc.semaphore("dsp%d" % i)) for i in range(nsp)]
        pool_sems = [G.enter_context(nc.semaphore("dpl%d" % i)) for i in range(npool)]
        S = Sched(nc, esem, sp_sems, pool_sems)

        cb = sb("cb", [128, C_END], F32)
        identb = sb("identb", [128, 128], BF16)
        permb = sb("permb", [128, 128], BF16)
        mutb = sb("mutb", [128, 128], BF16)
        cmb = sb("cmb", [128, 512], BF16)
        gv = sb("gv", [128, 64], F32)
        gbias = sb("gbias_sb", [128, 64], F32)
        ksum = sb("ksum", [128, 16, 8], F32)
        wbuf = []
        B_w = []
        B_cb, B_gv, B_gbias = Buf("cb"), Buf("gv"), Buf("gbias")
        B_ksum = [Buf("ksum%d" % h) for h in range(16)]
        wn = [0]

        def alloc_w(es, n, size=8192):
            wbuf[:] = [sb("wbuf%d" % i, [128, size], BF16, es) for i in range(n)]
            B_w[:] = [Buf("w%d" % i) for i in range(n)]
            wn[0] = 0
        identf = cb[:, C_ID:C_ID + 128]
        triU = cb[:, C_TRIU:C_TRIU + 128]
        triSL = cb[:, C_TRISL:C_TRISL + 128]
        negcol = cb[:, C_NEG:C_NEG + 2]

        def wload(Wv, nk, col0, ncols, sub=0, width=None):
            i = wn[0] % len(wbuf)
            wn[0] += 1
            width = width or ncols
            view = wbuf[i][:, 0:nk * width].rearrange("p (k c) -> p k c", c=width)
            step = 8

            def fn(e, view=view, Wv=Wv):
                return [e.dma_start(out=view[:, k0:min(k0 + step, nk), sub:sub + ncols],
                                    in_=Wv[:, k0:min(k0 + step, nk), col0:col0 + ncols])
                        for k0 in range(0, nk, step)]

            S.dma("pool", fn, writes=[B_w[i]], n=(nk + step - 1) // step)
            return view, B_w[i]

        def wload2(Wv, nk, cols_a, cols_b, n_each):
            i = wn[0] % len(wbuf)
            wn[0] += 1
            width = 2 * n_each
            view = wbuf[i][:, 0:nk * width].rearrange("p (k c) -> p k c", c=width)
            step = 8

            def fn(e, view=view, Wv=Wv):
                ins = []
                for k0 in range(0, nk, step):
                    ins.append(e.dma_start(out=view[:, k0:k0 + step, 0:n_each], in_=Wv[:, k0:k0 + step, cols_a:cols_a + n_each]))
                    ins.append(e.dma_start(out=view[:, k0:k0 + step, n_each:width], in_=Wv[:, k0:k0 + step, cols_b:cols_b + n_each]))
                return ins

            S.dma("pool", fn, writes=[B_w[i]], n=2 * (nk // step))
            return view, B_w[i]

        S.dma("sp", DMA(cb[:], cb_d[:, :]), writes=[B_cb])
        S.dma("sp", DMA(gv[:], gvec[:, :]), writes=[B_gv])
        S.dma("sp", DMA(gbias[:], gbias_d[:, :]), writes=[B_gbias])
        S.op("dve", CP(identb[:], cb[:, C_ID:C_ID + 128]), reads=[B_cb], writes=[B_cb])
        S.op("dve", CP(permb[:], cb[:, C_PERM:C_PERM + 128]), reads=[B_cb], writes=[B_cb])
        S.op("dve", CP(mutb[:], cb[:, C_MUT:C_MUT + 128]), reads=[B_cb], writes=[B_cb])
        S.op("dve", CP(cmb[:], cb[:, C_CM:C_CM + 512]), reads=[B_cb], writes=[B_cb])
        S.end_phase()

        def norm_transpose(es, src, hT, B_hT, goff, ps_t, B_pst, x1_dst=None):
            xt = [sb("xt%d" % i, [128, D], F32, es) for i in range(2)]
            xn = [sb("xn%d" % i, [128, D], BF16, es) for i in range(2)]
            st = sb("nt_st", [128, 4, NT], F32, es)
            B_xt = [Buf(), Buf()]
            B_xn = [Buf(), Buf()]
            B_st = [Buf() for _ in range(NT)]
            for i in range(NT):
                j = i % 2
                S.dma("sp", DMA(xt[j][:], src[i * 128:(i + 1) * 128, :]), writes=[B_xt[j]])
                S.op("act", ACT(xn[j][:], xt[j][:], AF.Square, accum_out=st[:, 0, i:i + 1]),
                     reads=[B_xt[j]], writes=[B_xn[j], B_st[i]])
                S.op("dve", TS(st[:, 1, i:i + 1], st[:, 0, i:i + 1], 1.0 / D, EPS, ALU.mult, ALU.add),
                     reads=[B_st[i]], writes=[B_st[i]])
                S.op("act", ACT(st[:, 2, i:i + 1], st[:, 1, i:i + 1], AF.Sqrt), reads=[B_st[i]], writes=[B_st[i]])
                S.op("dve", RCP(st[:, 3, i:i + 1], st[:, 2, i:i + 1]), reads=[B_st[i]], writes=[B_st[i]])
                S.op("act", ACT(xn[j][:], xt[j][:], AF.Copy, scale=st[:, 3, i:i + 1]),
                     reads=[B_xt[j], B_st[i]], writes=[B_xn[j]])
                for g in range(4):
                    pj = g % 2
                    for c in range(8):
                        k = g * 8 + c
                        S.op("pe", TR(ps_t[pj][:, c * 128:(c + 1) * 128], xn[j][:, k * 128:(k + 1) * 128], identb[:]),
                             reads=[B_xn[j], B_cb], writes=[B_pst[pj]])
                    S.op("dve", TT(hT[:, g * 8:(g + 1) * 8, i * 128:(i + 1) * 128],
                                   ps_t[pj][:].rearrange("p (a b) -> p a b", b=128),
                                   gv[:, goff + g * 8:goff + (g + 1) * 8].unsqueeze(2).to_broadcast([128, 8, 128]),
                                   ALU.mult),
                         reads=[B_pst[pj], B_gv], writes=[B_hT[i]])

        def rope_evict(ps, B_ps, dst, B_dst, pos0, ps_r, B_psr, tmp, B_tmp, cosT, sinT, B_rope):
            xb, ta, tb = tmp
            S.op("act", ACT(xb[:], ps[:], AF.Copy), reads=[B_ps], writes=[B_tmp[0]])
            S.op("pe", MM(ps_r[:], permb[:], xb[:]), reads=[B_tmp[0], B_cb], writes=[B_psr])
            S.op("dve", TT(ta[:], ps[:], cosT[:, pos0:pos0 + 512], ALU.mult), reads=[B_ps, B_rope], writes=[B_tmp[1]])
            S.op("dve", TT(tb[:], ps_r[:], sinT[:, pos0:pos0 + 512], ALU.mult), reads=[B_psr, B_rope], writes=[B_tmp[2]])
            S.op("dve", TT(dst, ta[:], tb[:], ALU.add), reads=[B_tmp[1], B_tmp[2]], writes=[B_dst])

        def run_pass(own):
            with ExitStack() as es:
                src = x_own if own else x_pre
                alloc_w(es, 3)
                hT = sb("hT", [128, KT, TOK], BF16, es)
                B_hT = [Buf("hT%d" % i) for i in range(NT)]
                pss = [psum("ps%d" % i, [128, 512], F32, es) for i in range(7)]
                B_ps = [PB("ps%d" % i) for i in range(7)]
                ps_t = psum("pst", [128, 1024], BF16, es)
                B_pst1 = PB("pst")
                with ExitStack() as es2:
                    norm_transpose(es2, src, hT, B_hT, 0, [ps_t, ps_t], [B_pst1, B_pst1])
                if "hT_dbg" in DEBUG_OUT and own:
                    hT_dbg = nc.dram_tensor("hT_dbg", [128, KT, TOK], BF16, kind="ExternalOutput").ap()
                    S.dma("sp", DMA(hT_dbg[:, :, :], hT[:]), reads=B_hT)
                S.end_phase()
                pos_base = 1024 if own else 0
                ntg = 2

                with ExitStack() as es2:
                    cosT = sb("cosT", [128, 2048], F32, es2)
                    sinT = sb("sinT", [128, 2048], F32, es2)
                    B_rope = Buf("rope")
                    S.dma("sp", DMA(cosT[:], cosT_d[:, :]), writes=[B_rope])
                    S.dma("sp", DMA(sinT[:], sinT_d[:, :]), reads=[B_rope], writes=[B_rope])
                    nset = 2 if own else 1
                    KTs = [sb("KTb", [128, 2, 2048], BF16, es2) for _ in range(nset)]
                    Vs = [sb("Vb", [128, 16, 2, 129], BF16, es2) for _ in range(nset)]
                    QTs = [sb("QTb", [128, 2, 1024], BF16, es2) for _ in range(nset)]
                    xb = sb("r_xb", [128, 512], BF16, es2)
                    ta = sb("r_ta", [128, 512], F32, es2)
                    tb = sb("r_tb", [128, 512], F32, es2)
                    B_tmp = [Buf(), Buf(), Buf()]
                    B_KTs = [[[Buf() for _ in range(4)] for _ in range(2)] for _ in range(nset)]
                    B_Vs = [[Buf() for _ in range(16)] for _ in range(nset)]
                    B_QTs = [[[Buf() for _ in range(2)] for _ in range(2)] for _ in range(nset)]
                    for s_ in range(nset):
                        S.op("dve", MSET(Vs[s_][:, :, :, 128:129], 1.0), writes=B_Vs[s_])
                    if own:
                        ksb = sb("ksb", [128, 8], BF16, es2)
                        Gb = sb("Gb", [128, 8, 8], F32, es2)
                        top8 = sb("top8", [128, 8, 8], F32, es2)
                        thr = sb("thr", [128, 8], F32, es2)
                        sel = sb("sel", [128, 8, 8], F32, es2)
                        PT = [sb("PT%d" % i, [128, 2, 256], BF16, es2) for i in range(2)]
                        Oacc = sb("Oacc", [128, 8, 129], F32, es2)
                        rec = sb("rec", [128, 8], F32, es2)
                        mo_tok = sb("mo_tok", [128, 8, 256], BF16, es2)
                        moT_g = sb("moT_g", [128, 2, 1024], BF16, es2)
                        B_ksb, B_Gb, B_top8, B_thr, B_sel = Buf(), Buf(), Buf(), Buf(), Buf()
                        B_PT = [Buf(), Buf()]
                        B_Oacc = [Buf() for _ in range(8)]
                        B_rec = Buf()
                        B_mo = [Buf() for _ in range(8)]
                        B_moTg = Buf()

                    def proj_gen(g, s_):
                        KTb, Vb, QTb = KTs[s_], Vs[s_], QTs[s_]
                        B_KT, B_V, B_QT = B_KTs[s_], B_Vs[s_], B_QTs[s_]
                        if own:
                            for hl in range(2):
                                S.dma("sp", DMA(KTb[:, hl, 0:1024], kt_s[2 * g + hl, :, :]), writes=[B_KT[hl][0], B_KT[hl][1]])
                            S.dma("sp", DMA(Vb[:, 0:8, :, :].rearrange("p a b c -> p (a b c)"), v_s[g, :, :]), writes=B_V[0:8])
                        wk, Bwk = wload(w_in_v, KT, OK_ + g * 256, 256)
                        for hl in range(2):
                            for tg in range(ntg):
                                ps = pss[tg % 2]
                                Bp = B_ps[tg % 2]
                                for k in range(KT):
                                    S.op("pe", MM(ps[:], wk[:, k, hl * 128:(hl + 1) * 128], hT[:, k, tg * 512:(tg + 1) * 512],
                                                  start=(k == 0), stop=(k == KT - 1)),
                                         reads=[Bwk] + B_hT[tg * 4:(tg + 1) * 4], writes=[Bp])
                                    if k % 8 == 7 and k != KT - 1:
                                        yield
                                gi = (2 if own else 0) + tg
                                rope_evict(ps, Bp, KTb[:, hl, gi * 512:(gi + 1) * 512], B_KT[hl][gi], pos_base + tg * 512,
                                           pss[2], B_ps[2], (xb, ta, tb), B_tmp, cosT, sinT, B_rope)
                                yield
                            b0 = 4 if own else 0
                            S.op("dve", lambda e, hl=hl, b0=b0, g=g, KTb=KTb: e.tensor_reduce(
                                out=ksum[:, 2 * g + hl, b0:b0 + 4],
                                in_=KTb[:, hl, b0 * 256:(b0 + 4) * 256].rearrange("p (b t) -> p b t", t=256),
                                axis=AX.X, op=ALU.add),
                                 reads=B_KT[hl][(2 if own else 0):(4 if own else 2)], writes=[B_ksum[2 * g + hl]])
                        wv, Bwv = wload(w_in_v, KT, OV + g * 256, 256)
                        for i in range(NT):
                            ps = pss[i % 2]
                            Bp = B_ps[i % 2]
                            for k in range(KT):
                                S.op("pe", MM(ps[:, 0:256], hT[:, k, i * 128:(i + 1) * 128], wv[:, k, :],
                                              start=(k == 0), stop=(k == KT - 1)),
                                     reads=[Bwv, B_hT[i]], writes=[Bp])
                                if k == 15:
                                    yield
                            ti = (8 if own else 0) + i
                            S.op("act", ACT(Vb[:, ti, :, 0:128], ps[:, 0:256].rearrange("p (a b) -> p a b", b=128), AF.Copy),
                                 reads=[Bp], writes=[B_V[ti]])
                            yield
                        if not own:
                            for hl in range(2):
                                S.dma("sp", DMA(kt_s[2 * g + hl, :, :], KTb[:, hl, 0:1024]), reads=[B_KT[hl][0], B_KT[hl][1]])
                            S.dma("sp", DMA(v_s[g, :, :], Vb[:, 0:8, :, :].rearrange("p a b c -> p (a b c)")), reads=B_V[0:8])
                            return
                        wq, Bwq = wload(w_in_v, KT, OQ + g * 256, 256)
                        for hl in range(2):
                            for tg in range(ntg):
                                ps = pss[tg % 2]
                                Bp = B_ps[tg % 2]
                                for k in range(KT):
                                    S.op("pe", MM(ps[:], wq[:, k, hl * 128:(hl + 1) * 128], hT[:, k, tg * 512:(tg + 1) * 512],
                                                  start=(k == 0), stop=(k == KT - 1)),
                                         reads=[Bwq] + B_hT[tg * 4:(tg + 1) * 4], writes=[Bp])
                                    if k % 8 == 7 and k != KT - 1:
                                        yield
                                rope_evict(ps, Bp, QTb[:, hl, tg * 512:(tg + 1) * 512], B_QT[hl][tg], pos_base + tg * 512,
                                           pss[2], B_ps[2], (xb, ta, tb), B_tmp, cosT, sinT, B_rope)
                                yield

                    def attn_gen(g, s_):
                        KTb, Vb, QTb = KTs[s_], Vs[s_], QTs[s_]
                        B_KT, B_V, B_QT = B_KTs[s_], B_Vs[s_], B_QTs[s_]
                        for hl in range(2):
                            h = 2 * g + hl
                            S.op("act", ACT(ksb[:], ksum[:, h, :], AF.Copy), reads=[B_ksum[h]], writes=[B_ksb])
                            gps = pss[2]
                            for qt in range(8):
                                S.op("pe", MM(gps[:, qt * 8:(qt + 1) * 8], QTb[:, hl, qt * 128:(qt + 1) * 128], ksb[:]),
                                     reads=[B_QT[hl][qt // 4], B_ksb], writes=[B_ps[2]])
                            S.op("dve", TT(Gb[:].rearrange("p a b -> p (a b)"), gps[:, 0:64], gbias[:], ALU.add),
                                 reads=[B_ps[2], B_gbias], writes=[B_Gb])
                            for qt in range(8):
                                S.op("dve", lambda e, qt=qt: e.max(out=top8[:, qt, :], in_=Gb[:, qt, :]),
                                     reads=[B_Gb], writes=[B_top8])
                            S.op("dve", TS(thr[:], top8[:, :, 2], -1.0e30, None, ALU.max), reads=[B_top8], writes=[B_thr])
                            S.op("dve", TT(sel[:], Gb[:], thr[:].unsqueeze(2).to_broadcast([128, 8, 8]), ALU.is_ge),
                                 reads=[B_Gb, B_thr], writes=[B_sel])
                            yield
                            step = 0
                            for jb in range(4):
                                for n in range(5 + jb):
                                    ownb = (n == 4 + jb)
                                    sps = pss[3 + step % 2]
                                    Bsp = B_ps[3 + step % 2]
                                    pt = PT[step % 2]
                                    Bpt = B_PT[step % 2]
                                    ops = pss[5 + step % 2]
                                    Bop = B_ps[5 + step % 2]
                                    step += 1
                                    for a in range(2):
                                        kti = 2 * n + a
                                        S.op("pe", MM(sps[:, a * 256:(a + 1) * 256], KTb[:, hl, kti * 128:(kti + 1) * 128],
                                                      QTb[:, hl, jb * 256:(jb + 1) * 256]),
                                             reads=[B_KT[hl][kti // 4], B_QT[hl][jb // 2]], writes=[Bsp])
                                    S.op("act", ACT(pt[:].rearrange("p a b -> p (a b)"), sps[:], AF.Exp, scale=128.0 ** -0.5),
                                         reads=[Bsp], writes=[Bpt])
                                    if ownb:
                                        S.op("dve", TT(pt[:].rearrange("p a b -> p (a b)"), pt[:].rearrange("p a b -> p (a b)"),
                                                       cmb[:], ALU.mult), reads=[Bpt, B_cb], writes=[Bpt])
                                    yield
                                    for b in range(2):
                                        qt = jb * 2 + b
                                        alist = [0] if (ownb and b == 0) else [0, 1]
                                        for ai, a in enumerate(alist):
                                            S.op("pe", MM(ops[:, b * 160:b * 160 + 129], pt[:, a, b * 128:(b + 1) * 128],
                                                          Vb[:, 2 * n + a, hl, :], start=(ai == 0), stop=(ai == len(alist) - 1)),
                                                 reads=[Bpt, B_V[2 * n + a]], writes=[Bop])
                                        o_ps = ops[:, b * 160:b * 160 + 129]
                                        if n == 0:
                                            S.op("dve", TS(Oacc[:, qt, :], o_ps, sel[:, qt, 0:1], None, ALU.mult),
                                                 reads=[Bop, B_sel], writes=[B_Oacc[qt]])
                                        elif ownb:
                                            S.op("dve", TT(Oacc[:, qt, :], o_ps, Oacc[:, qt, :], ALU.add),
                                                 reads=[Bop, B_Oacc[qt]], writes=[B_Oacc[qt]])
                                        else:
                                            S.op("dve", STT(Oacc[:, qt, :], o_ps, sel[:, qt, n:n + 1], Oacc[:, qt, :], ALU.mult, ALU.add),
                                                 reads=[Bop, B_sel, B_Oacc[qt]], writes=[B_Oacc[qt]])
                                    yield
                            S.op("dve", RCP(rec[:], Oacc[:, :, 128]), reads=B_Oacc, writes=[B_rec])
                            for qt in range(8):
                                S.op("dve", TS(mo_tok[:, qt, hl * 128:(hl + 1) * 128], Oacc[:, qt, 0:128], rec[:, qt:qt + 1], None, ALU.mult),
                                     reads=[B_Oacc[qt], B_rec], writes=[B_mo[qt]])
                        for qt in range(8):
                            for hl in range(2):
                                S.op("pe", TR(ps_t[:, hl * 128:(hl + 1) * 128], mo_tok[:, qt, hl * 128:(hl + 1) * 128], identb[:]),
                                     reads=[B_mo[qt], B_cb], writes=[B_pst1])
                            S.op("act", ACT(moT_g[:, :, qt * 128:(qt + 1) * 128], ps_t[:, 0:256].rearrange("p (a b) -> p a b", b=128), AF.Copy),
                                 reads=[B_pst1], writes=[B_moTg])
                        for hl in range(2):
                            S.dma("sp", DMA(moT_s[2 * g + hl, :, :], moT_g[:, hl, :]), reads=[B_moTg])
                        yield

                    def drain(gen):
                        for _ in gen:
                            pass

                    if not own:
                        for g in range(8):
                            drain(proj_gen(g, 0))
                    else:
                        drain(proj_gen(0, 0))
                        for g in range(8):
                            A_ = attn_gen(g, g % 2)
                            P_ = proj_gen(g + 1, (g + 1) % 2) if g < 7 else iter(())
                            a_alive, p_alive = True, True
                            while a_alive or p_alive:
                                if a_alive:
                                    try:
                                        next(A_)
                                    except StopIteration:
                                        a_alive = False
                                if p_alive:
                                    try:
                                        next(P_)
                                    except StopIteration:
                                        p_alive = False
                S.end_phase()

                with ExitStack() as es2:
                    gadT = sb("gadT", [16, TOK], F32, es2)
                    gup = sb("gup_sb", [16, 2048], F32, es2)
                    glabs = [sb("glab_sb", [128, 256], F32, es2) for _ in range(2)]
                    B_glabs = [Buf(), Buf()]
                    g512 = sb("g512_sb", [128, 512], F32, es2)
                    B_gad, B_gup, B_glab, B_g512 = Buf(), Buf(), Buf(), Buf()
                    S.dma("sp", DMA(gup[:], gup_d[:, :]), writes=[B_gup])
                    S.dma("sp", DMA(g512[:], g512_d[:, :]), writes=[B_g512])
                    wga, Bwga = wload(w_in_v, KT, OGA, 16)
                    for tg in range(2):
                        for k in range(KT):
                            S.op("pe", MM(pss[0][0:16, :], wga[:, k, 0:16], hT[:, k, tg * 512:(tg + 1) * 512],
                                          start=(k == 0), stop=(k == KT - 1)),
                                 reads=[Bwga] + B_hT[tg * 4:(tg + 1) * 4], writes=[B_ps[0]])
                        S.op("act", ACT(gadT[:, tg * 512:(tg + 1) * 512], pss[0][0:16, :], AF.Copy), reads=[B_ps[0]], writes=[B_gad])
                    zb = sb("zb", [128, 256], F32, es2)
                    e1 = sb("e1", [128, 256], F32, es2)
                    lg_all = sb("lg_all", [128, NT, 256], F32, es2)
                    dec = [sb("dec%d" % i, [128, 256], F32, es2) for i in range(2)]
                    ebl = sb("ebl", [128, 4], F32, es2)
                    qd_all = sb("qd_all", [128, NT, 256], BF16, es2)
                    ki_all = sb("ki_all", [128, NT, 256], BF16, es2)
                    ke_all = sb("ke_all", [128, NT, 256], BF16, es2)
                    vb_all = sb("vb_all", [128, NT, 512], BF16, es2)
                    sg_all = sb("sg_all", [128, NT, 512], BF16, es2)
                    qdT = sb("qdT", [128, 2, 128], BF16, es2)
                    kiT = sb("kiT", [128, 2, 128], BF16, es2)
                    At = sb("At", [128, 128], BF16, es2)
                    Sbf = sb("Sbf", [128, 2, 512], BF16, es2)
                    st_h = [sb("st_h%d" % i, [128, 2, 512], F32, es2) for i in range(2)]
                    gst = sb("gst", [128, 4], F32, es2)
                    on_all = sb("on_all", [128, NT, 512], BF16, es2)
                    ot = sb("ot", [128, 512], BF16, es2)
                    goT_h = sb("goT_h", [128, 4, TOK], BF16, es2)
                    (B_zb, B_e1, B_ebl, B_qdT, B_kiT, B_At, B_Sbf, B_gst, B_ot, B_junk2, B_goT) = [Buf() for _ in range(11)]
                    B_on = [Buf() for _ in range(NT)]
                    B_lg = [Buf() for _ in range(NT)]
                    B_qd = [Buf() for _ in range(NT)]
                    B_ki = [Buf() for _ in range(NT)]
                    B_ke = [Buf() for _ in range(NT)]
                    B_vb = [Buf() for _ in range(NT)]
                    B_sg = [Buf() for _ in range(NT)]
                    B_dec = [Buf(), Buf()]
                    B_sth = [Buf(), Buf()]
                    z_ps, bc_ps = pss[3][:, 0:256], pss[3][:, 256:512]
                    rv_ps, bl_ps, A_ps = pss[4][:, 0:256], pss[4][:, 256:260], pss[4][:, 264:392]
                    B_z, B_bc = Buf("z", B_ps[3].tok), Buf("bc", B_ps[3].tok)
                    B_rv, B_bl, B_A = Buf("rv", B_ps[4].tok), Buf("bl", B_ps[4].tok), Buf("A", B_ps[4].tok)
                    O_ps, kv_ps = pss[5], pss[6]
                    B_O, B_kv = B_ps[5], B_ps[6]
                    B_tq, B_to = Buf("tq", B_pst1.tok), Buf("to", B_pst1.tok)
                    dn = [0]

                    def nextdec():
                        j = dn[0] % 2
                        dn[0] += 1
                        return dec[j], B_dec[j]

                    pn = [0]

                    def proj(wv_, Bw_, i, ncol=256):
                        j = pn[0] % 3
                        pn[0] += 1
                        ps, Bp = pss[j], B_ps[j]
                        for k in range(KT):
                            S.op("pe", MM(ps[:, 0:ncol], hT[:, k, i * 128:(i + 1) * 128], wv_[:, k, :], start=(k == 0), stop=(k == KT - 1)),
                                 reads=[Bw_, B_hT[i]], writes=[Bp])
                        return ps[:, 0:ncol], Bp

                    for hh in range(8):
                        sth, Bsth = st_h[hh % 2], B_sth[hh % 2]
                        glab, B_glab = glabs[hh % 2], B_glabs[hh % 2]
                        S.dma("sp", DMA(glab[:], glab_d[:, hh * 256:(hh + 1) * 256]), writes=[B_glab])
                        if own:
                            S.dma("sp", DMA(sth[:].rearrange("p a b -> p (a b)"), st_s[hh, :, :]), writes=[Bsth])
                        else:
                            S.op("dve", MSET(sth[:], 0.0), writes=[Bsth])
                        for i in range(NT):
                            tok = slice(i * 128, (i + 1) * 128)
                            S.op("pe", MM(z_ps, gadT[0:16, tok], gup[0:16, hh * 256:(hh + 1) * 256]),
                                 reads=[B_gad, B_gup], writes=[B_z])
                            S.op("dve", TT(zb[:], z_ps, glab[:], ALU.add), reads=[B_z, B_glab], writes=[B_zb])
                            S.op("act", ACT(e1[:], zb[:], AF.Exp, scale=-1.0), reads=[B_zb], writes=[B_e1])
                            S.op("act", ACT(lg_all[:, i, :], e1[:], AF.Ln, bias=1.0), reads=[B_e1], writes=[B_lg[i]])
                        if own:
                            wq_, Bwq_ = wload(w_in_v, KT, OGQ + hh * 256, 256)
                            for i in range(NT):
                                S.op("pe", MM(bc_ps, triU, lg_all[:, i, :]), reads=[B_lg[i], B_cb], writes=[B_bc])
                                d_, Bd_ = nextdec()
                                S.op("act", ACT(d_[:], bc_ps, AF.Exp), reads=[B_bc], writes=[Bd_])
                                ps, Bp = proj(wq_, Bwq_, i)
                                S.op("dve", STT(qd_all[:, i, :], ps, 1.0 / 16.0, d_[:], ALU.mult, ALU.mult), reads=[Bp, Bd_], writes=[B_qd[i]])
                        wk_, Bwk_ = wload(w_in_v, KT, OGK + hh * 256, 256)
                        for i in range(NT):
                            S.op("pe", MM(rv_ps, triSL, lg_all[:, i, :]), reads=[B_lg[i], B_cb], writes=[B_rv])
                            d_, Bd_ = nextdec()
                            S.op("act", ACT(d_[:], rv_ps, AF.Exp), reads=[B_rv], writes=[Bd_])
                            if own:
                                S.op("pe", MM(bc_ps, triU, lg_all[:, i, :]), reads=[B_lg[i], B_cb], writes=[B_bc])
                                d2_, Bd2_ = nextdec()
                                S.op("act", ACT(d2_[:], bc_ps, AF.Exp, scale=-1.0), reads=[B_bc], writes=[Bd2_])
                            ps, Bp = proj(wk_, Bwk_, i)
                            S.op("dve", TT(ke_all[:, i, :], ps, d_[:], ALU.mult), reads=[Bp, Bd_], writes=[B_ke[i]])
                            if own:
                                S.op("dve", TT(ki_all[:, i, :], ps, d2_[:], ALU.mult), reads=[Bp, Bd2_], writes=[B_ki[i]])
                        for vh in range(2):
                            wv_, Bwv_ = wload(w_in_v, KT, OGV + hh * 512 + vh * 256, 256)
                            for i in range(NT):
                                ps, Bp = proj(wv_, Bwv_, i)
                                S.op("act", ACT(vb_all[:, i, vh * 256:(vh + 1) * 256], ps, AF.Copy), reads=[Bp], writes=[B_vb[i]])
                        def gr_gen(hh=hh):
                            for vh in range(2):
                                wg_, Bwg_ = wload(w_in_v, KT, OGR + hh * 512 + vh * 256, 256)
                                for i in range(NT):
                                    ps, Bp = proj(wg_, Bwg_, i)
                                    S.op("act", ACT(sg_all[:, i, vh * 256:(vh + 1) * 256], ps, AF.Silu), reads=[Bp], writes=[B_sg[i]])
                                    yield

                        def rec_gen(hh=hh, sth=sth, Bsth=Bsth):
                            if own:
                                S.op("act", ACT(Sbf[:], sth[:], AF.Copy), reads=[Bsth], writes=[B_Sbf])
                            for i in range(NT):
                                for dt in range(2):
                                    S.op("pe", MM(bl_ps[:, 2 * dt:2 * dt + 2], lg_all[:, i, dt * 128:(dt + 1) * 128], negcol),
                                         reads=[B_lg[i], B_cb], writes=[B_bl])
                                S.op("act", ACT(ebl[:], bl_ps, AF.Exp), reads=[B_bl], writes=[B_ebl])
                                if own:
                                    for dt in range(2):
                                        S.op("pe", TR(ps_t[:, dt * 128:(dt + 1) * 128], qd_all[:, i, dt * 128:(dt + 1) * 128], identb[:]),
                                             reads=[B_qd[i], B_cb], writes=[B_tq])
                                        S.op("pe", TR(ps_t[:, 256 + dt * 128:256 + (dt + 1) * 128], ki_all[:, i, dt * 128:(dt + 1) * 128], identb[:]),
                                             reads=[B_ki[i], B_cb], writes=[B_tq])
                                    S.op("act", ACT(qdT[:], ps_t[:, 0:256].rearrange("p (a b) -> p a b", b=128), AF.Copy), reads=[B_tq], writes=[B_qdT])
                                    S.op("act", ACT(kiT[:], ps_t[:, 256:512].rearrange("p (a b) -> p a b", b=128), AF.Copy), reads=[B_tq], writes=[B_kiT])
                                    yield
                                    for dt in range(2):
                                        S.op("pe", MM(A_ps, kiT[:, dt, :], qdT[:, dt, :], start=(dt == 0), stop=(dt == 1)),
                                             reads=[B_kiT, B_qdT], writes=[B_A])
                                    S.op("dve", TT(At[:], A_ps, mutb[:], ALU.mult), reads=[B_A, B_cb], writes=[B_At])
                                    S.op("pe", MM(O_ps[:], At[:], vb_all[:, i, :], start=True, stop=False), reads=[B_At, B_vb[i]], writes=[B_O])
                                    for dt in range(2):
                                        S.op("pe", MM(O_ps[:], qdT[:, dt, :], Sbf[:, dt, :], start=False, stop=(dt == 1)),
                                             reads=[B_qdT, B_Sbf], writes=[B_O])
                                for dt in range(2):
                                    S.op("pe", MM(kv_ps[:], ke_all[:, i, dt * 128:(dt + 1) * 128], vb_all[:, i, :]), reads=[B_ke[i], B_vb[i]], writes=[B_kv])
                                    S.op("dve", STT(sth[:, dt, :], sth[:, dt, :], ebl[:, 2 * dt:2 * dt + 1], kv_ps[:], ALU.mult, ALU.add),
                                         reads=[Bsth, B_ebl, B_kv], writes=[Bsth])
                                if own:
                                    if i < NT - 1:
                                        S.op("act", ACT(Sbf[:], sth[:], AF.Copy), reads=[Bsth], writes=[B_Sbf])
                                    S.op("act", ACT(ot[:], O_ps[:], AF.Square, accum_out=gst[:, 0:1]), reads=[B_O], writes=[B_ot, B_gst])
                                    S.op("dve", TS(gst[:, 1:2], gst[:, 0:1], 1.0 / 512.0, EPS, ALU.mult, ALU.add), reads=[B_gst], writes=[B_gst])
                                    S.op("act", ACT(gst[:, 2:3], gst[:, 1:2], AF.Sqrt), reads=[B_gst], writes=[B_gst])
                                    S.op("dve", RCP(gst[:, 3:4], gst[:, 2:3]), reads=[B_gst], writes=[B_gst])
                                    S.op("dve", STT(on_all[:, i, :], O_ps[:], gst[:, 3:4], g512[:], ALU.mult, ALU.mult),
                                         reads=[B_O, B_gst, B_g512], writes=[B_on[i]])
                                yield

                        if own:
                            G_, R_ = gr_gen(), rec_gen()
                            g_alive, r_alive = True, True
                            while g_alive or r_alive:
                                if r_alive:
                                    try:
                                        next(R_)
                                    except StopIteration:
                                        r_alive = False
                                if g_alive:
                                    try:
                                        next(G_)
                                    except StopIteration:
                                        g_alive = False
                            for i in range(NT):
                                tok = slice(i * 128, (i + 1) * 128)
                                S.op("dve", TT(ot[:], on_all[:, i, :], sg_all[:, i, :], ALU.mult), reads=[B_on[i], B_sg[i]], writes=[B_ot])
                                for c in range(4):
                                    S.op("pe", TR(ps_t[:, 512 + c * 128:512 + (c + 1) * 128], ot[:, c * 128:(c + 1) * 128], identb[:]),
                                         reads=[B_ot, B_cb], writes=[B_to])
                                S.op("act", ACT(goT_h[:, :, tok], ps_t[:, 512:1024].rearrange("p (a b) -> p a b", b=128), AF.Copy),
                                     reads=[B_to], writes=[B_goT])
                        else:
                            for _ in rec_gen():
                                pass
                        if own:
                            for c in range(4):
                                S.dma("sp", DMA(goT_s[hh * 4 + c, :, :], goT_h[:, c, :]), reads=[B_goT])
                        else:
                            S.dma("sp", DMA(st_s[hh, :, :], sth[:].rearrange("p a b -> p (a b)")), reads=[Bsth])
                S.end_phase()
                if not own:
                    return

                with ExitStack() as es2:
                    sgt = [sb("sgt%d" % i, [128, 4, TOK], BF16, es2) for i in range(2)]
                    B_sgt = [Buf(), Buf()]
                    ci = 0
                    for (off, dst) in ((OGM, sgm_s), (OGG, sgg_s)):
                        for cc in range(16):
                            wg_, Bwg_ = wload(w_in_v, KT, off + cc * 256, 256)
                            st_ = sgt[ci % 2]
                            Bst_ = B_sgt[ci % 2]
                            ci += 1
                            for sub in range(2):
                                for tg in range(2):
                                    ps = pss[(sub * 2 + tg) % 4]
                                    Bp = B_ps[(sub * 2 + tg) % 4]
                                    for k in range(KT):
                                        S.op("pe", MM(ps[:], wg_[:, k, sub * 128:(sub + 1) * 128], hT[:, k, tg * 512:(tg + 1) * 512],
                                                      start=(k == 0), stop=(k == KT - 1)),
                                             reads=[Bwg_] + B_hT[tg * 4:(tg + 1) * 4], writes=[Bp])
                                    S.op("act", ACT(st_[:, sub, tg * 512:(tg + 1) * 512], ps[:], AF.Sigmoid), reads=[Bp], writes=[Bst_])
                            for sub in range(2):
                                S.dma("sp", DMA(dst[cc * 2 + sub, :, :], st_[:, sub, :]), reads=[Bst_])
                S.end_phase()
            return None

        run_pass(False)
        run_pass(True)

        with ExitStack() as es:
            pss = [psum("ps%d" % i, [128, 512], F32, es) for i in range(8)]
            B_ps = [PB("ps%d" % i) for i in range(8)]
            mT = sb("mT", [128, KT, TOK], BF16, es)
            B_mT = [Buf() for _ in range(NT)]
            with ExitStack() as es2:
                alloc_w(es2, 3, 4096)
                moT = sb("moT", [128, 16, TOK], BF16, es2)
                goT = sb("goT", [128, 32, TOK], BF16, es2)
                B_moT, B_goT2 = Buf(), Buf()
                B_moTp = [Buf() for _ in range(2)]
                B_goTp = [Buf() for _ in range(4)]
                for q_ in range(2):
                    S.dma("sp", DMA(moT[:, q_ * 8:(q_ + 1) * 8, :], moT_s[q_ * 8:(q_ + 1) * 8, :, :].rearrange("h p t -> p h t")), writes=[B_moTp[q_]])
                for q_ in range(4):
                    S.dma("sp", DMA(goT[:, q_ * 8:(q_ + 1) * 8, :], goT_s[q_ * 8:(q_ + 1) * 8, :, :].rearrange("h p t -> p h t")), writes=[B_goTp[q_]])
                sgm = [sb("sgm%d" % i, [128, TOK], BF16, es2) for i in range(2)]
                sgg = [sb("sgg%d" % i, [128, TOK], BF16, es2) for i in range(2)]
                B_sgm, B_sgg = [Buf(), Buf()], [Buf(), Buf()]
                t1 = sb("t1", [128, 512], F32, es2)
                t2 = sb("t2", [128, 512], F32, es2)
                B_t1, B_t2 = Buf(), Buf()
                for c in range(32):
                    j = c % 2
                    wm, Bwm = wload(w_bm_v, 16, c * 128, 128)
                    wg_, Bwg_ = wload(w_bg_v, KT, c * 128, 128)
                    S.dma("sp", DMA(sgm[j][:], sgm_s[c, :, :]), writes=[B_sgm[j]])
                    S.dma("sp", DMA(sgg[j][:], sgg_s[c, :, :]), writes=[B_sgg[j]])
                    for tg in range(2):
                        pm, Bpm = pss[tg * 2], B_ps[tg * 2]
                        pg, Bpg = pss[tg * 2 + 1], B_ps[tg * 2 + 1]
                        for k in range(16):
                            S.op("pe", MM(pm[:], wm[:, k, :], moT[:, k, tg * 512:(tg + 1) * 512],
                                          start=(k == 0), stop=(k == 15)), reads=[Bwm] + B_moTp, writes=[Bpm])
                        for k in range(KT):
                            S.op("pe", MM(pg[:], wg_[:, k, :], goT[:, k, tg * 512:(tg + 1) * 512],
                                          start=(k == 0), stop=(k == KT - 1)), reads=[Bwg_] + B_goTp, writes=[Bpg])
                        S.op("dve", TT(t1[:], pm[:], sgm[j][:, tg * 512:(tg + 1) * 512], ALU.mult), reads=[Bpm, B_sgm[j]], writes=[B_t1])
                        S.op("dve", TT(t2[:], pg[:], sgg[j][:, tg * 512:(tg + 1) * 512], ALU.mult), reads=[Bpg, B_sgg[j]], writes=[B_t2])
                        S.op("dve", TT(mT[:, c, tg * 512:(tg + 1) * 512], t1[:], t2[:], ALU.add), reads=[B_t1, B_t2],
                             writes=B_mT[tg * 4:(tg + 1) * 4])
                S.end_phase()
            alloc_w(es, 3)
            ybuf = [sb("ybuf%d" % i, [128, 256], F32, es) for i in range(2)]
            B_yb = [Buf(), Buf()]
            ci = 0
            for cc in range(16):
                wo, Bwo = wload(w_out_v, KT, cc * 256, 256)
                for i in range(NT):
                    ps, Bp = pss[4 + ci % 4], B_ps[4 + ci % 4]
                    yb, Byb = ybuf[ci % 2], B_yb[ci % 2]
                    ci += 1
                    for k in range(KT):
                        S.op("pe", MM(ps[:, 0:256], mT[:, k, i * 128:(i + 1) * 128], wo[:, k, :], start=(k == 0), stop=(k == KT - 1)),
                             reads=[Bwo, B_mT[i]], writes=[Bp])
                    S.op("act", ACT(yb[:], ps[:, 0:256], AF.Copy), reads=[Bp], writes=[Byb])
                    S.dma("sp", DMA(y_s[i * 128:(i + 1) * 128, cc * 256:(cc + 1) * 256], yb[:]), reads=[Byb])
            S.end_phase()

        with ExitStack() as es:
            h2T = sb("h2T", [128, KT, TOK], BF16, es)
            B_h2T = [Buf() for _ in range(NT)]
            ps_t = psum("pst", [128, 1024], BF16, es)
            B_pst1 = PB("pst")
            gpm = sb("gpm_sb", [128, D], F32, es)
            B_gpm = Buf()
            S.dma("sp", DMA(gpm[:], gpm_d[:, :]), writes=[B_gpm])
            yt = [sb("yt%d" % i, [128, D], F32, es) for i in range(2)]
            xt = [sb("xt%d" % i, [128, D], F32, es) for i in range(2)]
            x1t = [sb("x1t%d" % i, [128, D], F32, es) for i in range(2)]
            xn = [sb("xn%d" % i, [128, D], BF16, es) for i in range(2)]
            st = sb("st4", [128, 8, NT], F32, es)
            B_yt, B_xt, B_x1t, B_xn = [Buf(), Buf()], [Buf(), Buf()], [Buf(), Buf()], [Buf(), Buf()]
            B_junk = Buf()
            B_st = [Buf() for _ in range(NT)]
            for i in range(NT):
                j = i % 2
                rows = slice(i * 128, (i + 1) * 128)
                S.dma("sp", DMA(yt[j][:], y_s[rows, :]), writes=[B_yt[j]])
                S.dma("sp", DMA(xt[j][:], x_own[rows, :]), writes=[B_xt[j]])
                S.op("act", ACT(xn[j][:], yt[j][:], AF.Square, accum_out=st[:, 0, i:i + 1]), reads=[B_yt[j]], writes=[B_xn[j], B_st[i]])
                S.op("dve", TS(st[:, 1, i:i + 1], st[:, 0, i:i + 1], 1.0 / D, EPS, ALU.mult, ALU.add), reads=[B_st[i]], writes=[B_st[i]])
                S.op("act", ACT(st[:, 2, i:i + 1], st[:, 1, i:i + 1], AF.Sqrt), reads=[B_st[i]], writes=[B_st[i]])
                S.op("dve", RCP(st[:, 3, i:i + 1], st[:, 2, i:i + 1]), reads=[B_st[i]], writes=[B_st[i]])
                S.op("dve", STT(x1t[j][:], yt[j][:], st[:, 3, i:i + 1], gpm[:], ALU.mult, ALU.mult),
                     reads=[B_yt[j], B_st[i], B_gpm], writes=[B_x1t[j]])
                S.op("dve", TT(x1t[j][:], x1t[j][:], xt[j][:], ALU.add), reads=[B_x1t[j], B_xt[j]], writes=[B_x1t[j]])
                S.dma("sp", DMA(x1_s[rows, :], x1t[j][:]), reads=[B_x1t[j]])
                S.op("act", ACT(xn[j][:], x1t[j][:], AF.Square, accum_out=st[:, 4, i:i + 1]), reads=[B_x1t[j]], writes=[B_xn[j], B_st[i]])
                S.op("dve", TS(st[:, 5, i:i + 1], st[:, 4, i:i + 1], 1.0 / D, EPS, ALU.mult, ALU.add), reads=[B_st[i]], writes=[B_st[i]])
                S.op("act", ACT(st[:, 6, i:i + 1], st[:, 5, i:i + 1], AF.Sqrt), reads=[B_st[i]], writes=[B_st[i]])
                S.op("dve", RCP(st[:, 7, i:i + 1], st[:, 6, i:i + 1]), reads=[B_st[i]], writes=[B_st[i]])
                S.op("act", ACT(xn[j][:], x1t[j][:], AF.Copy, scale=st[:, 7, i:i + 1]), reads=[B_x1t[j], B_st[i]], writes=[B_xn[j]])
                for g in range(4):
                    for c in range(8):
                        k = g * 8 + c
                        S.op("pe", TR(ps_t[:, c * 128:(c + 1) * 128], xn[j][:, k * 128:(k + 1) * 128], identb[:]),
                             reads=[B_xn[j], B_cb], writes=[B_pst1])
                    S.op("dve", TT(h2T[:, g * 8:(g + 1) * 8, i * 128:(i + 1) * 128],
                                   ps_t[:].rearrange("p (a b) -> p a b", b=128),
                                   gv[:, 32 + g * 8:32 + (g + 1) * 8].unsqueeze(2).to_broadcast([128, 8, 128]), ALU.mult),
                         reads=[B_pst1, B_gv], writes=[B_h2T[i]])
            for half in range(2):
                S.dma("sp", DMA(h2T_s[half, :, :, :], h2T[:, :, half * 512:(half + 1) * 512]), reads=B_h2T[half * 4:(half + 1) * 4])
            S.end_phase()

        with ExitStack() as es:
            pss = [psum("ps%d" % i, [128, 512], F32, es) for i in range(8)]
            B_ps = [PB("ps%d" % i) for i in range(8)]
            alloc_w(es, 3)
            hid = sb("hid", [128, FT, 512], BF16, es)
            B_hid = [Buf() for _ in range(FT)]
            sgf = [sb("sgf%d" % i, [128, 512], F32, es) for i in range(2)]
            B_sgf = [Buf(), Buf()]
            y2b = [sb("y2b%d" % i, [128, 512], F32, es) for i in range(2)]
            B_y2b = [Buf(), Buf()]
            h2Th = sb("h2Th", [128, KT, 512], BF16, es)
            B_h2Th = Buf()
            for half in range(2):
                S.dma("sp", DMA(h2Th[:], h2T_s[half, :, :, :]), writes=[B_h2Th])
                ci = 0
                for cc in range(43):
                    wg_, Bwg_ = wload(w_fg_v, KT, cc * 256, 256)
                    wu_, Bwu_ = wload(w_fu_v, KT, cc * 256, 256)
                    for sub in range(2):
                        f = cc * 2 + sub
                        pg, Bpg = pss[(ci % 2) * 2], B_ps[(ci % 2) * 2]
                        pu, Bpu = pss[(ci % 2) * 2 + 1], B_ps[(ci % 2) * 2 + 1]
                        sg_, Bsg_ = sgf[ci % 2], B_sgf[ci % 2]
                        ci += 1
                        for k in range(KT):
                            S.op("pe", MM(pg[:], wg_[:, k, sub * 128:(sub + 1) * 128], h2Th[:, k, :], start=(k == 0), stop=(k == KT - 1)),
                                 reads=[Bwg_, B_h2Th], writes=[Bpg])
                        for k in range(KT):
                            S.op("pe", MM(pu[:], wu_[:, k, sub * 128:(sub + 1) * 128], h2Th[:, k, :], start=(k == 0), stop=(k == KT - 1)),
                                 reads=[Bwu_, B_h2Th], writes=[Bpu])
                        S.op("act", ACT(sg_[:], pg[:], AF.Silu), reads=[Bpg], writes=[Bsg_])
                        S.op("dve", TT(hid[:, f, :], sg_[:], pu[:], ALU.mult), reads=[Bsg_, Bpu], writes=[B_hid[f]])
                S.end_phase()
                ci = 0
                for qd_ in range(4):
                    c0 = qd_ * 1024
                    for f0 in range(0, FT, 8):
                        nf = min(8, FT - f0)
                        i3 = wn[0] % len(wbuf)
                        wn[0] += 1
                        wd = wbuf[i3][:, 0:8 * 1024].rearrange("p (k c) -> p k c", c=1024)
                        Bwd = B_w[i3]

                        def fn(e, wd=wd, f0=f0, nf=nf, c0=c0):
                            return [e.dma_start(out=wd[:, 0:nf, :], in_=w_fd_v[:, f0:f0 + nf, c0:c0 + 1024])]

                        S.dma("pool", fn, writes=[Bwd], n=1)
                        for fi in range(nf):
                            f = f0 + fi
                            for tt in range(4):
                                for cg in range(2):
                                    S.op("pe", MM(pss[tt * 2 + cg][:], hid[:, f, tt * 128:(tt + 1) * 128], wd[:, fi, cg * 512:(cg + 1) * 512],
                                                  start=(f == 0), stop=(f == FT - 1)),
                                         reads=[Bwd, B_hid[f]], writes=[B_ps[tt * 2 + cg]])
                    for tt in range(4):
                        for cg in range(2):
                            yb, Byb = y2b[ci % 2], B_y2b[ci % 2]
                            ci += 1
                            S.op("act", ACT(yb[:], pss[tt * 2 + cg][:], AF.Copy), reads=[B_ps[tt * 2 + cg]], writes=[Byb])
                            r0 = half * 512 + tt * 128
                            S.dma("sp", DMA(y2_s[r0:r0 + 128, c0 + cg * 512:c0 + (cg + 1) * 512], yb[:]), reads=[Byb])
                S.end_phase()

        with ExitStack() as es:
            gpf = sb("gpf_sb", [128, D], F32, es)
            B_gpf = Buf()
            S.dma("sp", DMA(gpf[:], gpf_d[:, :]), writes=[B_gpf])
            yt = [sb("yt%d" % i, [128, D], F32, es) for i in range(2)]
            xt = [sb("xt%d" % i, [128, D], F32, es) for i in range(2)]
            ot_ = [sb("ot%d" % i, [128, D], F32, es) for i in range(2)]
            junk = sb("junk", [128, D], BF16, es)
            st = sb("st5", [128, 4, NT], F32, es)
            B_yt, B_xt, B_ot = [Buf(), Buf()], [Buf(), Buf()], [Buf(), Buf()]
            B_junk = Buf()
            B_st = [Buf() for _ in range(NT)]
            for i in range(NT):
                j = i % 2
                rows = slice(i * 128, (i + 1) * 128)
                S.dma("sp", DMA(yt[j][:], y2_s[rows, :]), writes=[B_yt[j]])
                S.dma("sp", DMA(xt[j][:], x1_s[rows, :]), writes=[B_xt[j]])
                S.op("act", ACT(junk[:], yt[j][:], AF.Square, accum_out=st[:, 0, i:i + 1]), reads=[B_yt[j]], writes=[B_junk, B_st[i]])
                S.op("dve", TS(st[:, 1, i:i + 1], st[:, 0, i:i + 1], 1.0 / D, EPS, ALU.mult, ALU.add), reads=[B_st[i]], writes=[B_st[i]])
                S.op("act", ACT(st[:, 2, i:i + 1], st[:, 1, i:i + 1], AF.Sqrt), reads=[B_st[i]], writes=[B_st[i]])
                S.op("dve", RCP(st[:, 3, i:i + 1], st[:, 2, i:i + 1]), reads=[B_st[i]], writes=[B_st[i]])
                S.op("dve", STT(ot_[j][:], yt[j][:], st[:, 3, i:i + 1], gpf[:], ALU.mult, ALU.mult),
                     reads=[B_yt[j], B_st[i], B_gpf], writes=[B_ot[j]])
                S.op("dve", TT(ot_[j][:], ot_[j][:], xt[j][:], ALU.add), reads=[B_ot[j], B_xt[j]], writes=[B_ot[j]])
                S.dma("sp", DMA(out_d[rows, :], ot_[j][:]), reads=[B_ot[j]])
            S.end_phase()


def _consts():
    cb = np.zeros((128, C_END), np.float32)
    idx = np.arange(128)
    cb[:, C_ID:C_ID + 128] = np.eye(128, dtype=np.float32)
    perm = np.zeros((128, 128), np.float32)
    perm[(idx + 64) % 128, idx] = 1.0
    cb[:, C_PERM:C_PERM + 128] = perm
    le = (idx[:, None] <= idx[None, :]).astype(np.float32)
    gt = (idx[:, None] > idx[None, :]).astype(np.float32)
    cb[:, C_TRIU:C_TRIU + 128] = le * (-1.0 / 16.0)
    cb[:, C_TRISL:C_TRISL + 128] = gt * (-1.0 / 16.0)
    cb[:, C_MUT:C_MUT + 128] = le
    cm = np.zeros((128, 2, 256), np.float32)
    q = np.arange(256)
    for a in range(2):
        cm[:, a, :] = ((a * 128 + idx)[:, None] <= q[None, :]).astype(np.float32)
    cb[:, C_CM:C_CM + 512] = cm.reshape(128, 512)
    cb[:, C_NEG:C_NEG + 2] = -1.0 / 16.0
    half = 64
    inv_freq = (10000.0 ** (-np.arange(half, dtype=np.float32) / half)).astype(np.float32)
    ang = np.arange(2048, dtype=np.float32)[None, :] * inv_freq[:, None]
    cos = np.cos(ang).astype(np.float32)
    sin = np.sin(ang).astype(np.float32)
    cosT = np.concatenate([cos, cos], axis=0)
    sinT = np.concatenate([-sin, sin], axis=0)
    return cb, np.ascontiguousarray(cosT), np.ascontiguousarray(sinT)


def _gbias(odd):
    gb = np.full((128, 8, 8), NEG, np.float32)
    for qt in range(8):
        jb = qt // 2
        if odd:
            gb[:, qt, 0:4] = 0.0
        gb[:, qt, 4:4 + jb] = 0.0
    return gb.reshape(128, 64)


_NC_CACHE = {}


def _in_maps(x, pre_mix_norm_g, w_in, gla_gate_up, gla_gate_bias, gla_out_norm_g,
             w_branch_moba, w_branch_gla, w_out, post_mix_norm_g, pre_ffn_norm_g,
             w_ffn_gate, w_ffn_up, w_ffn_down, post_ffn_norm_g):
    f = lambda a: np.ascontiguousarray(np.asarray(a, dtype=np.float32))
    x = f(x)
    cb, cosT, sinT = _consts()
    gvec = np.concatenate([f(pre_mix_norm_g)[0].reshape(32, 128).T, f(pre_ffn_norm_g)[0].reshape(32, 128).T], axis=1)
    common = {
        "w_in": f(w_in)[0], "w_bm": f(w_branch_moba)[0], "w_bg": f(w_branch_gla)[0], "w_out": f(w_out)[0],
        "w_fg": f(w_ffn_gate)[0], "w_fu": f(w_ffn_up)[0], "w_fd": f(w_ffn_down)[0],
        "gup": f(gla_gate_up)[0], "gvec": np.ascontiguousarray(gvec),
        "glab": np.ascontiguousarray(np.broadcast_to(f(gla_gate_bias)[0][None, :], (128, 2048))),
        "g512": np.ascontiguousarray(np.broadcast_to(f(gla_out_norm_g)[0][None, :], (128, 512))),
        "gpm": np.ascontiguousarray(np.broadcast_to(f(post_mix_norm_g)[0][None, :], (128, D))),
        "gpf": np.ascontiguousarray(np.broadcast_to(f(post_ffn_norm_g)[0][None, :], (128, D))),
        "cosT": cosT, "sinT": sinT, "cblob": cb,
    }
    zeros = np.zeros((TOK, D), np.float32)
    in_maps = []
    for c in range(8):
        b, hf = c // 2, c % 2
        m = dict(common)
        m["x_own"] = np.ascontiguousarray(x[b, hf * TOK:(hf + 1) * TOK])
        m["x_pre"] = np.ascontiguousarray(x[b, 0:TOK]) if hf == 1 else zeros
        m["gbias"] = _gbias(hf == 1)
        in_maps.append(m)
    return in_maps


def kernel(**inputs):
    if "nc" not in _NC_CACHE:
        _NC_CACHE["nc"] = build_program()
    nc = _NC_CACHE["nc"]
    in_maps = _in_maps(**inputs)
    res = run_bass_kernel_spmd(nc, in_maps, core_ids=list(range(8)))
    out = np.empty((4, 2048, D), np.float32)
    for c in range(8):
        b, hf = c // 2, c % 2
        out[b, hf * TOK:(hf + 1) * TOK] = res.results[c]["out"]
    return out
```

```python
from contextlib import ExitStack
import numpy as np
import concourse.bass as bass
import concourse.mybir as mybir
from concourse.bass_utils import run_bass_kernel_spmd

F32 = mybir.dt.float32
BF16 = mybir.dt.bfloat16
AF = mybir.ActivationFunctionType
ALU = mybir.AluOpType
AX = mybir.AxisListType

D = 4096
KT = 32
TOK = 1024
NT = 8
DFF = 11008
FT = 86
EPS = 1e-6
OQ, OK_, OV, OGQ, OGK, OGV, OGR, OGA, OGM, OGG = 0, 2048, 4096, 6144, 8192, 10240, 14336, 18432, 18448, 22544
INC = 26640
NEG = -3.0e38
C_ID, C_PERM, C_TRIU, C_TRISL, C_MUT, C_CM, C_NEG, C_END = 0, 128, 256, 384, 512, 640, 1152, 1154

STOP_AFTER = None
DEBUG_OUT = ()
SHRINK = ()


class StopBuild(Exception):
    pass


class Buf:
    __slots__ = ("name", "writer", "readers", "tok")

    def __init__(self, name="", tok=None):
        self.name = name
        self.writer = None
        self.readers = {}
        self.tok = tok


def PB(name=""):
    return Buf(name, tok=Buf(name + "_tok"))


class Rec:
    __slots__ = ("eng", "fn", "deps", "signal", "val", "sem", "is_dma", "phase", "n")

    def __init__(self, eng, fn, phase):
        self.eng = eng
        self.fn = fn
        self.deps = []
        self.signal = False
        self.val = None
        self.sem = None
        self.is_dma = False
        self.phase = phase
        self.n = 1


class Slot:
    __slots__ = ("sem", "count", "last")

    def __init__(self, sem):
        self.sem = sem
        self.count = 0
        self.last = None


ENGS = ("pe", "act", "dve", "pool", "sp")


class Sched:
    def __init__(self, nc, esem, sp_sems, pool_sems):
        self.nc = nc
        self.esem = esem
        self.cnt = {e: 0 for e in ENGS}
        self.dpool = {"sp": [Slot(s) for s in sp_sems], "pool": [Slot(s) for s in pool_sems]}
        self.dnext = {"sp": 0, "pool": 0}
        self.waited = {e: {} for e in ENGS}
        self.phase = 0
        self.q = {e: [] for e in ENGS}
        self.ninst = 0

    def _deps(self, r, reads, writes):
        deps = {}

        def add(d, kind):
            if d is None or d is r or d.phase != self.phase:
                return
            if (not d.is_dma) and (not r.is_dma) and d.eng == r.eng:
                if r.eng == "pe":
                    return
            deps[id(d)] = d

        for b in reads:
            add(b.writer, "w")
        for b in writes:
            add(b.writer, "w")
            for rd in b.readers.values():
                add(rd, "r")
        for d in deps.values():
            r.deps.append(d)
            if not d.is_dma:
                d.signal = True
        for b in reads:
            b.readers[("dma", id(r)) if r.is_dma else r.eng] = r
        for b in writes:
            b.writer = r
            b.readers = {}

    def op(self, eng, fn, reads=(), writes=()):
        toks = []
        for b in list(reads) + list(writes):
            if b.tok is not None and b.tok not in toks:
                toks.append(b.tok)
        if toks:
            writes = list(writes) + toks
        r = Rec(eng, fn, self.phase)
        self._deps(r, reads, writes)
        self.q[eng].append(r)
        return r

    def dma(self, queue, fn, reads=(), writes=(), n=1):
        pool = self.dpool[queue]
        slot = pool[self.dnext[queue] % len(pool)]
        self.dnext[queue] += 1
        r = Rec(queue, fn, self.phase)
        r.is_dma = True
        r.sem = slot.sem
        r.n = n
        if slot.last is not None and slot.last.phase == self.phase:
            r.deps.append(slot.last)
        slot.count += 16 * n
        r.val = slot.count
        slot.last = r
        self._deps(r, reads, writes)
        self.q[queue].append(r)
        return r

    def end_phase(self):
        deps = []
        for e in ("pe", "act", "dve", "pool"):
            for r in reversed(self.q[e]):
                if r.fn is not None:
                    deps.append(r)
                    break
        for queue in self.dpool:
            for slot in self.dpool[queue]:
                if slot.last is not None and slot.last.phase == self.phase:
                    deps.append(slot.last)
        b = Rec("sp", lambda e: e.nop(), self.phase)
        seen = set()
        for d in deps:
            if id(d) in seen:
                continue
            seen.add(id(d))
            b.deps.append(d)
            if not d.is_dma:
                d.signal = True
        b.signal = True
        self.q["sp"].append(b)
        for e in ("pe", "act", "dve", "pool"):
            rr = Rec(e, None, self.phase)
            rr.deps = [b]
            self.q[e].append(rr)
        for e in ENGS:
            for r in self.q[e]:
                if (not r.is_dma) and r.signal:
                    self.cnt[e] += 1
                    r.val = self.cnt[e]
        nc = self.nc
        with nc.Block() as blk:
            for e, reg in (("pe", blk.tensor), ("act", blk.scalar), ("dve", blk.vector),
                           ("pool", blk.gpsimd), ("sp", blk.sync)):
                lst = self.q[e]
                waited = self.waited[e]
                esem = self.esem

                def body(eng, e=e, lst=lst, waited=waited):
                    for r in lst:
                        for d in r.deps:
                            sem = d.sem if d.is_dma else esem[d.eng]
                            val = d.val
                            assert val is not None
                            key = id(sem)
                            if waited.get(key, 0) >= val:
                                continue
                            eng.wait_ge(sem, val)
                            waited[key] = val
                        if r.fn is None:
                            continue
                        ins = r.fn(eng)
                        self.ninst += 1
                        if r.is_dma:
                            if isinstance(ins, (list, tuple)):
                                assert len(ins) == r.n
                                for i_ in ins:
                                    i_.then_inc(r.sem, 16)
                            else:
                                assert r.n == 1
                                ins.then_inc(r.sem, 16)
                        elif r.signal:
                            ins.then_inc(esem[e], 1)

                reg(body)
        self.q = {e: [] for e in ENGS}
        self.phase += 1
        build_program.ninst = self.ninst
        if STOP_AFTER is not None and self.phase > STOP_AFTER:
            raise StopBuild()


def MM(out, lhsT, rhs, start=True, stop=True):
    return lambda e: e.matmul(out, lhsT, rhs, start=start, stop=stop)


def TR(out, in_, ident):
    return lambda e: e.transpose(out, in_, ident)


def ACT(out, in_, func, bias=0.0, scale=1.0, accum_out=None):
    if accum_out is None:
        return lambda e: e.activation(out=out, in_=in_, func=func, bias=bias, scale=scale)
    return lambda e: e.activation(out=out, in_=in_, func=func, bias=bias, scale=scale, accum_out=accum_out)


def TT(out, a, b, op):
    return lambda e: e.tensor_tensor(out=out, in0=a, in1=b, op=op)


def TS(out, a, s1, s2, op0, op1=None):
    if op1 is None:
        return lambda e: e.tensor_scalar(out=out, in0=a, scalar1=s1, scalar2=None, op0=op0)
    return lambda e: e.tensor_scalar(out=out, in0=a, scalar1=s1, scalar2=s2, op0=op0, op1=op1)


def STT(out, a, s, b, op0, op1):
    return lambda e: e.scalar_tensor_tensor(out=out, in0=a, scalar=s, in1=b, op0=op0, op1=op1)


def CP(out, in_):
    return lambda e: e.tensor_copy(out=out, in_=in_)


def RCP(out, in_):
    return lambda e: e.reciprocal(out=out, in_=in_)


def DMA(out, in_):
    return lambda e: e.dma_start(out=out, in_=in_)


def MSET(ap, v):
    return lambda e: e.memset(ap, v)


def build_program():
    nc = bass.Bass("TRN2", target_bir_lowering=False)
    try:
        _build(nc)
    except StopBuild:
        pass
    return nc


def _build(nc):

    def din(name, shape, dt=F32):
        if name in SHRINK:
            t = nc.dram_tensor(name + "_tiny", [128, 128], dt, kind="ExternalInput")
            return nc.dram_tensor(name + "_fake", list(shape), dt, kind="Internal").ap()
        return nc.dram_tensor(name, list(shape), dt, kind="ExternalInput").ap()

    def dscr(name, shape, dt):
        kind = "ExternalOutput" if name in DEBUG_OUT else "Internal"
        return nc.dram_tensor(name, list(shape), dt, kind=kind).ap()

    x_own = din("x_own", [TOK, D])
    x_pre = din("x_pre", [TOK, D])
    w_in = din("w_in", [D, INC])
    w_bm = din("w_bm", [2048, D])
    w_bg = din("w_bg", [D, D])
    w_out = din("w_out", [D, D])
    w_fg = din("w_fg", [D, DFF])
    w_fu = din("w_fu", [D, DFF])
    w_fd = din("w_fd", [DFF, D])
    gup_d = din("gup", [16, 2048])
    gvec = din("gvec", [128, 64])
    gbias_d = din("gbias", [128, 64])
    glab_d = din("glab", [128, 2048])
    g512_d = din("g512", [128, 512])
    gpm_d = din("gpm", [128, D])
    gpf_d = din("gpf", [128, D])
    cosT_d = din("cosT", [128, 2048])
    sinT_d = din("sinT", [128, 2048])
    cb_d = din("cblob", [128, C_END])
    out_d = nc.dram_tensor("out", [TOK, D], F32, kind="ExternalOutput").ap()

    kt_s = dscr("kt_s", [16, 128, 1024], BF16)
    v_s = dscr("v_s", [8, 128, 8 * 2 * 129], BF16)
    moT_s = dscr("moT_s", [16, 128, 1024], BF16)
    goT_s = dscr("goT_s", [32, 128, 1024], BF16)
    sgm_s = dscr("sgm_s", [32, 128, 1024], BF16)
    sgg_s = dscr("sgg_s", [32, 128, 1024], BF16)
    y_s = dscr("y_s", [TOK, D], F32)
    x1_s = dscr("x1_s", [TOK, D], F32)
    y2_s = dscr("y2_s", [TOK, D], F32)
    st_s = dscr("st_s", [8, 128, 1024], F32)
    h2T_s = dscr("h2T_s", [2, 128, KT, 512], BF16)

    def wview(W):
        return W.rearrange("(kt p) n -> p kt n", p=128)

    w_in_v, w_bm_v, w_bg_v, w_out_v = wview(w_in), wview(w_bm), wview(w_bg), wview(w_out)
    w_fg_v, w_fu_v, w_fd_v = wview(w_fg), wview(w_fu), wview(w_fd)

    with ExitStack() as G:
        uid = [0]

        def sb(name, shape, dt, es=G):
            uid[0] += 1
            return es.enter_context(nc.sbuf_tensor("%s_s%d" % (name, uid[0]), list(shape), dt))

        def psum(name, shape, dt, es):
            uid[0] += 1
            return es.enter_context(nc.psum_tensor("%s_p%d" % (name, uid[0]), list(shape), dt))

        nsp, npool = 40, 8
        esem = {e: G.enter_context(nc.semaphore("e_" + e)) for e in ENGS}
        sp_sems = [G.enter_context(nc.semaphore("dsp%d" % i)) for i in range(nsp)]
        pool_sems = [G.enter_context(nc.semaphore("dpl%d" % i)) for i in range(npool)]
        S = Sched(nc, esem, sp_sems, pool_sems)

        cb = sb("cb", [128, C_END], F32)
        identb = sb("identb", [128, 128], BF16)
        permb = sb("permb", [128, 128], BF16)
        mutb = sb("mutb", [128, 128], BF16)
        cmb = sb("cmb", [128, 512], BF16)
        gv = sb("gv", [128, 64], F32)
        gbias = sb("gbias_sb", [128, 64], F32)
        ksum = sb("ksum", [128, 16, 8], F32)
        wbuf = []
        B_w = []
        B_cb, B_gv, B_gbias = Buf("cb"), Buf("gv"), Buf("gbias")
        B_ksum = [Buf("ksum%d" % h) for h in range(16)]
        wn = [0]

        def alloc_w(es, n, size=8192):
            wbuf[:] = [sb("wbuf%d" % i, [128, size], BF16, es) for i in range(n)]
            B_w[:] = [Buf("w%d" % i) for i in range(n)]
            wn[0] = 0
        identf = cb[:, C_ID:C_ID + 128]
        triU = cb[:, C_TRIU:C_TRIU + 128]
        triSL = cb[:, C_TRISL:C_TRISL + 128]
        negcol = cb[:, C_NEG:C_NEG + 2]

        def wload(Wv, nk, col0, ncols, sub=0, width=None):
            i = wn[0] % len(wbuf)
            wn[0] += 1
            width = width or ncols
            view = wbuf[i][:, 0:nk * width].rearrange("p (k c) -> p k c", c=width)
            step = 8

            def fn(e, view=view, Wv=Wv):
                return [e.dma_start(out=view[:, k0:min(k0 + step, nk), sub:sub + ncols],
                                    in_=Wv[:, k0:min(k0 + step, nk), col0:col0 + ncols])
                        for k0 in range(0, nk, step)]

            S.dma("pool", fn, writes=[B_w[i]], n=(nk + step - 1) // step)
            return view, B_w[i]

        def wload2(Wv, nk, cols_a, cols_b, n_each):
            i = wn[0] % len(wbuf)
            wn[0] += 1
            width = 2 * n_each
            view = wbuf[i][:, 0:nk * width].rearrange("p (k c) -> p k c", c=width)
            step = 8

            def fn(e, view=view, Wv=Wv):
                ins = []
                for k0 in range(0, nk, step):
                    ins.append(e.dma_start(out=view[:, k0:k0 + step, 0:n_each], in_=Wv[:, k0:k0 + step, cols_a:cols_a + n_each]))
                    ins.append(e.dma_start(out=view[:, k0:k0 + step, n_each:width], in_=Wv[:, k0:k0 + step, cols_b:cols_b + n_each]))
                return ins

            S.dma("pool", fn, writes=[B_w[i]], n=2 * (nk // step))
            return view, B_w[i]

        S.dma("sp", DMA(cb[:], cb_d[:, :]), writes=[B_cb])
        S.dma("sp", DMA(gv[:], gvec[:, :]), writes=[B_gv])
        S.dma("sp", DMA(gbias[:], gbias_d[:, :]), writes=[B_gbias])
        S.op("dve", CP(identb[:], cb[:, C_ID:C_ID + 128]), reads=[B_cb], writes=[B_cb])
        S.op("dve", CP(permb[:], cb[:, C_PERM:C_PERM + 128]), reads=[B_cb], writes=[B_cb])
        S.op("dve", CP(mutb[:], cb[:, C_MUT:C_MUT + 128]), reads=[B_cb], writes=[B_cb])
        S.op("dve", CP(cmb[:], cb[:, C_CM:C_CM + 512]), reads=[B_cb], writes=[B_cb])
        S.end_phase()

        def norm_transpose(es, src, hT, B_hT, goff, ps_t, B_pst, x1_dst=None):
            xt = [sb("xt%d" % i, [128, D], F32, es) for i in range(2)]
            xn = [sb("xn%d" % i, [128, D], BF16, es) for i in range(2)]
            st = sb("nt_st", [128, 4, NT], F32, es)
            B_xt = [Buf(), Buf()]
            B_xn = [Buf(), Buf()]
            B_st = [Buf() for _ in range(NT)]
            for i in range(NT):
                j = i % 2
                S.dma("sp", DMA(xt[j][:], src[i * 128:(i + 1) * 128, :]), writes=[B_xt[j]])
                S.op("act", ACT(xn[j][:], xt[j][:], AF.Square, accum_out=st[:, 0, i:i + 1]),
                     reads=[B_xt[j]], writes=[B_xn[j], B_st[i]])
                S.op("dve", TS(st[:, 1, i:i + 1], st[:, 0, i:i + 1], 1.0 / D, EPS, ALU.mult, ALU.add),
                     reads=[B_st[i]], writes=[B_st[i]])
                S.op("act", ACT(st[:, 2, i:i + 1], st[:, 1, i:i + 1], AF.Sqrt), reads=[B_st[i]], writes=[B_st[i]])
                S.op("dve", RCP(st[:, 3, i:i + 1], st[:, 2, i:i + 1]), reads=[B_st[i]], writes=[B_st[i]])
                S.op("act", ACT(xn[j][:], xt[j][:], AF.Copy, scale=st[:, 3, i:i + 1]),
                     reads=[B_xt[j], B_st[i]], writes=[B_xn[j]])
                for g in range(4):
                    pj = g % 2
                    for c in range(8):
                        k = g * 8 + c
                        S.op("pe", TR(ps_t[pj][:, c * 128:(c + 1) * 128], xn[j][:, k * 128:(k + 1) * 128], identb[:]),
                             reads=[B_xn[j], B_cb], writes=[B_pst[pj]])
                    S.op("dve", TT(hT[:, g * 8:(g + 1) * 8, i * 128:(i + 1) * 128],
                                   ps_t[pj][:].rearrange("p (a b) -> p a b", b=128),
                                   gv[:, goff + g * 8:goff + (g + 1) * 8].unsqueeze(2).to_broadcast([128, 8, 128]),
                                   ALU.mult),
                         reads=[B_pst[pj], B_gv], writes=[B_hT[i]])

        def rope_evict(ps, B_ps, dst, B_dst, pos0, ps_r, B_psr, tmp, B_tmp, cosT, sinT, B_rope):
            xb, ta, tb = tmp
            S.op("act", ACT(xb[:], ps[:], AF.Copy), reads=[B_ps], writes=[B_tmp[0]])
            S.op("pe", MM(ps_r[:], permb[:], xb[:]), reads=[B_tmp[0], B_cb], writes=[B_psr])
            S.op("dve", TT(ta[:], ps[:], cosT[:, pos0:pos0 + 512], ALU.mult), reads=[B_ps, B_rope], writes=[B_tmp[1]])
            S.op("dve", TT(tb[:], ps_r[:], sinT[:, pos0:pos0 + 512], ALU.mult), reads=[B_psr, B_rope], writes=[B_tmp[2]])
            S.op("dve", TT(dst, ta[:], tb[:], ALU.add), reads=[B_tmp[1], B_tmp[2]], writes=[B_dst])

        def run_pass(own):
            with ExitStack() as es:
                src = x_own if own else x_pre
                alloc_w(es, 3)
                hT = sb("hT", [128, KT, TOK], BF16, es)
                B_hT = [Buf("hT%d" % i) for i in range(NT)]
                pss = [psum("ps%d" % i, [128, 512], F32, es) for i in range(7)]
                B_ps = [PB("ps%d" % i) for i in range(7)]
                ps_t = psum("pst", [128, 1024], BF16, es)
                B_pst1 = PB("pst")
                with ExitStack() as es2:
                    norm_transpose(es2, src, hT, B_hT, 0, [ps_t, ps_t], [B_pst1, B_pst1])
                if "hT_dbg" in DEBUG_OUT and own:
                    hT_dbg = nc.dram_tensor("hT_dbg", [128, KT, TOK], BF16, kind="ExternalOutput").ap()
                    S.dma("sp", DMA(hT_dbg[:, :, :], hT[:]), reads=B_hT)
                S.end_phase()
                pos_base = 1024 if own else 0
                ntg = 2

                with ExitStack() as es2:
                    cosT = sb("cosT", [128, 2048], F32, es2)
                    sinT = sb("sinT", [128, 2048], F32, es2)
                    B_rope = Buf("rope")
                    S.dma("sp", DMA(cosT[:], cosT_d[:, :]), writes=[B_rope])
                    S.dma("sp", DMA(sinT[:], sinT_d[:, :]), reads=[B_rope], writes=[B_rope])
                    nset = 2 if own else 1
                    KTs = [sb("KTb", [128, 2, 2048], BF16, es2) for _ in range(nset)]
                    Vs = [sb("Vb", [128, 16, 2, 129], BF16, es2) for _ in range(nset)]
                    QTs = [sb("QTb", [128, 2, 1024], BF16, es2) for _ in range(nset)]
                    xb = sb("r_xb", [128, 512], BF16, es2)
                    ta = sb("r_ta", [128, 512], F32, es2)
                    tb = sb("r_tb", [128, 512], F32, es2)
                    B_tmp = [Buf(), Buf(), Buf()]
                    B_KTs = [[[Buf() for _ in range(4)] for _ in range(2)] for _ in range(nset)]
                    B_Vs = [[Buf() for _ in range(16)] for _ in range(nset)]
                    B_QTs = [[[Buf() for _ in range(2)] for _ in range(2)] for _ in range(nset)]
                    for s_ in range(nset):
                        S.op("dve", MSET(Vs[s_][:, :, :, 128:129], 1.0), writes=B_Vs[s_])
                    if own:
                        ksb = sb("ksb", [128, 8], BF16, es2)
                        Gb = sb("Gb", [128, 8, 8], F32, es2)
                        top8 = sb("top8", [128, 8, 8], F32, es2)
                        thr = sb("thr", [128, 8], F32, es2)
                        sel = sb("sel", [128, 8, 8], F32, es2)
                        PT = [sb("PT%d" % i, [128, 2, 256], BF16, es2) for i in range(2)]
                        Oacc = sb("Oacc", [128, 8, 129], F32, es2)
                        rec = sb("rec", [128, 8], F32, es2)
                        mo_tok = sb("mo_tok", [128, 8, 256], BF16, es2)
                        moT_g = sb("moT_g", [128, 2, 1024], BF16, es2)
                        B_ksb, B_Gb, B_top8, B_thr, B_sel = Buf(), Buf(), Buf(), Buf(), Buf()
                        B_PT = [Buf(), Buf()]
                        B_Oacc = [Buf() for _ in range(8)]
                        B_rec = Buf()
                        B_mo = [Buf() for _ in range(8)]
                        B_moTg = Buf()

                    def proj_gen(g, s_):
                        KTb, Vb, QTb = KTs[s_], Vs[s_], QTs[s_]
                        B_KT, B_V, B_QT = B_KTs[s_], B_Vs[s_], B_QTs[s_]
                        if own:
                            for hl in range(2):
                                S.dma("sp", DMA(KTb[:, hl, 0:1024], kt_s[2 * g + hl, :, :]), writes=[B_KT[hl][0], B_KT[hl][1]])
                            S.dma("sp", DMA(Vb[:, 0:8, :, :].rearrange("p a b c -> p (a b c)"), v_s[g, :, :]), writes=B_V[0:8])
                        wk, Bwk = wload(w_in_v, KT, OK_ + g * 256, 256)
                        for hl in range(2):
                            for tg in range(ntg):
                                ps = pss[tg % 2]
                                Bp = B_ps[tg % 2]
                                for k in range(KT):
                                    S.op("pe", MM(ps[:], wk[:, k, hl * 128:(hl + 1) * 128], hT[:, k, tg * 512:(tg + 1) * 512],
                                                  start=(k == 0), stop=(k == KT - 1)),
                                         reads=[Bwk] + B_hT[tg * 4:(tg + 1) * 4], writes=[Bp])
                                    if k % 8 == 7 and k != KT - 1:
                                        yield
                                gi = (2 if own else 0) + tg
                                rope_evict(ps, Bp, KTb[:, hl, gi * 512:(gi + 1) * 512], B_KT[hl][gi], pos_base + tg * 512,
                                           pss[2], B_ps[2], (xb, ta, tb), B_tmp, cosT, sinT, B_rope)
                                yield
                            b0 = 4 if own else 0
                            S.op("dve", lambda e, hl=hl, b0=b0, g=g, KTb=KTb: e.tensor_reduce(
                                out=ksum[:, 2 * g + hl, b0:b0 + 4],
                                in_=KTb[:, hl, b0 * 256:(b0 + 4) * 256].rearrange("p (b t) -> p b t", t=256),
                                axis=AX.X, op=ALU.add),
                                 reads=B_KT[hl][(2 if own else 0):(4 if own else 2)], writes=[B_ksum[2 * g + hl]])
                        wv, Bwv = wload(w_in_v, KT, OV + g * 256, 256)
                        for i in range(NT):
                            ps = pss[i % 2]
                            Bp = B_ps[i % 2]
                            for k in range(KT):
                                S.op("pe", MM(ps[:, 0:256], hT[:, k, i * 128:(i + 1) * 128], wv[:, k, :],
                                              start=(k == 0), stop=(k == KT - 1)),
                                     reads=[Bwv, B_hT[i]], writes=[Bp])
                                if k == 15:
                                    yield
                            ti = (8 if own else 0) + i
                            S.op("act", ACT(Vb[:, ti, :, 0:128], ps[:, 0:256].rearrange("p (a b) -> p a b", b=128), AF.Copy),
                                 reads=[Bp], writes=[B_V[ti]])
                            yield
                        if not own:
                            for hl in range(2):
                                S.dma("sp", DMA(kt_s[2 * g + hl, :, :], KTb[:, hl, 0:1024]), reads=[B_KT[hl][0], B_KT[hl][1]])
                            S.dma("sp", DMA(v_s[g, :, :], Vb[:, 0:8, :, :].rearrange("p a b c -> p (a b c)")), reads=B_V[0:8])
                            return
                        wq, Bwq = wload(w_in_v, KT, OQ + g * 256, 256)
                        for hl in range(2):
                            for tg in range(ntg):
                                ps = pss[tg % 2]
                                Bp = B_ps[tg % 2]
                                for k in range(KT):
                                    S.op("pe", MM(ps[:], wq[:, k, hl * 128:(hl + 1) * 128], hT[:, k, tg * 512:(tg + 1) * 512],
                                                  start=(k == 0), stop=(k == KT - 1)),
                                         reads=[Bwq] + B_hT[tg * 4:(tg + 1) * 4], writes=[Bp])
                                    if k % 8 == 7 and k != KT - 1:
                                        yield
                                rope_evict(ps, Bp, QTb[:, hl, tg * 512:(tg + 1) * 512], B_QT[hl][tg], pos_base + tg * 512,
                                           pss[2], B_ps[2], (xb, ta, tb), B_tmp, cosT, sinT, B_rope)
                                yield

                    def attn_gen(g, s_):
                        KTb, Vb, QTb = KTs[s_], Vs[s_], QTs[s_]
                        B_KT, B_V, B_QT = B_KTs[s_], B_Vs[s_], B_QTs[s_]
                        for hl in range(2):
                            h = 2 * g + hl
                            S.op("act", ACT(ksb[:], ksum[:, h, :], AF.Copy), reads=[B_ksum[h]], writes=[B_ksb])
                            gps = pss[2]
                            for qt in range(8):
                                S.op("pe", MM(gps[:, qt * 8:(qt + 1) * 8], QTb[:, hl, qt * 128:(qt + 1) * 128], ksb[:]),
                                     reads=[B_QT[hl][qt // 4], B_ksb], writes=[B_ps[2]])
                            S.op("dve", TT(Gb[:].rearrange("p a b -> p (a b)"), gps[:, 0:64], gbias[:], ALU.add),
                                 reads=[B_ps[2], B_gbias], writes=[B_Gb])
                            for qt in range(8):
                                S.op("dve", lambda e, qt=qt: e.max(out=top8[:, qt, :], in_=Gb[:, qt, :]),
                                     reads=[B_Gb], writes=[B_top8])
                            S.op("dve", TS(thr[:], top8[:, :, 2], -1.0e30, None, ALU.max), reads=[B_top8], writes=[B_thr])
                            S.op("dve", TT(sel[:], Gb[:], thr[:].unsqueeze(2).to_broadcast([128, 8, 8]), ALU.is_ge),
                                 reads=[B_Gb, B_thr], writes=[B_sel])
                            yield
                            step = 0
                            for jb in range(4):
                                for n in range(5 + jb):
                                    ownb = (n == 4 + jb)
                                    sps = pss[3 + step % 2]
                                    Bsp = B_ps[3 + step % 2]
                                    pt = PT[step % 2]
                                    Bpt = B_PT[step % 2]
                                    step += 1
                                    for a in range(2):
                                        kti = 2 * n + a
                                        S.op("pe", MM(sps[:, a * 256:(a + 1) * 256], KTb[:, hl, kti * 128:(kti + 1) * 128],
                                                      QTb[:, hl, jb * 256:(jb + 1) * 256]),
                                             reads=[B_KT[hl][kti // 4], B_QT[hl][jb // 2]], writes=[Bsp])
                                    S.op("act", ACT(pt[:].rearrange("p a b -> p (a b)"), sps[:], AF.Exp, scale=128.0 ** -0.5),
                                         reads=[Bsp], writes=[Bpt])
                                    if ownb:
                                        S.op("dve", TT(pt[:].rearrange("p a b -> p (a b)"), pt[:].rearrange("p a b -> p (a b)"),
                                                       cmb[:], ALU.mult), reads=[Bpt, B_cb], writes=[Bpt])
                                    yield "mid"
                                    for b in range(2):
                                        qt = jb * 2 + b
                                        alist = [0] if (ownb and b == 0) else [0, 1]
                                        for ai, a in enumerate(alist):
                                            S.op("pe", MM(pss[5 + b][:, 0:129], pt[:, a, b * 128:(b + 1) * 128],
                                                          Vb[:, 2 * n + a, hl, :], start=(ai == 0), stop=(ai == len(alist) - 1)),
                                                 reads=[Bpt, B_V[2 * n + a]], writes=[B_ps[5 + b]])
                                        Bop = B_ps[5 + b]
                                        o_ps = pss[5 + b][:, 0:129]
                                        if n == 0:
                                            S.op("dve", TS(Oacc[:, qt, :], o_ps, sel[:, qt, 0:1], None, ALU.mult),
                                                 reads=[Bop, B_sel], writes=[B_Oacc[qt]])
                                        elif ownb:
                                            S.op("dve", TT(Oacc[:, qt, :], o_ps, Oacc[:, qt, :], ALU.add),
                                                 reads=[Bop, B_Oacc[qt]], writes=[B_Oacc[qt]])
                                        else:
                                            S.op("dve", STT(Oacc[:, qt, :], o_ps, sel[:, qt, n:n + 1], Oacc[:, qt, :], ALU.mult, ALU.add),
                                                 reads=[Bop, B_sel, B_Oacc[qt]], writes=[B_Oacc[qt]])
                                    yield
                            S.op("dve", RCP(rec[:], Oacc[:, :, 128]), reads=B_Oacc, writes=[B_rec])
                            for qt in range(8):
                                S.op("dve", TS(mo_tok[:, qt, hl * 128:(hl + 1) * 128], Oacc[:, qt, 0:128], rec[:, qt:qt + 1], None, ALU.mult),
                                     reads=[B_Oacc[qt], B_rec], writes=[B_mo[qt]])
                        for qt in range(8):
                            for hl in range(2):
                                S.op("pe", TR(ps_t[:, hl * 128:(hl + 1) * 128], mo_tok[:, qt, hl * 128:(hl + 1) * 128], identb[:]),
                                     reads=[B_mo[qt], B_cb], writes=[B_pst1])
                            S.op("act", ACT(moT_g[:, :, qt * 128:(qt + 1) * 128], ps_t[:, 0:256].rearrange("p (a b) -> p a b", b=128), AF.Copy),
                                 reads=[B_pst1], writes=[B_moTg])
                        for hl in range(2):
                            S.dma("sp", DMA(moT_s[2 * g + hl, :, :], moT_g[:, hl, :]), reads=[B_moTg])
                        yield

                    def drain(gen):
                        for _ in gen:
                            pass

                    if not own:
                        for g in range(8):
                            drain(proj_gen(g, 0))
                    else:
                        drain(proj_gen(0, 0))
                        for g in range(8):
                            A_ = attn_gen(g, g % 2)
                            P_ = proj_gen(g + 1, (g + 1) % 2) if g < 7 else iter(())
                            a_alive, p_alive = True, True
                            while a_alive or p_alive:
                                tag = None
                                if a_alive:
                                    try:
                                        tag = next(A_)
                                    except StopIteration:
                                        a_alive = False
                                if p_alive and (tag == "mid" or not a_alive):
                                    try:
                                        next(P_)
                                    except StopIteration:
                                        p_alive = False
                S.end_phase()

                with ExitStack() as es2:
                    gadT = sb("gadT", [16, TOK], F32, es2)
                    gup = sb("gup_sb", [16, 2048], F32, es2)
                    glabs = [sb("glab_sb", [128, 256], F32, es2) for _ in range(2)]
                    B_glabs = [Buf(), Buf()]
                    g512 = sb("g512_sb", [128, 512], F32, es2)
                    B_gad, B_gup, B_glab, B_g512 = Buf(), Buf(), Buf(), Buf()
                    S.dma("sp", DMA(gup[:], gup_d[:, :]), writes=[B_gup])
                    S.dma("sp", DMA(g512[:], g512_d[:, :]), writes=[B_g512])
                    wga, Bwga = wload(w_in_v, KT, OGA, 16)
                    for tg in range(2):
                        for k in range(KT):
                            S.op("pe", MM(pss[0][0:16, :], wga[:, k, 0:16], hT[:, k, tg * 512:(tg + 1) * 512],
                                          start=(k == 0), stop=(k == KT - 1)),
                                 reads=[Bwga] + B_hT[tg * 4:(tg + 1) * 4], writes=[B_ps[0]])
                        S.op("act", ACT(gadT[:, tg * 512:(tg + 1) * 512], pss[0][0:16, :], AF.Copy), reads=[B_ps[0]], writes=[B_gad])
                    zb = sb("zb", [128, 256], F32, es2)
                    e1 = sb("e1", [128, 256], F32, es2)
                    lg_all = sb("lg_all", [128, NT, 256], F32, es2)
                    dec = [sb("dec%d" % i, [128, 256], F32, es2) for i in range(2)]
                    ebl = sb("ebl", [128, 4], F32, es2)
                    qd_all = sb("qd_all", [128, NT, 256], BF16, es2)
                    ki_all = sb("ki_all", [128, NT, 256], BF16, es2)
                    ke_all = sb("ke_all", [128, NT, 256], BF16, es2)
                    vb_all = sb("vb_all", [128, NT, 512], BF16, es2)
                    sg_all = sb("sg_all", [128, NT, 512], BF16, es2)
                    qdT = sb("qdT", [128, 2, 128], BF16, es2)
                    kiT = sb("kiT", [128, 2, 128], BF16, es2)
                    At = sb("At", [128, 128], BF16, es2)
                    Sbf = sb("Sbf", [128, 2, 512], BF16, es2)
                    st_h = [sb("st_h%d" % i, [128, 2, 512], F32, es2) for i in range(2)]
                    gst = sb("gst", [128, 4], F32, es2)
                    on_all = sb("on_all", [128, NT, 512], BF16, es2)
                    ot = sb("ot", [128, 512], BF16, es2)
                    goT_h = sb("goT_h", [128, 4, TOK], BF16, es2)
                    (B_zb, B_e1, B_ebl, B_qdT, B_kiT, B_At, B_Sbf, B_gst, B_ot, B_junk2, B_goT) = [Buf() for _ in range(11)]
                    B_on = [Buf() for _ in range(NT)]
                    B_lg = [Buf() for _ in range(NT)]
                    B_qd = [Buf() for _ in range(NT)]
                    B_ki = [Buf() for _ in range(NT)]
                    B_ke = [Buf() for _ in range(NT)]
                    B_vb = [Buf() for _ in range(NT)]
                    B_sg = [Buf() for _ in range(NT)]
                    B_dec = [Buf(), Buf()]
                    B_sth = [Buf(), Buf()]
                    z_ps, bc_ps = pss[3][:, 0:256], pss[3][:, 256:512]
                    rv_ps, bl_ps, A_ps = pss[4][:, 0:256], pss[4][:, 256:260], pss[4][:, 264:392]
                    B_z, B_bc = Buf("z", B_ps[3].tok), Buf("bc", B_ps[3].tok)
                    B_rv, B_bl, B_A = Buf("rv", B_ps[4].tok), Buf("bl", B_ps[4].tok), Buf("A", B_ps[4].tok)
                    O_ps, kv_ps = pss[5], pss[6]
                    B_O, B_kv = B_ps[5], B_ps[6]
                    B_tq, B_to = Buf("tq", B_pst1.tok), Buf("to", B_pst1.tok)
                    dn = [0]

                    def nextdec():
                        j = dn[0] % 2
                        dn[0] += 1
                        return dec[j], B_dec[j]

                    pn = [0]

                    def proj(wv_, Bw_, i, ncol=256):
                        j = pn[0] % 3
                        pn[0] += 1
                        ps, Bp = pss[j], B_ps[j]
                        for k in range(KT):
                            S.op("pe", MM(ps[:, 0:ncol], hT[:, k, i * 128:(i + 1) * 128], wv_[:, k, :], start=(k == 0), stop=(k == KT - 1)),
                                 reads=[Bw_, B_hT[i]], writes=[Bp])
                        return ps[:, 0:ncol], Bp

                    for hh in range(8):
                        sth, Bsth = st_h[hh % 2], B_sth[hh % 2]
                        glab, B_glab = glabs[hh % 2], B_glabs[hh % 2]
                        S.dma("sp", DMA(glab[:], glab_d[:, hh * 256:(hh + 1) * 256]), writes=[B_glab])
                        if own:
                            S.dma("sp", DMA(sth[:].rearrange("p a b -> p (a b)"), st_s[hh, :, :]), writes=[Bsth])
                        else:
                            S.op("dve", MSET(sth[:], 0.0), writes=[Bsth])
                        for i in range(NT):
                            tok = slice(i * 128, (i + 1) * 128)
                            S.op("pe", MM(z_ps, gadT[0:16, tok], gup[0:16, hh * 256:(hh + 1) * 256]),
                                 reads=[B_gad, B_gup], writes=[B_z])
                            S.op("dve", TT(zb[:], z_ps, glab[:], ALU.add), reads=[B_z, B_glab], writes=[B_zb])
                            S.op("act", ACT(e1[:], zb[:], AF.Exp, scale=-1.0), reads=[B_zb], writes=[B_e1])
                            S.op("act", ACT(lg_all[:, i, :], e1[:], AF.Ln, bias=1.0), reads=[B_e1], writes=[B_lg[i]])
                        if own:
                            wq_, Bwq_ = wload(w_in_v, KT, OGQ + hh * 256, 256)
                            for i in range(NT):
                                S.op("pe", MM(bc_ps, triU, lg_all[:, i, :]), reads=[B_lg[i], B_cb], writes=[B_bc])
                                d_, Bd_ = nextdec()
                                S.op("act", ACT(d_[:], bc_ps, AF.Exp), reads=[B_bc], writes=[Bd_])
                                ps, Bp = proj(wq_, Bwq_, i)
                                S.op("dve", STT(qd_all[:, i, :], ps, 1.0 / 16.0, d_[:], ALU.mult, ALU.mult), reads=[Bp, Bd_], writes=[B_qd[i]])
                        wk_, Bwk_ = wload(w_in_v, KT, OGK + hh * 256, 256)
                        for i in range(NT):
                            S.op("pe", MM(rv_ps, triSL, lg_all[:, i, :]), reads=[B_lg[i], B_cb], writes=[B_rv])
                            d_, Bd_ = nextdec()
                            S.op("act", ACT(d_[:], rv_ps, AF.Exp), reads=[B_rv], writes=[Bd_])
                            if own:
                                S.op("pe", MM(bc_ps, triU, lg_all[:, i, :]), reads=[B_lg[i], B_cb], writes=[B_bc])
                                d2_, Bd2_ = nextdec()
                                S.op("act", ACT(d2_[:], bc_ps, AF.Exp, scale=-1.0), reads=[B_bc], writes=[Bd2_])
                            ps, Bp = proj(wk_, Bwk_, i)
                            S.op("dve", TT(ke_all[:, i, :], ps, d_[:], ALU.mult), reads=[Bp, Bd_], writes=[B_ke[i]])
                            if own:
                                S.op("dve", TT(ki_all[:, i, :], ps, d2_[:], ALU.mult), reads=[Bp, Bd2_], writes=[B_ki[i]])
                        for vh in range(2):
                            wv_, Bwv_ = wload(w_in_v, KT, OGV + hh * 512 + vh * 256, 256)
                            for i in range(NT):
                                ps, Bp = proj(wv_, Bwv_, i)
                                S.op("act", ACT(vb_all[:, i, vh * 256:(vh + 1) * 256], ps, AF.Copy), reads=[Bp], writes=[B_vb[i]])
                        def gr_gen(hh=hh):
                            for vh in range(2):
                                wg_, Bwg_ = wload(w_in_v, KT, OGR + hh * 512 + vh * 256, 256)
                                for i in range(NT):
                                    ps, Bp = proj(wg_, Bwg_, i)
                                    S.op("act", ACT(sg_all[:, i, vh * 256:(vh + 1) * 256], ps, AF.Silu), reads=[Bp], writes=[B_sg[i]])
                                    yield

                        def rec_gen(hh=hh, sth=sth, Bsth=Bsth):
                            if own:
                                S.op("act", ACT(Sbf[:], sth[:], AF.Copy), reads=[Bsth], writes=[B_Sbf])
                            for i in range(NT):
                                for dt in range(2):
                                    S.op("pe", MM(bl_ps[:, 2 * dt:2 * dt + 2], lg_all[:, i, dt * 128:(dt + 1) * 128], negcol),
                                         reads=[B_lg[i], B_cb], writes=[B_bl])
                                S.op("act", ACT(ebl[:], bl_ps, AF.Exp), reads=[B_bl], writes=[B_ebl])
                                if own:
                                    for dt in range(2):
                                        S.op("pe", TR(ps_t[:, dt * 128:(dt + 1) * 128], qd_all[:, i, dt * 128:(dt + 1) * 128], identb[:]),
                                             reads=[B_qd[i], B_cb], writes=[B_tq])
                                        S.op("pe", TR(ps_t[:, 256 + dt * 128:256 + (dt + 1) * 128], ki_all[:, i, dt * 128:(dt + 1) * 128], identb[:]),
                                             reads=[B_ki[i], B_cb], writes=[B_tq])
                                    S.op("act", ACT(qdT[:], ps_t[:, 0:256].rearrange("p (a b) -> p a b", b=128), AF.Copy), reads=[B_tq], writes=[B_qdT])
                                    S.op("act", ACT(kiT[:], ps_t[:, 256:512].rearrange("p (a b) -> p a b", b=128), AF.Copy), reads=[B_tq], writes=[B_kiT])
                                    yield
                                    for dt in range(2):
                                        S.op("pe", MM(A_ps, kiT[:, dt, :], qdT[:, dt, :], start=(dt == 0), stop=(dt == 1)),
                                             reads=[B_kiT, B_qdT], writes=[B_A])
                                    S.op("dve", TT(At[:], A_ps, mutb[:], ALU.mult), reads=[B_A, B_cb], writes=[B_At])
                                    S.op("pe", MM(O_ps[:], At[:], vb_all[:, i, :], start=True, stop=False), reads=[B_At, B_vb[i]], writes=[B_O])
                                    for dt in range(2):
                                        S.op("pe", MM(O_ps[:], qdT[:, dt, :], Sbf[:, dt, :], start=False, stop=(dt == 1)),
                                             reads=[B_qdT, B_Sbf], writes=[B_O])
                                for dt in range(2):
                                    S.op("pe", MM(kv_ps[:], ke_all[:, i, dt * 128:(dt + 1) * 128], vb_all[:, i, :]), reads=[B_ke[i], B_vb[i]], writes=[B_kv])
                                    S.op("dve", STT(sth[:, dt, :], sth[:, dt, :], ebl[:, 2 * dt:2 * dt + 1], kv_ps[:], ALU.mult, ALU.add),
                                         reads=[Bsth, B_ebl, B_kv], writes=[Bsth])
                                if own:
                                    if i < NT - 1:
                                        S.op("act", ACT(Sbf[:], sth[:], AF.Copy), reads=[Bsth], writes=[B_Sbf])
                                    S.op("act", ACT(ot[:], O_ps[:], AF.Square, accum_out=gst[:, 0:1]), reads=[B_O], writes=[B_ot, B_gst])
                                    S.op("dve", TS(gst[:, 1:2], gst[:, 0:1], 1.0 / 512.0, EPS, ALU.mult, ALU.add), reads=[B_gst], writes=[B_gst])
                                    S.op("act", ACT(gst[:, 2:3], gst[:, 1:2], AF.Sqrt), reads=[B_gst], writes=[B_gst])
                                    S.op("dve", RCP(gst[:, 3:4], gst[:, 2:3]), reads=[B_gst], writes=[B_gst])
                                    S.op("dve", STT(on_all[:, i, :], O_ps[:], gst[:, 3:4], g512[:], ALU.mult, ALU.mult),
                                         reads=[B_O, B_gst, B_g512], writes=[B_on[i]])
                                yield

                        if own:
                            G_, R_ = gr_gen(), rec_gen()
                            g_alive, r_alive = True, True
                            while g_alive or r_alive:
                                if r_alive:
                                    try:
                                        next(R_)
                                    except StopIteration:
                                        r_alive = False
                                if g_alive:
                                    try:
                                        next(G_)
                                    except StopIteration:
                                        g_alive = False
                            for i in range(NT):
                                tok = slice(i * 128, (i + 1) * 128)
                                S.op("dve", TT(ot[:], on_all[:, i, :], sg_all[:, i, :], ALU.mult), reads=[B_on[i], B_sg[i]], writes=[B_ot])
                                for c in range(4):
                                    S.op("pe", TR(ps_t[:, 512 + c * 128:512 + (c + 1) * 128], ot[:, c * 128:(c + 1) * 128], identb[:]),
                                         reads=[B_ot, B_cb], writes=[B_to])
                                S.op("act", ACT(goT_h[:, :, tok], ps_t[:, 512:1024].rearrange("p (a b) -> p a b", b=128), AF.Copy),
                                     reads=[B_to], writes=[B_goT])
                        else:
                            for _ in rec_gen():
                                pass
                        if own:
                            for c in range(4):
                                S.dma("sp", DMA(goT_s[hh * 4 + c, :, :], goT_h[:, c, :]), reads=[B_goT])
                        else:
                            S.dma("sp", DMA(st_s[hh, :, :], sth[:].rearrange("p a b -> p (a b)")), reads=[Bsth])
                S.end_phase()
                if not own:
                    return

                with ExitStack() as es2:
                    sgt = [sb("sgt%d" % i, [128, 4, TOK], BF16, es2) for i in range(2)]
                    B_sgt = [Buf(), Buf()]
                    ci = 0
                    for (off, dst) in ((OGM, sgm_s), (OGG, sgg_s)):
                        for cc in range(16):
                            wg_, Bwg_ = wload(w_in_v, KT, off + cc * 256, 256)
                            st_ = sgt[ci % 2]
                            Bst_ = B_sgt[ci % 2]
                            ci += 1
                            for sub in range(2):
                                for tg in range(2):
                                    ps = pss[(sub * 2 + tg) % 4]
                                    Bp = B_ps[(sub * 2 + tg) % 4]
                                    for k in range(KT):
                                        S.op("pe", MM(ps[:], wg_[:, k, sub * 128:(sub + 1) * 128], hT[:, k, tg * 512:(tg + 1) * 512],
                                                      start=(k == 0), stop=(k == KT - 1)),
                                             reads=[Bwg_] + B_hT[tg * 4:(tg + 1) * 4], writes=[Bp])
                                    S.op("act", ACT(st_[:, sub, tg * 512:(tg + 1) * 512], ps[:], AF.Sigmoid), reads=[Bp], writes=[Bst_])
                            for sub in range(2):
                                S.dma("sp", DMA(dst[cc * 2 + sub, :, :], st_[:, sub, :]), reads=[Bst_])
                S.end_phase()
            return None

        run_pass(False)
        run_pass(True)

        with ExitStack() as es:
            pss = [psum("ps%d" % i, [128, 512], F32, es) for i in range(8)]
            B_ps = [PB("ps%d" % i) for i in range(8)]
            mT = sb("mT", [128, KT, TOK], BF16, es)
            B_mT = [Buf() for _ in range(NT)]
            with ExitStack() as es2:
                alloc_w(es2, 3, 4096)
                moT = sb("moT", [128, 16, TOK], BF16, es2)
                goT = sb("goT", [128, 32, TOK], BF16, es2)
                B_moT, B_goT2 = Buf(), Buf()
                B_moTp = [Buf() for _ in range(2)]
                B_goTp = [Buf() for _ in range(4)]
                for q_ in range(2):
                    S.dma("sp", DMA(moT[:, q_ * 8:(q_ + 1) * 8, :], moT_s[q_ * 8:(q_ + 1) * 8, :, :].rearrange("h p t -> p h t")), writes=[B_moTp[q_]])
                for q_ in range(4):
                    S.dma("sp", DMA(goT[:, q_ * 8:(q_ + 1) * 8, :], goT_s[q_ * 8:(q_ + 1) * 8, :, :].rearrange("h p t -> p h t")), writes=[B_goTp[q_]])
                sgm = [sb("sgm%d" % i, [128, TOK], BF16, es2) for i in range(2)]
                sgg = [sb("sgg%d" % i, [128, TOK], BF16, es2) for i in range(2)]
                B_sgm, B_sgg = [Buf(), Buf()], [Buf(), Buf()]
                t1 = sb("t1", [128, 512], F32, es2)
                t2 = sb("t2", [128, 512], F32, es2)
                B_t1, B_t2 = Buf(), Buf()
                for c in range(32):
                    j = c % 2
                    wm, Bwm = wload(w_bm_v, 16, c * 128, 128)
                    wg_, Bwg_ = wload(w_bg_v, KT, c * 128, 128)
                    S.dma("sp", DMA(sgm[j][:], sgm_s[c, :, :]), writes=[B_sgm[j]])
                    S.dma("sp", DMA(sgg[j][:], sgg_s[c, :, :]), writes=[B_sgg[j]])
                    for tg in range(2):
                        pm, Bpm = pss[tg * 2], B_ps[tg * 2]
                        pg, Bpg = pss[tg * 2 + 1], B_ps[tg * 2 + 1]
                        for k in range(16):
                            S.op("pe", MM(pm[:], wm[:, k, :], moT[:, k, tg * 512:(tg + 1) * 512],
                                          start=(k == 0), stop=(k == 15)), reads=[Bwm] + B_moTp, writes=[Bpm])
                        for k in range(KT):
                            S.op("pe", MM(pg[:], wg_[:, k, :], goT[:, k, tg * 512:(tg + 1) * 512],
                                          start=(k == 0), stop=(k == KT - 1)), reads=[Bwg_] + B_goTp, writes=[Bpg])
                        S.op("dve", TT(t1[:], pm[:], sgm[j][:, tg * 512:(tg + 1) * 512], ALU.mult), reads=[Bpm, B_sgm[j]], writes=[B_t1])
                        S.op("dve", TT(t2[:], pg[:], sgg[j][:, tg * 512:(tg + 1) * 512], ALU.mult), reads=[Bpg, B_sgg[j]], writes=[B_t2])
                        S.op("dve", TT(mT[:, c, tg * 512:(tg + 1) * 512], t1[:], t2[:], ALU.add), reads=[B_t1, B_t2],
                             writes=B_mT[tg * 4:(tg + 1) * 4])
                S.end_phase()
            alloc_w(es, 3)
            ybuf = [sb("ybuf%d" % i, [128, 256], F32, es) for i in range(2)]
            B_yb = [Buf(), Buf()]
            ci = 0
            for cc in range(16):
                wo, Bwo = wload(w_out_v, KT, cc * 256, 256)
                for i in range(NT):
                    ps, Bp = pss[4 + ci % 4], B_ps[4 + ci % 4]
                    yb, Byb = ybuf[ci % 2], B_yb[ci % 2]
                    ci += 1
                    for k in range(KT):
                        S.op("pe", MM(ps[:, 0:256], mT[:, k, i * 128:(i + 1) * 128], wo[:, k, :], start=(k == 0), stop=(k == KT - 1)),
                             reads=[Bwo, B_mT[i]], writes=[Bp])
                    S.op("act", ACT(yb[:], ps[:, 0:256], AF.Copy), reads=[Bp], writes=[Byb])
                    S.dma("sp", DMA(y_s[i * 128:(i + 1) * 128, cc * 256:(cc + 1) * 256], yb[:]), reads=[Byb])
            S.end_phase()

        with ExitStack() as es:
            h2T = sb("h2T", [128, KT, TOK], BF16, es)
            B_h2T = [Buf() for _ in range(NT)]
            ps_t = psum("pst", [128, 1024], BF16, es)
            B_pst1 = PB("pst")
            gpm = sb("gpm_sb", [128, D], F32, es)
            B_gpm = Buf()
            S.dma("sp", DMA(gpm[:], gpm_d[:, :]), writes=[B_gpm])
            yt = [sb("yt%d" % i, [128, D], F32, es) for i in range(2)]
            xt = [sb("xt%d" % i, [128, D], F32, es) for i in range(2)]
            x1t = [sb("x1t%d" % i, [128, D], F32, es) for i in range(2)]
            xn = [sb("xn%d" % i, [128, D], BF16, es) for i in range(2)]
            st = sb("st4", [128, 8, NT], F32, es)
            B_yt, B_xt, B_x1t, B_xn = [Buf(), Buf()], [Buf(), Buf()], [Buf(), Buf()], [Buf(), Buf()]
            B_junk = Buf()
            B_st = [Buf() for _ in range(NT)]
            for i in range(NT):
                j = i % 2
                rows = slice(i * 128, (i + 1) * 128)
                S.dma("sp", DMA(yt[j][:], y_s[rows, :]), writes=[B_yt[j]])
                S.dma("sp", DMA(xt[j][:], x_own[rows, :]), writes=[B_xt[j]])
                S.op("act", ACT(xn[j][:], yt[j][:], AF.Square, accum_out=st[:, 0, i:i + 1]), reads=[B_yt[j]], writes=[B_xn[j], B_st[i]])
                S.op("dve", TS(st[:, 1, i:i + 1], st[:, 0, i:i + 1], 1.0 / D, EPS, ALU.mult, ALU.add), reads=[B_st[i]], writes=[B_st[i]])
                S.op("act", ACT(st[:, 2, i:i + 1], st[:, 1, i:i + 1], AF.Sqrt), reads=[B_st[i]], writes=[B_st[i]])
                S.op("dve", RCP(st[:, 3, i:i + 1], st[:, 2, i:i + 1]), reads=[B_st[i]], writes=[B_st[i]])
                S.op("dve", STT(x1t[j][:], yt[j][:], st[:, 3, i:i + 1], gpm[:], ALU.mult, ALU.mult),
                     reads=[B_yt[j], B_st[i], B_gpm], writes=[B_x1t[j]])
                S.op("dve", TT(x1t[j][:], x1t[j][:], xt[j][:], ALU.add), reads=[B_x1t[j], B_xt[j]], writes=[B_x1t[j]])
                S.dma("sp", DMA(x1_s[rows, :], x1t[j][:]), reads=[B_x1t[j]])
                S.op("act", ACT(xn[j][:], x1t[j][:], AF.Square, accum_out=st[:, 4, i:i + 1]), reads=[B_x1t[j]], writes=[B_xn[j], B_st[i]])
                S.op("dve", TS(st[:, 5, i:i + 1], st[:, 4, i:i + 1], 1.0 / D, EPS, ALU.mult, ALU.add), reads=[B_st[i]], writes=[B_st[i]])
                S.op("act", ACT(st[:, 6, i:i + 1], st[:, 5, i:i + 1], AF.Sqrt), reads=[B_st[i]], writes=[B_st[i]])
                S.op("dve", RCP(st[:, 7, i:i + 1], st[:, 6, i:i + 1]), reads=[B_st[i]], writes=[B_st[i]])
                S.op("act", ACT(xn[j][:], x1t[j][:], AF.Copy, scale=st[:, 7, i:i + 1]), reads=[B_x1t[j], B_st[i]], writes=[B_xn[j]])
                for g in range(4):
                    for c in range(8):
                        k = g * 8 + c
                        S.op("pe", TR(ps_t[:, c * 128:(c + 1) * 128], xn[j][:, k * 128:(k + 1) * 128], identb[:]),
                             reads=[B_xn[j], B_cb], writes=[B_pst1])
                    S.op("dve", TT(h2T[:, g * 8:(g + 1) * 8, i * 128:(i + 1) * 128],
                                   ps_t[:].rearrange("p (a b) -> p a b", b=128),
                                   gv[:, 32 + g * 8:32 + (g + 1) * 8].unsqueeze(2).to_broadcast([128, 8, 128]), ALU.mult),
                         reads=[B_pst1, B_gv], writes=[B_h2T[i]])
            for half in range(2):
                S.dma("sp", DMA(h2T_s[half, :, :, :], h2T[:, :, half * 512:(half + 1) * 512]), reads=B_h2T[half * 4:(half + 1) * 4])
            S.end_phase()

        with ExitStack() as es:
            pss = [psum("ps%d" % i, [128, 512], F32, es) for i in range(8)]
            B_ps = [PB("ps%d" % i) for i in range(8)]
            alloc_w(es, 3)
            hid = sb("hid", [128, FT, 512], BF16, es)
            B_hid = [Buf() for _ in range(FT)]
            sgf = [sb("sgf%d" % i, [128, 512], F32, es) for i in range(2)]
            B_sgf = [Buf(), Buf()]
            y2b = [sb("y2b%d" % i, [128, 512], F32, es) for i in range(2)]
            B_y2b = [Buf(), Buf()]
            h2Th = sb("h2Th", [128, KT, 512], BF16, es)
            B_h2Th = Buf()
            for half in range(2):
                S.dma("sp", DMA(h2Th[:], h2T_s[half, :, :, :]), writes=[B_h2Th])
                ci = 0
                for cc in range(43):
                    wg_, Bwg_ = wload(w_fg_v, KT, cc * 256, 256)
                    wu_, Bwu_ = wload(w_fu_v, KT, cc * 256, 256)
                    for sub in range(2):
                        f = cc * 2 + sub
                        pg, Bpg = pss[(ci % 2) * 2], B_ps[(ci % 2) * 2]
                        pu, Bpu = pss[(ci % 2) * 2 + 1], B_ps[(ci % 2) * 2 + 1]
                        sg_, Bsg_ = sgf[ci % 2], B_sgf[ci % 2]
                        ci += 1
                        for k in range(KT):
                            S.op("pe", MM(pg[:], wg_[:, k, sub * 128:(sub + 1) * 128], h2Th[:, k, :], start=(k == 0), stop=(k == KT - 1)),
                                 reads=[Bwg_, B_h2Th], writes=[Bpg])
                        for k in range(KT):
                            S.op("pe", MM(pu[:], wu_[:, k, sub * 128:(sub + 1) * 128], h2Th[:, k, :], start=(k == 0), stop=(k == KT - 1)),
                                 reads=[Bwu_, B_h2Th], writes=[Bpu])
                        S.op("act", ACT(sg_[:], pg[:], AF.Silu), reads=[Bpg], writes=[Bsg_])
                        S.op("dve", TT(hid[:, f, :], sg_[:], pu[:], ALU.mult), reads=[Bsg_, Bpu], writes=[B_hid[f]])
                S.end_phase()
                ci = 0
                for qd_ in range(4):
                    c0 = qd_ * 1024
                    for f0 in range(0, FT, 8):
                        nf = min(8, FT - f0)
                        i3 = wn[0] % len(wbuf)
                        wn[0] += 1
                        wd = wbuf[i3][:, 0:8 * 1024].rearrange("p (k c) -> p k c", c=1024)
                        Bwd = B_w[i3]

                        def fn(e, wd=wd, f0=f0, nf=nf, c0=c0):
                            return [e.dma_start(out=wd[:, 0:nf, :], in_=w_fd_v[:, f0:f0 + nf, c0:c0 + 1024])]

                        S.dma("pool", fn, writes=[Bwd], n=1)
                        for fi in range(nf):
                            f = f0 + fi
                            for tt in range(4):
                                for cg in range(2):
                                    S.op("pe", MM(pss[tt * 2 + cg][:], hid[:, f, tt * 128:(tt + 1) * 128], wd[:, fi, cg * 512:(cg + 1) * 512],
                                                  start=(f == 0), stop=(f == FT - 1)),
                                         reads=[Bwd, B_hid[f]], writes=[B_ps[tt * 2 + cg]])
                    for tt in range(4):
                        for cg in range(2):
                            yb, Byb = y2b[ci % 2], B_y2b[ci % 2]
                            ci += 1
                            S.op("act", ACT(yb[:], pss[tt * 2 + cg][:], AF.Copy), reads=[B_ps[tt * 2 + cg]], writes=[Byb])
                            r0 = half * 512 + tt * 128
                            S.dma("sp", DMA(y2_s[r0:r0 + 128, c0 + cg * 512:c0 + (cg + 1) * 512], yb[:]), reads=[Byb])
                S.end_phase()

        with ExitStack() as es:
            gpf = sb("gpf_sb", [128, D], F32, es)
            B_gpf = Buf()
            S.dma("sp", DMA(gpf[:], gpf_d[:, :]), writes=[B_gpf])
            yt = [sb("yt%d" % i, [128, D], F32, es) for i in range(2)]
            xt = [sb("xt%d" % i, [128, D], F32, es) for i in range(2)]
            ot_ = [sb("ot%d" % i, [128, D], F32, es) for i in range(2)]
            junk = sb("junk", [128, D], BF16, es)
            st = sb("st5", [128, 4, NT], F32, es)
            B_yt, B_xt, B_ot = [Buf(), Buf()], [Buf(), Buf()], [Buf(), Buf()]
            B_junk = Buf()
            B_st = [Buf() for _ in range(NT)]
            for i in range(NT):
                j = i % 2
                rows = slice(i * 128, (i + 1) * 128)
                S.dma("sp", DMA(yt[j][:], y2_s[rows, :]), writes=[B_yt[j]])
                S.dma("sp", DMA(xt[j][:], x1_s[rows, :]), writes=[B_xt[j]])
                S.op("act", ACT(junk[:], yt[j][:], AF.Square, accum_out=st[:, 0, i:i + 1]), reads=[B_yt[j]], writes=[B_junk, B_st[i]])
                S.op("dve", TS(st[:, 1, i:i + 1], st[:, 0, i:i + 1], 1.0 / D, EPS, ALU.mult, ALU.add), reads=[B_st[i]], writes=[B_st[i]])
                S.op("act", ACT(st[:, 2, i:i + 1], st[:, 1, i:i + 1], AF.Sqrt), reads=[B_st[i]], writes=[B_st[i]])
                S.op("dve", RCP(st[:, 3, i:i + 1], st[:, 2, i:i + 1]), reads=[B_st[i]], writes=[B_st[i]])
                S.op("dve", STT(ot_[j][:], yt[j][:], st[:, 3, i:i + 1], gpf[:], ALU.mult, ALU.mult),
                     reads=[B_yt[j], B_st[i], B_gpf], writes=[B_ot[j]])
                S.op("dve", TT(ot_[j][:], ot_[j][:], xt[j][:], ALU.add), reads=[B_ot[j], B_xt[j]], writes=[B_ot[j]])
                S.dma("sp", DMA(out_d[rows, :], ot_[j][:]), reads=[B_ot[j]])
            S.end_phase()


def _consts():
    cb = np.zeros((128, C_END), np.float32)
    idx = np.arange(128)
    cb[:, C_ID:C_ID + 128] = np.eye(128, dtype=np.float32)
    perm = np.zeros((128, 128), np.float32)
    perm[(idx + 64) % 128, idx] = 1.0
    cb[:, C_PERM:C_PERM + 128] = perm
    le = (idx[:, None] <= idx[None, :]).astype(np.float32)
    gt = (idx[:, None] > idx[None, :]).astype(np.float32)
    cb[:, C_TRIU:C_TRIU + 128] = le * (-1.0 / 16.0)
    cb[:, C_TRISL:C_TRISL + 128] = gt * (-1.0 / 16.0)
    cb[:, C_MUT:C_MUT + 128] = le
    cm = np.zeros((128, 2, 256), np.float32)
    q = np.arange(256)
    for a in range(2):
        cm[:, a, :] = ((a * 128 + idx)[:, None] <= q[None, :]).astype(np.float32)
    cb[:, C_CM:C_CM + 512] = cm.reshape(128, 512)
    cb[:, C_NEG:C_NEG + 2] = -1.0 / 16.0
    half = 64
    inv_freq = (10000.0 ** (-np.arange(half, dtype=np.float32) / half)).astype(np.float32)
    ang = np.arange(2048, dtype=np.float32)[None, :] * inv_freq[:, None]
    cos = np.cos(ang).astype(np.float32)
    sin = np.sin(ang).astype(np.float32)
    cosT = np.concatenate([cos, cos], axis=0)
    sinT = np.concatenate([-sin, sin], axis=0)
    return cb, np.ascontiguousarray(cosT), np.ascontiguousarray(sinT)


def _gbias(odd):
    gb = np.full((128, 8, 8), NEG, np.float32)
    for qt in range(8):
        jb = qt // 2
        if odd:
            gb[:, qt, 0:4] = 0.0
        gb[:, qt, 4:4 + jb] = 0.0
    return gb.reshape(128, 64)


_NC_CACHE = {}


def _in_maps(x, pre_mix_norm_g, w_in, gla_gate_up, gla_gate_bias, gla_out_norm_g,
             w_branch_moba, w_branch_gla, w_out, post_mix_norm_g, pre_ffn_norm_g,
             w_ffn_gate, w_ffn_up, w_ffn_down, post_ffn_norm_g):
    f = lambda a: np.ascontiguousarray(np.asarray(a, dtype=np.float32))
    x = f(x)
    cb, cosT, sinT = _consts()
    gvec = np.concatenate([f(pre_mix_norm_g)[0].reshape(32, 128).T, f(pre_ffn_norm_g)[0].reshape(32, 128).T], axis=1)
    common = {
        "w_in": f(w_in)[0], "w_bm": f(w_branch_moba)[0], "w_bg": f(w_branch_gla)[0], "w_out": f(w_out)[0],
        "w_fg": f(w_ffn_gate)[0], "w_fu": f(w_ffn_up)[0], "w_fd": f(w_ffn_down)[0],
        "gup": f(gla_gate_up)[0], "gvec": np.ascontiguousarray(gvec),
        "glab": np.ascontiguousarray(np.broadcast_to(f(gla_gate_bias)[0][None, :], (128, 2048))),
        "g512": np.ascontiguousarray(np.broadcast_to(f(gla_out_norm_g)[0][None, :], (128, 512))),
        "gpm": np.ascontiguousarray(np.broadcast_to(f(post_mix_norm_g)[0][None, :], (128, D))),
        "gpf": np.ascontiguousarray(np.broadcast_to(f(post_ffn_norm_g)[0][None, :], (128, D))),
        "cosT": cosT, "sinT": sinT, "cblob": cb,
    }
    zeros = np.zeros((TOK, D), np.float32)
    in_maps = []
    for c in range(8):
        b, hf = c // 2, c % 2
        m = dict(common)
        m["x_own"] = np.ascontiguousarray(x[b, hf * TOK:(hf + 1) * TOK])
        m["x_pre"] = np.ascontiguousarray(x[b, 0:TOK]) if hf == 1 else zeros
        m["gbias"] = _gbias(hf == 1)
        in_maps.append(m)
    return in_maps


def kernel(**inputs):
    if "nc" not in _NC_CACHE:
        _NC_CACHE["nc"] = build_program()
    nc = _NC_CACHE["nc"]
    in_maps = _in_maps(**inputs)
    res = run_bass_kernel_spmd(nc, in_maps, core_ids=list(range(8)))
    out = np.empty((4, 2048, D), np.float32)
    for c in range(8):
        b, hf = c // 2, c % 2
        out[b, hf * TOK:(hf + 1) * TOK] = res.results[c]["out"]
    return out
```
